# Optimizing a Trainium2 kernel written in Bass

```python
import jax, jax.numpy as jnp
from jax import lax
import numpy as np

D_MODEL = 2048
BATCH = 4
SEQ = 2048
DEPTH = 4
DEC_BATCH = 128
DEC_SEQ = 8
PAST_LEN = 16384
PAGE_SIZE = 128

D_A = D_MODEL // 2
HEAD_A = 64
N_HEADS_A = D_A // HEAD_A
LORA_W = 64
LORA_A = 64
D_B = D_MODEL // 4
POOL_WINDOWS = (2, 4, 8, 16)
N_POOL = len(POOL_WINDOWS)
POOL_GC = D_B // N_POOL
POOL_BUF = max(POOL_WINDOWS) - 1
D_C = D_MODEL - D_A - D_B
N_GROUPS_C = 4
GC = D_C // N_GROUPS_C
CHUNK = 128
SHIFT_W = 3 * D_A + LORA_W + LORA_A
SPLIT_SIZES = (SHIFT_W, D_A, D_B, D_B, D_C, D_C, D_C)
SPLIT_IDX = tuple(int(i) for i in np.cumsum(SPLIT_SIZES)[:-1])
D_IN = sum(SPLIT_SIZES)
EPS = 1e-6
GN_EPS = HEAD_A * 1e-5
LN_EPS = 1e-5

kernel_name = "hybrid_rwkv7_pool_chunkgate_decode_step"

F32 = jnp.float32


def rms_norm(x, g):
    xf = x.astype(F32)
    y = xf * lax.rsqrt(jnp.mean(xf * xf, axis=-1, keepdims=True) + EPS)
    return (y * g.astype(F32)).astype(x.dtype)


def wkv7_scan(r, w, k, v, kk, a, s0):
    def step(S, inp):
        r_t, w_t, k_t, v_t, kk_t, a_t = inp
        sa = jnp.einsum('bhij,bhj->bhi', S, -kk_t)
        S = (S * w_t[:, :, None, :] + sa[..., None] * (kk_t * a_t)[:, :, None, :]
             + v_t[..., None] * k_t[:, :, None, :])
        return S, jnp.einsum('bhij,bhj->bhi', S, r_t)
    xs = tuple(jnp.moveaxis(t, 1, 0) for t in (r, w, k, v, kk, a))
    s_T, ys = lax.scan(step, s0, xs)
    return jnp.moveaxis(ys, 0, 1), s_T


def rwkv7_mix(ps, prev, s0, mu, w0, w_up, a0, a_up, k_k, k_a, r_k, lnx_g, lnx_b):
    B, T, _ = ps.shape
    psf = ps.astype(F32)
    shifted = jnp.concatenate([prev.astype(F32)[:, None], psf[:, :-1]], axis=1)
    xs = psf + (shifted - psf) * mu.astype(F32)
    r, k, v, xw, xa = jnp.split(xs, [D_A, 2 * D_A, 3 * D_A, 3 * D_A + LORA_W], axis=-1)
    w_log = -jax.nn.softplus(-(w0 + jnp.tanh(xw) @ w_up)) - 0.5
    w = jnp.exp(-jnp.exp(w_log))
    a = jax.nn.sigmoid(a0 + xa @ a_up)
    hs = lambda t: t.reshape(B, T, N_HEADS_A, HEAD_A)
    kk = hs(k * k_k)
    kk = kk * lax.rsqrt(jnp.maximum(jnp.sum(kk * kk, axis=-1, keepdims=True), 1e-12))
    k = k * (1.0 + (a - 1.0) * k_a)
    r, w, k, v, a = hs(r), hs(w), hs(k), hs(v), hs(a)
    y, s_T = wkv7_scan(r, w, k, v, kk, a, s0.astype(F32))
    ym = jnp.mean(y, axis=-1, keepdims=True)
    yv = jnp.mean(jnp.square(y - ym), axis=-1, keepdims=True)
    yn = ((y - ym) * lax.rsqrt(yv + GN_EPS)).reshape(B, T, D_A) * lnx_g + lnx_b
    bonus = (jnp.sum(r * k * r_k, axis=-1, keepdims=True) * v).reshape(B, T, D_A)
    return yn + bonus, ps[:, -1], s_T


def pool_mix(u, buf, pos0, pool_w, pool_scale):
    B, T, _ = u.shape
    full = jnp.concatenate([buf.astype(F32), u.astype(F32)], axis=1)
    cs = jnp.concatenate([jnp.zeros((B, 1, D_B), F32), jnp.cumsum(full, axis=1)], axis=1)
    hi = cs[:, POOL_BUF + 1:]
    pos = pos0 + jnp.arange(T)
    groups = []
    for g, win in enumerate(POOL_WINDOWS):
        sl = slice(g * POOL_GC, (g + 1) * POOL_GC)
        lo = cs[:, POOL_BUF + 1 - win:POOL_BUF + 1 - win + T, sl]
        cnt = jnp.minimum(pos + 1, win).astype(F32)[None, :, None]
        groups.append((hi[..., sl] - lo) / cnt)
    pooled = jnp.concatenate(groups, axis=-1) - full[:, POOL_BUF:]
    mixed = jnp.einsum('btgc,gcd->btgd', pooled.reshape(B, T, N_POOL, POOL_GC), pool_w)
    mixed = mixed.reshape(B, T, D_B) * pool_scale
    return mixed, full[:, -POOL_BUF:].astype(u.dtype)


def chunk_gate(u, v, ln_g, w_s, b_s):
    B, T, _ = v.shape
    vf = v.astype(F32)
    vm = jnp.mean(vf, axis=-1, keepdims=True)
    vv = jnp.mean(jnp.square(vf - vm), axis=-1, keepdims=True)
    vn = (vf - vm) * lax.rsqrt(vv + LN_EPS) * ln_g
    L = min(T, CHUNK)
    n = -(-T // L)
    pad = n * L - T
    ws = w_s[:, :L, :L] * jnp.tril(jnp.ones((L, L), F32))
    vp = jnp.pad(vn, ((0, 0), (0, pad), (0, 0))).reshape(B, n, L, N_GROUPS_C, GC)
    mix = jnp.einsum('gts,bnsgc->bntgc', ws, vp) + jnp.transpose(b_s[:, :L])[None, None, :, :, None]
    mix = mix.reshape(B, n * L, D_C)[:, :T]
    return u.astype(F32) * mix, vn.astype(v.dtype)


def hybrid_layer(x, st_shift, st_wkv, st_pool, pos0, p):
    h = rms_norm(x, p['norm_g'])
    proj = h @ p['w_in']
    ps, g_a, u_b, g_b, u_c, v_c, g_c = jnp.split(proj, SPLIT_IDX, axis=-1)
    y_a, new_shift, new_wkv = rwkv7_mix(ps, st_shift, st_wkv, p['mu'], p['w0'], p['w_up'], p['a0'],
                                        p['a_up'], p['k_k'], p['k_a'], p['r_k'], p['lnx_g'], p['lnx_b'])
    y_b, new_pool = pool_mix(u_b, st_pool, pos0, p['pool_w'], p['pool_scale'])
    y_c, vn_c = chunk_gate(u_c, v_c, p['gmlp_ln_g'], p['gmlp_ws'], p['gmlp_b'])
    cat = jnp.concatenate([y_a * jax.nn.silu(g_a.astype(F32)),
                           y_b * jax.nn.silu(g_b.astype(F32)),
                           y_c * jax.nn.silu(g_c.astype(F32))], axis=-1)
    out = x + (cat @ p['w_out'].astype(F32)).astype(x.dtype)
    return out, new_shift, new_wkv, new_pool, vn_c


def setup_inputs(seed: int = 0) -> dict:
    key = jax.random.key(seed)
    ks = jax.random.split(key, 26)
    nrm = lambda k, s: jax.random.normal(k, s, F32)
    return {
        "x_prompt": nrm(ks[0], (BATCH, SEQ, D_MODEL)),
        "x_sample": nrm(ks[1], (DEC_BATCH, DEC_SEQ, D_MODEL)),
        "state_shift": nrm(ks[2], (DEPTH, DEC_BATCH, SHIFT_W)),
        "state_wkv": 0.3 * nrm(ks[3], (DEPTH, DEC_BATCH, N_HEADS_A, HEAD_A, HEAD_A)),
        "state_pool": nrm(ks[4], (DEPTH, DEC_BATCH, POOL_BUF, D_B)),
        "norm_g": 1.0 + 0.02 * nrm(ks[5], (DEPTH, D_MODEL)),
        "final_norm_g": 1.0 + 0.02 * nrm(ks[6], (D_MODEL,)),
        "w_in": nrm(ks[7], (DEPTH, D_MODEL, D_IN)) * D_MODEL ** -0.5,
        "shift_mu": jax.random.uniform(ks[8], (DEPTH, SHIFT_W), F32),
        "w0": -0.5 + 0.5 * nrm(ks[9], (DEPTH, D_A)),
        "w_up": 0.1 * nrm(ks[10], (DEPTH, LORA_W, D_A)) * LORA_W ** -0.5,
        "a0": 0.1 * nrm(ks[11], (DEPTH, D_A)),
        "a_up": 0.1 * nrm(ks[12], (DEPTH, LORA_A, D_A)) * LORA_A ** -0.5,
        "k_k": 0.85 + 0.05 * nrm(ks[13], (DEPTH, D_A)),
        "k_a": 1.0 + 0.05 * nrm(ks[14], (DEPTH, D_A)),
        "r_k": 0.1 * nrm(ks[15], (DEPTH, N_HEADS_A, HEAD_A)),
        "lnx_g": 1.0 + 0.02 * nrm(ks[16], (DEPTH, D_A)),
        "lnx_b": 0.02 * nrm(ks[17], (DEPTH, D_A)),
        "pool_w": nrm(ks[18], (DEPTH, N_POOL, POOL_GC, POOL_GC)) * POOL_GC ** -0.5,
        "pool_scale": 0.5 + 0.1 * nrm(ks[19], (DEPTH, D_B)),
        "gmlp_ln_g": 1.0 + 0.02 * nrm(ks[20], (DEPTH, D_C)),
        "gmlp_ws": nrm(ks[21], (DEPTH, N_GROUPS_C, CHUNK, CHUNK)) * CHUNK ** -0.5,
        "gmlp_b": 1.0 + 0.1 * nrm(ks[22], (DEPTH, N_GROUPS_C, CHUNK)),
        "w_out": 0.5 * nrm(ks[23], (DEPTH, D_MODEL, D_MODEL)) * D_MODEL ** -0.5,
    }


def reference(x_prompt, x_sample, state_shift, state_wkv, state_pool, norm_g, final_norm_g, w_in,
              shift_mu, w0, w_up, a0, a_up, k_k, k_a, r_k, lnx_g, lnx_b, pool_w, pool_scale,
              gmlp_ln_g, gmlp_ws, gmlp_b, w_out):
    bp = x_prompt.shape[0]
    xp, xs = x_prompt, x_sample
    p_shift, p_wkv, p_pool = [], [], []
    s_shift, s_wkv, s_pool, s_v = [], [], [], []
    for l in range(DEPTH):
        prm = dict(norm_g=norm_g[l], w_in=w_in[l], mu=shift_mu[l], w0=w0[l], w_up=w_up[l], a0=a0[l],
                   a_up=a_up[l], k_k=k_k[l], k_a=k_a[l], r_k=r_k[l], lnx_g=lnx_g[l], lnx_b=lnx_b[l],
                   pool_w=pool_w[l], pool_scale=pool_scale[l], gmlp_ln_g=gmlp_ln_g[l],
                   gmlp_ws=gmlp_ws[l], gmlp_b=gmlp_b[l], w_out=w_out[l])
        xp, sh, wk, po, _ = hybrid_layer(
            xp, jnp.zeros((bp, SHIFT_W), xp.dtype), jnp.zeros((bp, N_HEADS_A, HEAD_A, HEAD_A), F32),
            jnp.zeros((bp, POOL_BUF, D_B), xp.dtype), 0, prm)
        p_shift.append(sh); p_wkv.append(wk); p_pool.append(po)
        xs, sh, wk, po, vc = hybrid_layer(xs, state_shift[l], state_wkv[l], state_pool[l], PAST_LEN, prm)
        s_shift.append(sh); s_wkv.append(wk); s_pool.append(po); s_v.append(vc)
    y_prompt = rms_norm(xp, final_norm_g)
    y_sample = rms_norm(xs, final_norm_g)
    return (y_prompt, y_sample, jnp.stack(p_shift), jnp.stack(p_wkv), jnp.stack(p_pool),
            jnp.stack(s_shift), jnp.stack(s_wkv), jnp.stack(s_pool), jnp.stack(s_v))
```

```python
import contextlib
import math
import numpy as np
import concourse.bass as bass
import concourse.mybir as mybir
from concourse.bass_utils import run_bass_kernel_spmd

F32 = mybir.dt.float32
BF = mybir.dt.bfloat16
AL = mybir.AluOpType
AF = mybir.ActivationFunctionType
AX = mybir.AxisListType

D = 2048
DIN = 6784
NCH = 53
EPS = 1e-6
GN_EPS = 64 * 1e-5
LN_EPS = 1e-5
NPP = 77
ENG = ("pe", "act", "dve", "pool", "sp")
KD = 16


class V:
    __slots__ = ("ap", "key")

    def __init__(s, ap, key):
        s.ap = ap
        s.key = key

    def __getitem__(s, i):
        return V(s.ap[i], s.key)

    def bc(s, shape):
        return V(s.ap.broadcast_to(list(shape)), s.key)

    def re(s, pat, **kw):
        return V(s.ap.rearrange(pat, **kw), s.key)

    def us(s, ax):
        return V(s.ap.unsqueeze(ax), s.key)

    def cast(s, dt):
        return V(s.ap.bitcast(dt), s.key)


class VK(V):
    __slots__ = ("keys",)

    def __init__(s, ap, keys):
        V.__init__(s, ap, "MULTI")
        s.keys = keys


def keys_of(v):
    if v is None or not isinstance(v, V) or v.key is None:
        return []
    if isinstance(v, VK):
        return list(v.keys)
    if v.key == "PSUM":
        ap = v.ap
        es = 2 if ap.dtype == BF else 4
        pstep = ap.ap[0][0]
        off = ap.offset % pstep
        dims = ap.ap[1:]
        starts = [off]
        for (st, cnt) in dims[:-1]:
            starts = [s0 + st * i for s0 in starts for i in range(cnt)]
        lst, lcnt = dims[-1]
        banks = set()
        for s0 in starts:
            banks.add((s0 * es) // 2048)
            banks.add(((s0 + lst * (lcnt - 1)) * es) // 2048)
        return [(ap.name, b) for b in banks]
    return [v.key]


class Prog:
    def __init__(s):
        s.ops = {e: [] for e in ENG}
        s.cnt = {e: 0 for e in ENG}
        s.lastw = {}
        s.readers = {}
        s.seen = {e: {} for e in ENG}
        s.dq = {"sp": 0, "pool": 0}

    def _deps(s, eng, rk, wk):
        need = {}

        def add(ev):
            sem, val, src = ev
            if eng == "pe" and src == "pe":
                return
            if need.get(sem, 0) < val:
                need[sem] = val

        for k in rk:
            if k in s.lastw:
                add(s.lastw[k])
        for k in wk:
            if k in s.lastw:
                add(s.lastw[k])
            for sem, (val, src) in s.readers.get(k, {}).items():
                add((sem, val, src))
        waits = []
        for sem, val in need.items():
            if s.seen[eng].get(sem, 0) >= val:
                continue
            s.seen[eng][sem] = val
            waits.append((sem, val))
        return waits

    def _commit(s, ev, rk, wk):
        sem, val, src = ev
        for k in rk:
            s.readers.setdefault(k, {})[sem] = (val, src)
        for k in wk:
            s.lastw[k] = ev
            s.readers[k] = {}

    def op(s, eng, fn, reads, writes):
        rk = [k for v in reads for k in keys_of(v)]
        wk = [k for v in writes for k in keys_of(v)]
        wk += [k for k in rk if isinstance(k, tuple) and isinstance(k[0], str) and k[0].startswith("pg")]
        waits = s._deps(eng, rk, wk)
        s.cnt[eng] += 1
        ev = ("E_" + eng, s.cnt[eng], eng)
        s.ops[eng].append((waits, fn, ("E_" + eng, 1)))
        s._commit(ev, rk, wk)

    def dma(s, out, in_, q="sp"):
        rk = keys_of(in_)
        wk = keys_of(out)
        i = s.dq[q]
        s.dq[q] += 1
        slot = i % KD
        val = 16 * (i // KD + 1)
        sem = "D_%s_%d" % (q, slot)
        waits = s._deps(q, rk, wk)
        if i >= KD and s.seen[q].get(sem, 0) < val - 16:
            s.seen[q][sem] = val - 16
            waits.append((sem, val - 16))
        o, a = out.ap, in_.ap
        s.ops[q].append((waits, lambda e: e.dma_start(out=o, in_=a), (sem, 16)))
        s._commit((sem, val, "dma"), rk, wk)

    def barrier(s):
        for e in ENG:
            waits = []
            for e2 in ENG:
                c = s.cnt[e2]
                sem = "E_" + e2
                if c > 0 and s.seen[e].get(sem, 0) < c:
                    s.seen[e][sem] = c
                    waits.append((sem, c))
            for q in ("sp", "pool"):
                n = s.dq[q]
                for slot in range(KD):
                    u = (n - slot + KD - 1) // KD if n > slot else 0
                    sem = "D_%s_%d" % (q, slot)
                    if u > 0 and s.seen[e].get(sem, 0) < 16 * u:
                        s.seen[e][sem] = 16 * u
                        waits.append((sem, 16 * u))
            if waits:
                s.ops[e].append((waits, None, None))

    def mm(s, out, lhsT, rhs, start=True, stop=True):
        o, l, r = out.ap, lhsT.ap, rhs.ap
        s.op("pe", lambda e: e.matmul(o, l, r, start=start, stop=stop), [lhsT, rhs], [out])

    def tr(s, out, in_, ident):
        o, i, d = out.ap, in_.ap, ident.ap
        s.op("pe", lambda e: e.transpose(o, i, d), [in_, ident], [out])

    def act(s, out, in_, func, bias=None, scale=None, accum=None):
        o, i = out.ap, in_.ap
        kw = {}
        rd = [in_]
        wr = [out]
        if bias is not None:
            kw["bias"] = bias.ap
            rd.append(bias)
        if scale is not None:
            kw["scale"] = scale
        if accum is not None:
            kw["accum_out"] = accum.ap
            wr.append(accum)
        s.op("act", lambda e: e.activation(out=o, in_=i, func=func, **kw), rd, wr)

    def tt(s, out, in0, in1, op, eng="dve"):
        o, a, b = out.ap, in0.ap, in1.ap
        s.op(eng, lambda e: e.tensor_tensor(out=o, in0=a, in1=b, op=op), [in0, in1], [out])

    def ts(s, out, in0, s1, s2, op0, op1=None, eng="dve"):
        o, a = out.ap, in0.ap
        rd = [in0]
        x1 = s1
        x2 = s2
        if isinstance(s1, V):
            rd.append(s1)
            x1 = s1.ap
        if isinstance(s2, V):
            rd.append(s2)
            x2 = s2.ap
        if op1 is None:
            s.op(eng, lambda e: e.tensor_scalar(out=o, in0=a, scalar1=x1, scalar2=None, op0=op0), rd, [out])
        else:
            s.op(eng, lambda e: e.tensor_scalar(out=o, in0=a, scalar1=x1, scalar2=x2, op0=op0, op1=op1), rd, [out])

    def stt(s, out, in0, scalar, in1, op0, op1):
        o, a, b = out.ap, in0.ap, in1.ap
        rd = [in0, in1]
        sc = scalar
        if isinstance(scalar, V):
            rd.append(scalar)
            sc = scalar.ap
        s.op("dve", lambda e: e.scalar_tensor_tensor(out=o, in0=a, scalar=sc, in1=b, op0=op0, op1=op1), rd, [out])

    def copy(s, out, in_, eng="act"):
        o, i = out.ap, in_.ap
        if eng == "act":
            s.op("act", lambda e: e.activation(out=o, in_=i, func=AF.Copy), [in_], [out])
        else:
            s.op(eng, lambda e: e.tensor_copy(out=o, in_=i), [in_], [out])

    def rsqrt(s, out, in_):
        o, i = out.ap, in_.ap
        s.op("act", lambda e: e.activation(out=o, in_=i, func=AF.Sqrt), [in_], [out])
        s.op("dve", lambda e: e.reciprocal(out=o, in_=o), [out], [out])

    def memset(s, out, val, eng="dve"):
        o = out.ap
        s.op(eng, lambda e: e.memset(o, val), [], [out])

    def reduce(s, out, in_, op=AL.add):
        o, i = out.ap, in_.ap
        s.op("dve", lambda e: e.tensor_reduce(out=o, in_=i, axis=AX.X, op=op), [in_], [out])


def build(NL, SEQ_P, NSQ):
    NPT = SEQ_P // 128
    SR = NSQ * 8
    NTOK = SEQ_P + SR
    rtiles = [(i * 128, 128) for i in range(NPT)] + [(SEQ_P, SR)]
    NRT = len(rtiles)
    tgroups = []
    t0 = 0
    while t0 < NTOK:
        n = min(512, NTOK - t0)
        tgroups.append((t0, n))
        t0 += n

    nc = bass.Bass("TRN2", target_bir_lowering=False)

    def dram(name, shape, dt=F32, kind="ExternalInput"):
        return nc.dram_tensor(name, list(shape), dt, kind=kind).ap()

    xp = dram("xp", [SEQ_P, D])
    xs = dram("xs", [SR, D])
    st_shift = dram("st_shift", [NL, NSQ, 3200])
    st_wkv = dram("st_wkv", [NL, NSQ, 16, 64, 64])
    st_pool = dram("st_pool", [NL, NSQ, 15, 512])
    norm_g = dram("norm_g", [NL, D])
    fnorm_g = dram("fnorm_g", [1, D])
    w_in = dram("w_in", [NL, D, DIN])
    w_out = dram("w_out", [NL, D, D])
    PPd = dram("pp", [NL, 128, NPP])
    w0d = dram("w0", [NL, 1, 1024])
    LWd = dram("lw", [NL, 128, 1024])
    pwd = dram("pool_w", [NL, 4, 128, 128])
    lngd = dram("lng", [NL, 1, 512])
    wsTd = dram("wsT", [NL, 128, 4, 128])
    gbd = dram("gb", [NL, 1, 512])
    cM4 = dram("cmask4", [128, 512])
    cNSL = dram("cnsl", [128, 128])
    cTRIS = dram("ctris", [128, 128])
    cIDF = dram("cidf", [128, 128])
    cBLK = dram("cblk", [128, 128])
    cIC0 = dram("cic0", [128, 512])
    cIC1 = dram("cic1", [128, 512])

    yp = dram("yp", [SEQ_P, D], kind="ExternalOutput")
    ys = dram("ys", [SR, D], kind="ExternalOutput")
    p_shift = dram("p_shift", [NL, 3200], kind="ExternalOutput")
    p_wkv = dram("p_wkv", [NL, 16, 64, 64], kind="ExternalOutput")
    p_pool = dram("p_pool", [NL, 15, 512], kind="ExternalOutput")
    s_shift = dram("s_shift", [NL, NSQ, 3200], kind="ExternalOutput")
    s_wkv = dram("s_wkv", [NL, NSQ, 16, 64, 64], kind="ExternalOutput")
    s_pool = dram("s_pool", [NL, NSQ, 15, 512], kind="ExternalOutput")
    s_cv = dram("s_cv", [NL, NSQ, 8, 512], kind="ExternalOutput")

    projT = dram("projT", [NCH * 128, NTOK], kind="Internal")
    vcTM = dram("vcTM", [NTOK, 512], kind="Internal")
    xbuf = dram("xbuf", [NTOK, D], kind="Internal")
    catT = dram("catT", [NRT, 128, 16, 128], BF, kind="Internal")

    projV = projT.rearrange("(ch p) t -> p ch t", p=128)

    P = Prog()
    es = contextlib.ExitStack()
    ARW = 47104
    AR = es.enter_context(nc.sbuf_tensor("arena", [128, ARW], F32))
    PG = [es.enter_context(nc.psum_tensor("pg%d" % i, [128, 4, 512], F32)) for i in range(2)]
    sems = {}
    for e in ENG:
        sems["E_" + e] = es.enter_context(nc.semaphore("E_" + e))
    for q in ("sp", "pool"):
        for i in range(KD):
            n = "D_%s_%d" % (q, i)
            sems[n] = es.enter_context(nc.semaphore(n))

    cur = [0]

    def carve(shape, dt, key, at=None):
        esz = 2 if dt == BF else 4
        n = 1
        for x in shape[1:]:
            n *= x
        words = (n * esz + 3) // 4
        words = (words + 7) // 8 * 8
        if at is None:
            at = cur[0]
            cur[0] += words
        assert at + words <= ARW, ("arena overflow", key, at, words)
        ap = AR[0:shape[0], at:at + words]
        if dt == BF:
            ap = ap.bitcast(BF)
        ap = ap[:, 0:n]
        if len(shape) == 3:
            ap = ap.rearrange("p (a b) -> p a b", b=shape[2])
        elif len(shape) == 4:
            ap = ap.rearrange("p (a b c) -> p a b c", b=shape[2], c=shape[3])
        v = V(ap, key)
        return v, at

    def psum(g, b0, nb=1):
        return V(PG[g][:, b0:b0 + nb, :].rearrange("p a b -> p (a b)"), "PSUM")

    def psum3(g, b0, nb, inner):
        return V(PG[g][:, b0:b0 + nb, :].rearrange("p a (b c) -> p (a b) c", c=inner), "PSUM")

    def psbf(g, b):
        return V(PG[g][:, b, :].bitcast(BF), "PSUM")

    MASK4, _ = carve([128, 4, 128], F32, "MASK4")
    NSL, _ = carve([128, 128], F32, "NSL")
    TRIS, _ = carve([128, 128], F32, "TRIS")
    IDF, _ = carve([128, 128], F32, "IDF")
    IDB, _ = carve([128, 128], BF, "IDB")
    BLK1, _ = carve([128, 128], BF, "BLK1")
    IC0, _ = carve([128, 4, 128], F32, "IC0")
    IC1, _ = carve([128, 4, 128], F32, "IC1")
    ONESF, _ = carve([128, 128], F32, "ONESF")
    PP, _ = carve([128, NPP], F32, "PP")
    OMK, _ = carve([128, 8], F32, "OMK")
    W0R, _ = carve([128, 1024], F32, "W0R")
    LW, _ = carve([128, 1024], BF, "LW")
    PW, _ = carve([128, 4, 128], BF, "PW")
    LNG, _ = carve([128, 512], F32, "LNG")
    BSB, _ = carve([128, 4, 128], F32, "BSB")
    WSTF, _ = carve([128, 4, 128], F32, "WSTF")
    WST, _ = carve([128, 4, 128], BF, "WST")
    HS32, _ = carve([128, 8, 64], F32, "HS32")
    HSB, _ = carve([128, 8, 64], BF, "HSB")
    base = cur[0]

    P.dma(MASK4, V(cM4.rearrange("p (a b) -> p a b", b=128), None))
    P.dma(NSL, V(cNSL, None))
    P.dma(TRIS, V(cTRIS, None))
    P.dma(IDF, V(cIDF, None))
    P.dma(IDB, V(cIDF, None), q="pool")
    P.dma(BLK1, V(cBLK, None), q="pool")
    P.dma(IC0, V(cIC0.rearrange("p (a b) -> p a b", b=128), None))
    P.dma(IC1, V(cIC1.rearrange("p (a b) -> p a b", b=128), None))
    P.memset(ONESF, 1.0)

    cur[0] = base
    HT, _ = carve([128, 16, NTOK], BF, "HT")
    abase = cur[0]
    XT = [carve([128, D], F32, "XT%d" % i)[0] for i in range(2)]
    HB = [carve([128, D], BF, "HB%d" % i)[0] for i in range(2)]
    NGB, _ = carve([128, D], F32, "NGB")
    SQJ, _ = carve([128, D], BF, "SQJ")
    SSA, _ = carve([128, 8], F32, "SSA")
    cur[0] = abase
    WCH = [carve([128, 16, 128], BF, "WCH%d" % i)[0] for i in range(3)]
    STG = [carve([128, 512], F32, "STG%d" % i)[0] for i in range(4)]
    cur[0] = base
    WO = [carve([128, 16, 512], BF, "WO%d" % i)[0] for i in range(4)]
    CT = [carve([128, 16, 128], BF, "CT%d" % i)[0] for i in range(2)]
    XO = [carve([128, D], F32, "XO%d" % i)[0] for i in range(2)]
    cur[0] = base
    PS, _ = carve([128, 25, 129], F32, "R1")
    XS, _ = carve([128, 25, 128], F32, "R1")
    r1end = cur[0]
    cur[0] = base
    AM = [carve([128, 4, 4, 128], BF, "R1")[0] for g in range(4)]
    SPm = [carve([128, 4, 2, 128], BF, "R1")[0] for g in range(4)]
    assert cur[0] <= r1end
    cur[0] = r1end
    AT, _ = carve([128, 8, 128], F32, "AT")
    EL, _ = carve([128, 8, 129], F32, "EL")
    ELI, _ = carve([128, 8, 128], F32, "ELI")
    KK, _ = carve([128, 8, 128], F32, "KK")
    RN, _ = carve([128, 8, 128], F32, "RN")
    SIG, _ = carve([128, 1024], F32, "SIG")
    KK2, _ = carve([128, 8, 128], BF, "KK2")
    KR, _ = carve([128, 8, 2, 128], BF, "KR")
    KHf, _ = carve([128, 8, 128], BF, "KHf")
    BHf, _ = carve([128, 8, 128], BF, "BHf")
    VBF, _ = carve([128, 8, 128], BF, "VBF")
    TW, _ = carve([128, 128], BF, "TW")
    GAM, _ = carve([128, 8], F32, "GAM")
    KTtm, _ = carve([128, 1024], BF, "KTtm")
    KHtm, _ = carve([128, 1024], BF, "KHtm")
    BHtm, _ = carve([128, 1024], BF, "BHtm")
    Vtm, _ = carve([128, 1024], BF, "Vtm")
    Qm = [carve([128, 4, 128], BF, "Q%d" % g)[0] for g in range(4)]
    AV, _ = carve([128, 1024], BF, "AV")
    KPf, _ = carve([128, 8, 128], BF, "KPf")
    U, _ = carve([128, 1024], BF, "U")
    HTMP, _ = carve([128, 8, 64], F32, "HTMP")
    WKO, _ = carve([128, 8, 128], F32, "WKO")
    WKI, _ = carve([128, 16, 64], F32, "WKI")
    GST, _ = carve([128, 64], F32, "GST")
    SHO, _ = carve([128, 128], F32, "SHO")
    SHI, _ = carve([128, 128], F32, "SHI")
    CATT, _ = carve([128, 16, 128], BF, "CATT")
    EXT, _ = carve([128, 4, 143], F32, "EXT")
    WA, _ = carve([128, 4, 143], F32, "WA")
    WB, _ = carve([128, 4, 143], F32, "WB")
    PLD, _ = carve([128, 4, 128], BF, "PLD")
    PTM, _ = carve([128, 4, 128], F32, "PTM")
    GB, _ = carve([128, 4, 128], F32, "GB")
    POUT, _ = carve([128, 512], F32, "POUT")
    PIN, _ = carve([128, 512], F32, "PIN")
    VC, _ = carve([128, 512], F32, "VC")
    VN32, _ = carve([128, 512], F32, "VN32")
    VNB, _ = carve([128, 512], BF, "VNB")
    UC, _ = carve([128, 4, 128], F32, "UC")
    GC, _ = carve([128, 4, 128], F32, "GC")
    T1, _ = carve([128, 4, 128], F32, "T1")
    CST, _ = carve([128, 16], F32, "CST")
    Y32 = V(AT.ap.rearrange("p a b -> p (a b)"), "AT")
    VP = V(EL.ap.rearrange("p a b -> p (a b)")[:, 0:1024], "EL")
    GA = ELI
    YA = KK
    SQT = V(RN.ap.rearrange("p a b -> p (a b)"), "RN")
    BON = V(SIG.ap.rearrange("p (a b) -> p a b", b=128), "SIG")

    def xsrc(l, r0, R):
        if l == 0:
            if r0 < SEQ_P:
                return V(xp[r0:r0 + R, :], None)
            return V(xs[0:R, :], None)
        return V(xbuf[r0:r0 + R, :], ("xb", r0))

    def phase_norm(l, gsrc, final):
        P.barrier()
        P.dma(NGB, V(gsrc.broadcast_to([128, D]), None))
        for ti, (r0, R) in enumerate(rtiles):
            xt = XT[ti % 2]
            hb = HB[ti % 2]
            P.dma(xt[0:R], xsrc(l, r0, R))
            ss = SSA[:, (ti % 2) * 2:(ti % 2) * 2 + 1]
            P.memset(ss[0:R], 0.0)
            P.act(SQJ[0:R], xt[0:R], AF.Square, accum=ss[0:R])
            rs = SSA[:, (ti % 2) * 2 + 1:(ti % 2) * 2 + 2]
            P.ts(rs[0:R], ss[0:R], 1.0 / D, EPS, AL.mult, AL.add)
            P.rsqrt(rs[0:R], rs[0:R])
            if final:
                P.stt(xt[0:R], xt[0:R], rs[0:R], NGB[0:R], AL.mult, AL.mult)
                if r0 < SEQ_P:
                    P.dma(V(yp[r0:r0 + R, :], None), xt[0:R])
                else:
                    P.dma(V(ys[0:R, :], None), xt[0:R])
                continue
            P.stt(hb[0:R], xt[0:R], rs[0:R], NGB[0:R], AL.mult, AL.mult)
            g = ti % 2
            for half in range(2):
                pb = psbf(g, half + 2 * ((ti // 2) % 2))
                for k in range(8):
                    kc = half * 8 + k
                    P.tr(pb[:, k * 128:k * 128 + R], hb[0:R, kc * 128:(kc + 1) * 128], IDB[0:R, 0:R])
                src = pb.re("p (a b) -> p a b", b=128)[:, :, 0:R]
                dst = HT[:, half * 8:half * 8 + 8, r0:r0 + R]
                if half == 0:
                    P.copy(dst, src, "act")
                else:
                    P.copy(dst, src, "dve")

    def phase_proj(l):
        P.barrier()
        wv = w_in[l].rearrange("(kc p) c -> p kc c", p=128)
        bank = 0
        sg = 0
        for ch in range(NCH):
            wch = WCH[ch % 3]
            P.dma(wch, V(wv[:, :, ch * 128:(ch + 1) * 128], None), q="pool")
            if 45 <= ch < 49:
                for ti, (r0, R) in enumerate(rtiles):
                    ps = psum(bank // 4, bank % 4)
                    for kc in range(16):
                        P.mm(ps[0:R, 0:128], HT[:, kc, r0:r0 + R], wch[:, kc, :], kc == 0, kc == 15)
                    st = STG[sg % 4]
                    if sg % 2 == 0:
                        P.copy(st[0:R, 0:128], ps[0:R, 0:128], "act")
                    else:
                        P.copy(st[0:R, 0:128], ps[0:R, 0:128], "dve")
                    P.dma(V(vcTM[r0:r0 + R, (ch - 45) * 128:(ch - 44) * 128], ("vc", ti, ch)), st[0:R, 0:128])
                    bank = (bank + 1) % 8
                    sg += 1
                continue
            for gi, (t0, n) in enumerate(tgroups):
                ps = psum(bank // 4, bank % 4)
                for kc in range(16):
                    P.mm(ps[:, 0:n], wch[:, kc, :], HT[:, kc, t0:t0 + n], kc == 0, kc == 15)
                st = STG[sg % 4]
                if sg % 2 == 0:
                    P.copy(st[:, 0:n], ps[:, 0:n], "act")
                else:
                    P.copy(st[:, 0:n], ps[:, 0:n], "dve")
                P.dma(V(projT[ch * 128:(ch + 1) * 128, t0:t0 + n], ("pj", ch, gi)), st[:, 0:n])
                bank = (bank + 1) % 8
                sg += 1

    def pj(c0, c1, a, b):
        ks = set()
        for gi, (t0, n) in enumerate(tgroups):
            if a < t0 + n and b > t0:
                for ch in range(c0, c1):
                    ks.add(("pj", ch, gi))
        return projV[:, c0:c1, a:b], sorted(ks)

    def dma_pj(dst, c0, c1, a, b):
        ap, ks = pj(c0, c1, a, b)
        P.dma(dst, VK(ap, ks))

    def layer_consts(l):
        P.barrier()
        P.dma(PP, V(PPd[l], None))
        P.dma(W0R[0:1], V(w0d[l], None))
        P.dma(LW, V(LWd[l], None), q="pool")
        P.dma(PW, V(pwd[l].rearrange("g c d -> c g d"), None), q="pool")
        P.dma(LNG, V(lngd[l].broadcast_to([128, 512]), None))
        P.dma(BSB, V(gbd[l].broadcast_to([128, 512]).rearrange("p (a b) -> p a b", b=128), None))
        P.dma(WSTF, V(wsTd[l], None))
        P.tt(WST, WSTF, MASK4[:, 3:4, :].bc([128, 4, 128]), AL.mult)
        P.ts(OMK, PP[:, 33:41], -1.0, 1.0, AL.mult, AL.add)
        P.memset(EL[:, :, 0:1], 1.0)

    import os
    SUB = float(os.environ.get("K_SUB", "99"))

    def tile_call(l, T, tok0, first, last, sb, rt, col0):
        nlev = int(math.log2(T)) - 1
        bc8 = lambda c0: PP[:, c0:c0 + 8].us(2).bc([128, 8, T])
        if first and sb is None:
            P.memset(PS[:, :, 0:1], 0.0)
            dma_pj(PS[:, :, 1:T + 1], 0, 25, tok0, tok0 + T)
        elif first:
            P.dma(SHI[0:25, :], V(st_shift[l, sb].rearrange("(ch p) -> ch p", p=128), None))
            pt = psum(1, 3)
            P.tr(pt[:, 0:25], SHI[0:25, :], IDF[0:25, 0:25])
            P.copy(PS[:, :, 0], pt[:, 0:25], "dve")
            dma_pj(PS[:, :, 1:T + 1], 0, 25, tok0, tok0 + T)
        else:
            dma_pj(PS[:, :, 0:T + 1], 0, 25, tok0 - 1, tok0 + T)
        if last:
            pt = psum(1, 3)
            P.tr(pt[0:25, 0:128], PS[:, :, T], IDF)
            P.copy(SHO[0:25, :], pt[0:25, 0:128], "act")
            dst = p_shift[l] if sb is None else s_shift[l, sb]
            P.dma(V(dst.rearrange("(ch p) -> ch p", p=128), None), SHO[0:25, :])
        if SUB < 1:
            return
        P.memset(EL[:, :, 0:1], 1.0)
        xs_ = XS[:, :, 0:T]
        P.tt(xs_, PS[:, :, 0:T], PS[:, :, 1:T + 1], AL.subtract)
        P.tt(xs_, xs_, PP[:, 0:25].us(2).bc([128, 25, T]), AL.mult)
        P.tt(xs_, xs_, PS[:, :, 1:T + 1], AL.add)
        Xr = XS[:, 0:8, 0:T]
        Xk = XS[:, 8:16, 0:T]
        Xv = XS[:, 16:24, 0:T]
        if SUB < 2:
            return
        P.act(TW[0:64, 0:T], XS[0:64, 24, 0:T], AF.Tanh)
        P.act(TW[64:128, 0:T], XS[64:128, 24, 0:T], AF.Copy)
        zt = psum(0, 0, 2)
        for h2 in range(2):
            P.mm(zt[0:T, h2 * 512:(h2 + 1) * 512], TW[0:64, 0:T], LW[0:64, h2 * 512:(h2 + 1) * 512], True, False)
            P.mm(zt[0:T, h2 * 512:(h2 + 1) * 512], ONESF[0:1, 0:T], W0R[0:1, h2 * 512:(h2 + 1) * 512], False, True)
        P.act(SIG[0:T, :], zt[0:T, :], AF.Sigmoid)
        lt = psum3(0, 2, 2, 128)
        for ch in range(8):
            P.mm(lt[:, ch, 0:T], SIG[0:T, ch * 128:(ch + 1) * 128], TRIS[0:T, 0:T])
        P.act(EL[:, :, 1:T + 1], lt[:, :, 0:T], AF.Exp)
        P.act(ELI[:, :, 0:T], lt[:, :, 0:T], AF.Exp, scale=-1.0)
        at = psum3(1, 0, 2, 128)
        for ch in range(8):
            P.mm(at[:, ch, 0:T], LW[64:128, ch * 128:(ch + 1) * 128], TW[64:128, 0:T])
        a_ = AT[:, :, 0:T]
        P.tt(a_, at[:, :, 0:T], bc8(65), AL.add)
        P.act(a_, a_, AF.Sigmoid)
        if SUB < 3:
            return
        kk = KK[:, :, 0:T]
        P.tt(kk, Xk, bc8(25), AL.mult)
        P.act(KK2[:, :, 0:T], kk, AF.Square)
        hs = psum3(1, 2, 2, 128)
        for ch in range(8):
            P.mm(hs[:, ch, 0:T], BLK1, KK2[:, ch, 0:T])
        rn = RN[:, :, 0:T]
        P.ts(rn, hs[:, :, 0:T], 1e-12, None, AL.max)
        P.rsqrt(rn, rn)
        P.tt(kk, kk, rn, AL.mult)
        P.tt(rn, a_, bc8(33), AL.mult)
        P.tt(rn, rn, OMK.us(2).bc([128, 8, T]), AL.add)
        P.tt(rn, rn, Xk, AL.mult)
        P.tt(a_, kk, a_, AL.mult)
        P.tt(KR[:, :, 0, 0:T], kk, EL[:, :, 0:T], AL.mult)
        P.tt(KR[:, :, 1, 0:T], Xr, EL[:, :, 1:T + 1], AL.mult)
        P.tt(KHf[:, :, 0:T], rn, ELI[:, :, 0:T], AL.mult)
        P.tt(BHf[:, :, 0:T], a_, ELI[:, :, 0:T], AL.mult)
        P.copy(GAM, EL[:, :, T], "dve")
        bon = BON[:, :, 0:T]
        P.tt(bon, Xr, bc8(41), AL.mult)
        P.tt(KK2[:, :, 0:T], bon, rn, AL.mult)
        bs = psum3(0, 0, 2, 128)
        for ch in range(8):
            P.mm(bs[:, ch, 0:T], BLK1, KK2[:, ch, 0:T])
        P.tt(bon, bs[:, :, 0:T], Xv, AL.mult)
        P.copy(VBF[:, :, 0:T], Xv, "act")
        if SUB < 4:
            return
        for qi, (src, dst) in enumerate(((KR[:, :, 0, :], KTtm), (KHf, KHtm), (BHf, BHtm), (VBF, Vtm))):
            pb = psbf(1, qi)
            for ch in range(8):
                P.tr(pb[0:T, ch * 128:(ch + 1) * 128], src[:, ch, 0:T], IDB)
            if qi == 2:
                P.act(dst[0:T, :], pb[0:T, :], AF.Copy, scale=-1.0)
            else:
                P.copy(dst[0:T, :], pb[0:T, :], "act" if qi % 2 == 0 else "dve")
        if SUB < 5:
            return
        m4 = MASK4[0:T, :, 0:T].us(1).bc([T, 4, 4, T])

        def pga(g):
            return V(PG[g % 2][:, :, :].rearrange("p h (a b) -> p h a b", b=128), "PSUM")

        def mm2(pg, hh, s0, lhsT, rhs3):
            if T == 128:
                P.mm(pg[0:T, hh, s0:s0 + 2, :].re("p a b -> p (a b)"), lhsT, rhs3.re("p a b -> p (a b)"))
            else:
                P.mm(pg[0:T, hh, s0, 0:T], lhsT, rhs3[:, 0, 0:T])
                P.mm(pg[0:T, hh, s0 + 1, 0:T], lhsT, rhs3[:, 1, 0:T])

        def a_mats(g):
            pg = pga(g)
            for hh in range(4):
                h = 4 * g + hh
                ch, po = h // 2, (h % 2) * 64
                mm2(pg, hh, 0, BHf[po:po + 64, ch, 0:T], KR[po:po + 64, ch])
                mm2(pg, hh, 2, KHf[po:po + 64, ch, 0:T], KR[po:po + 64, ch])
            if SUB < 5.01:
                return
            P.tt(AM[g][0:T, :, :, 0:T], pg[0:T, :, :, 0:T], m4, AL.mult)
            if SUB < 5.02:
                return
            for hh in range(4):
                h = 4 * g + hh
                ch, po = h // 2, (h % 2) * 64
                P.mm(pg[0:T, hh, 2, 0:T], KR[po:po + 64, ch, 0, 0:T], BHf[po:po + 64, ch, 0:T])
            P.tt(Qm[g][0:T, :, 0:T], pg[0:T, :, 2, 0:T], NSL[0:T, 0:T].us(1).bc([T, 4, T]), AL.mult)
            P.tt(SPm[g][0:T, :, 0, 0:T], AM[g][0:T, :, 0, 0:T], IDB[0:T, 0:T].us(1).bc([T, 4, T]), AL.add)

        def neu_pre_mm(g):
            pg = pga(g)
            for hh in range(4):
                P.mm(pg[0:T, hh, 1, 0:T], Qm[g][0:T, hh, 0:T], AM[g][0:T, hh, 0, 0:T])
                P.mm(pg[0:T, hh, 2, 0:T], AM[g][0:T, hh, 0, 0:T], Qm[g][0:T, hh, 0:T])

        def neu_pq_ev(g):
            pg = pga(g)
            P.copy(SPm[g][0:T, :, 1, 0:T], pg[0:T, :, 1, 0:T], "act")
            P.copy(Qm[g][0:T, :, 0:T], pg[0:T, :, 2, 0:T], "act")

        def neu_mm(g, lastlev):
            pg = pga(g)
            for hh in range(4):
                if lastlev:
                    P.mm(pg[0:T, hh, 0, 0:T], Qm[g][0:T, hh, 0:T], SPm[g][0:T, hh, 0, 0:T])
                else:
                    mm2(pg, hh, 0, Qm[g][0:T, hh, 0:T], SPm[g][0:T, hh])
                    P.mm(pg[0:T, hh, 2, 0:T], SPm[g][0:T, hh, 1, 0:T], Qm[g][0:T, hh, 0:T])

        def neu_ev(g, lastlev):
            pg = pga(g)
            P.tt(SPm[g][0:T, :, 0, 0:T], SPm[g][0:T, :, 0, 0:T], pg[0:T, :, 0, 0:T], AL.add)
            if not lastlev:
                neu_pq_ev(g)

        for pair in range(2):
            ga, gb = 2 * pair, 2 * pair + 1
            a_mats(ga)
            if SUB < 5.1:
                return
            a_mats(gb)
            neu_pre_mm(ga)
            neu_pre_mm(gb)
            if SUB < 5.2:
                return
            neu_pq_ev(ga)
            neu_pq_ev(gb)
            if SUB < 5.3:
                return
            for lev in range(nlev):
                lastlev = lev == nlev - 1
                neu_mm(ga, lastlev)
                neu_mm(gb, lastlev)
                neu_ev(ga, lastlev)
                neu_ev(gb, lastlev)
        if SUB < 6:
            return
        av = psum(0, 0, 2)
        for h in range(16):
            P.mm(av[0:T, h * 64:(h + 1) * 64], AM[h // 4][0:T, h % 4, 2, 0:T], Vtm[0:T, h * 64:(h + 1) * 64])
        if SUB < 6.05:
            return
        P.copy(AV[0:T, :], av[0:T, :], "act")
        if SUB < 6.1:
            return
        kp = V(PG[1][:, :, :].rearrange("p a (b c) -> p (a b) c", c=128), "PSUM")
        for h in range(16):
            ch = h // 2
            P.mm(kp[:, h, 0:T], KTtm[0:T, ch * 128:(ch + 1) * 128], SPm[h // 4][0:T, h % 4, 0, 0:T])
        if SUB < 6.2:
            return
        kp4 = V(PG[1][:, :, :].rearrange("p a (b two c) -> p (a b) two c", two=2, c=128), "PSUM")
        P.copy(KPf[0:64, :, 0:T], kp4[0:64, :, 0, 0:T], "act")
        P.copy(KPf[64:128, :, 0:T], kp4[64:128, :, 1, 0:T], "dve")
        if SUB < 6.3:
            return
        vp = psum(0, 2, 2)
        for h in range(16):
            P.mm(vp[0:T, h * 64:(h + 1) * 64], SPm[h // 4][0:T, h % 4, 0, 0:T], AV[0:T, h * 64:(h + 1) * 64])
        P.copy(VP[0:T, :], vp[0:T, :], "act")
        if SUB < 7:
            return
        if first:
            if sb is None:
                P.memset(HS32, 0.0)
                P.memset(HSB, 0.0)
            else:
                P.dma(WKI[0:64], V(st_wkv[l, sb].rearrange("h i j -> i h j"), None))
                hp = psum3(1, 3, 1, 64)
                for ch in range(8):
                    P.tr(hp[:, ch, :], WKI[0:64, 2 * ch:2 * ch + 2, :].re("p a b -> p (a b)"), IDF[0:64, 0:64])
                P.copy(HS32, hp, "act")
                P.copy(HSB, hp, "dve")
        if SUB < 7.01:
            return
        um = psum(0, 0, 2)
        hpar = [2 * c for c in range(8)] + [2 * c + 1 for c in range(8)]
        for h in hpar:
            ch, po = h // 2, (h % 2) * 64
            c0 = (h % 2) * 512 + ch * 64
            P.mm(um[0:T, c0:c0 + 64], KPf[po:po + 64, ch, 0:T], HSB[po:po + 64, ch, :])
        if SUB < 7.02:
            return
        hm = lambda v: v[0:T, :].re("p (c two i) -> p c two i", two=2, i=64)
        pm = lambda v: v[0:T, :].re("p (two c i) -> p c two i", two=2, i=64)
        P.tt(hm(U), hm(VP), pm(um), AL.add)
        if SUB < 7.1:
            return
        yps = psum(0, 2, 2)
        for h in hpar:
            ch, po = h // 2, (h % 2) * 64
            c0 = (h % 2) * 512 + ch * 64
            P.mm(yps[0:T, c0:c0 + 64], KR[po:po + 64, ch, 1, 0:T], HSB[po:po + 64, ch, :])
        yps2 = psum(1, 2, 2)
        for h in range(16):
            o = yps2[0:T, h * 64:(h + 1) * 64]
            P.mm(o, AM[h // 4][0:T, h % 4, 1, 0:T], U[0:T, h * 64:(h + 1) * 64], True, False)
            P.mm(o, AM[h // 4][0:T, h % 4, 3, 0:T], Vtm[0:T, h * 64:(h + 1) * 64], False, True)
        if SUB < 7.2:
            return
        hn = psum3(1, 0, 2, 64)
        for h in range(16):
            ch = h // 2
            P.mm(hn[:, h, :], BHtm[0:T, ch * 128:(ch + 1) * 128], U[0:T, h * 64:(h + 1) * 64], True, False)
            P.mm(hn[:, h, :], KHtm[0:T, ch * 128:(ch + 1) * 128], Vtm[0:T, h * 64:(h + 1) * 64], False, True)
        if SUB < 7.3:
            return
        hn4 = V(PG[1][:, 0:2, :].rearrange("p a (b two c) -> p (a b) two c", two=2, c=64), "PSUM")
        P.tt(HTMP[0:64], HS32[0:64], hn4[0:64, :, 0, :], AL.add)
        P.tt(HTMP[64:128], HS32[64:128], hn4[64:128, :, 1, :], AL.add)
        P.tt(HS32, HTMP, GAM.us(2).bc([128, 8, 64]), AL.mult)
        P.copy(HSB, HS32, "act")
        P.copy(Y32[0:T, :], yps2[0:T, :], "act")
        P.tt(hm(Y32), hm(Y32), pm(yps), AL.add)
        if last:
            wk = psum3(1, 2, 2, 128)
            for ch in range(8):
                P.tr(wk[0:64, ch, :], HS32[:, ch, :], IDF)
            P.copy(WKO[0:64], wk[0:64], "act")
            dst = p_wkv[l] if sb is None else s_wkv[l, sb]
            P.dma(V(dst.rearrange("(c two) i j -> i c two j", two=2), None),
                  WKO[0:64].re("p c (two j) -> p c two j", two=2))
        if SUB < 8:
            return
        y3 = Y32[0:T, :].re("p (h i) -> p h i", i=64)
        mean = GST[0:T, 0:16]
        P.reduce(mean, y3)
        P.ts(mean, mean, -1.0 / 64, None, AL.mult)
        P.tt(y3, y3, mean.us(2).bc([T, 16, 64]), AL.add)
        P.act(SQT[0:T, :], Y32[0:T, :], AF.Square)
        var = GST[0:T, 16:32]
        P.reduce(var, SQT[0:T, :].re("p (h i) -> p h i", i=64))
        P.ts(var, var, 1.0 / 64, GN_EPS, AL.mult, AL.add)
        P.rsqrt(var, var)
        P.tt(y3, y3, var.us(2).bc([T, 16, 64]), AL.mult)
        ynT = psum3(0, 0, 2, 128)
        for ch in range(8):
            P.tr(ynT[:, ch, 0:T], Y32[0:T, ch * 128:(ch + 1) * 128], IDF[0:T, 0:T])
        ya = YA[:, :, 0:T]
        P.tt(ya, ynT[:, :, 0:T], bc8(49), AL.mult)
        P.tt(ya, ya, bc8(57), AL.add)
        P.tt(ya, ya, bon, AL.add)
        ga_ = GA[:, :, 0:T]
        dma_pj(ga_, 25, 33, tok0, tok0 + T)
        P.act(ga_, ga_, AF.Silu)
        P.tt(CATT[:, 0:8, 0:T], ya, ga_, AL.mult)
        if SUB < 9:
            return
        ext = EXT[:, :, 0:15 + T]
        if first and sb is None:
            P.memset(EXT[:, :, 0:15], 0.0)
            dma_pj(EXT[:, :, 15:15 + T], 33, 37, tok0, tok0 + T)
        elif first:
            P.dma(PIN[0:15, :], V(st_pool[l, sb], None))
            pp_ = psum3(0, 2, 1, 128)
            for g4 in range(4):
                P.tr(pp_[:, g4, 0:15], PIN[0:15, g4 * 128:(g4 + 1) * 128], IDF[0:15, 0:15])
            P.copy(EXT[:, :, 0:15], pp_[:, :, 0:15], "dve")
            dma_pj(EXT[:, :, 15:15 + T], 33, 37, tok0, tok0 + T)
        else:
            dma_pj(EXT[:, :, 0:15 + T], 33, 37, tok0 - 15, tok0 + T)
        Lx = 15 + T
        P.tt(WA[:, 0:4, 1:Lx], EXT[:, 0:4, 1:Lx], EXT[:, 0:4, 0:Lx - 1], AL.add)
        P.tt(WB[:, 1:4, 3:Lx], WA[:, 1:4, 3:Lx], WA[:, 1:4, 1:Lx - 2], AL.add)
        P.tt(WA[:, 2:4, 7:Lx], WB[:, 2:4, 7:Lx], WB[:, 2:4, 3:Lx - 4], AL.add)
        P.tt(WB[:, 3:4, 15:Lx], WA[:, 3:4, 15:Lx], WA[:, 3:4, 7:Lx - 8], AL.add)
        ic = IC0 if (first and sb is None) else IC1
        for g4 in range(4):
            wsrc = (WA, WB, WA, WB)[g4]
            P.tt(PTM[:, g4, 0:T], wsrc[:, g4, 15:Lx], ic[:, g4, 0:T], AL.mult)
        P.tt(PLD[:, :, 0:T], PTM[:, :, 0:T], EXT[:, :, 15:Lx], AL.subtract)
        mx = psum3(0, 3, 1, 128)
        for g4 in range(4):
            P.mm(mx[:, g4, 0:T], PW[:, g4, :], PLD[:, g4, 0:T])
        gb_ = GB[:, :, 0:T]
        dma_pj(gb_, 37, 41, tok0, tok0 + T)
        P.act(gb_, gb_, AF.Silu)
        P.tt(PTM[:, :, 0:T], mx[:, :, 0:T], PP[:, 73:77].us(2).bc([128, 4, T]), AL.mult)
        P.tt(CATT[:, 8:12, 0:T], PTM[:, :, 0:T], gb_, AL.mult)
        if last:
            po_ = psum(0, 2)
            for g4 in range(4):
                P.tr(po_[0:15, g4 * 128:(g4 + 1) * 128], EXT[:, g4, T:T + 15], IDF)
            P.copy(POUT[0:15, :], po_[0:15, :], "act")
            dst = p_pool[l] if sb is None else s_pool[l, sb]
            P.dma(V(dst, None), POUT[0:15, :])
        if SUB < 10:
            return
        vks = [("vc", rt, ch) for ch in range(45, 49)]
        P.dma(VC[0:T, :], VK(vcTM[tok0:tok0 + T, :], vks))
        cm = CST[0:T, 0:1]
        P.reduce(cm, VC[0:T, :])
        P.ts(cm, cm, -1.0 / 512, None, AL.mult)
        P.ts(VN32[0:T, :], VC[0:T, :], cm, None, AL.add)
        cv = CST[0:T, 1:2]
        P.memset(cv, 0.0)
        P.act(VC[0:T, :], VN32[0:T, :], AF.Square, accum=cv)
        P.ts(cv, cv, 1.0 / 512, LN_EPS, AL.mult, AL.add)
        P.rsqrt(cv, cv)
        P.stt(VN32[0:T, :], VN32[0:T, :], cv, LNG[0:T, :], AL.mult, AL.mult)
        P.copy(VNB[0:T, :], VN32[0:T, :], "act")
        if sb is not None:
            P.dma(V(s_cv[l, sb], None), VN32[0:T, :])
        mxc = psum3(1, 3, 1, 128)
        for g4 in range(4):
            P.mm(mxc[:, g4, 0:T], VNB[0:T, g4 * 128:(g4 + 1) * 128], WST[0:T, g4, 0:T])
        t1 = T1[:, :, 0:T]
        P.tt(t1, mxc[:, :, 0:T], BSB[:, :, 0:T], AL.add)
        uc = UC[:, :, 0:T]
        dma_pj(uc, 41, 45, tok0, tok0 + T)
        P.tt(t1, t1, uc, AL.mult)
        gc = GC[:, :, 0:T]
        dma_pj(gc, 49, 53, tok0, tok0 + T)
        P.act(gc, gc, AF.Silu)
        P.tt(CATT[:, 12:16, 0:T], t1, gc, AL.mult)
        if SUB < 11:
            return
        P.dma(V(catT[rt, :, :, col0:col0 + T], ("cat", rt)), CATT[:, :, 0:T])

    def phase_mix(l):
        layer_consts(l)
        for b in range(NSQ):
            tile_call(l, 8, SEQ_P + 8 * b, True, True, b, NRT - 1, 8 * b)
        for i in range(NPT):
            tile_call(l, 128, i * 128, i == 0, i == NPT - 1, None, i, 0)

    def phase_out(l):
        P.barrier()
        wv = w_out[l].rearrange("(kc p) n -> p kc n", p=128)
        for n4 in range(4):
            P.dma(WO[n4], V(wv[:, :, n4 * 512:(n4 + 1) * 512], None), q="pool")
        for ti, (r0, R) in enumerate(rtiles):
            ct = CT[ti % 2]
            xo = XO[ti % 2]
            P.dma(ct[:, :, 0:R], V(catT[ti, :, :, 0:R], ("cat", ti)))
            P.dma(xo[0:R], xsrc(l, r0, R))
            for n4 in range(4):
                ps = psum(ti % 2, n4)
                for kc in range(16):
                    P.mm(ps[0:R, :], ct[:, kc, 0:R], WO[n4][:, kc, :], kc == 0, kc == 15)
                P.tt(xo[0:R, n4 * 512:(n4 + 1) * 512], xo[0:R, n4 * 512:(n4 + 1) * 512], ps[0:R, :], AL.add)
            P.dma(V(xbuf[r0:r0 + R, :], ("xb", r0)), xo[0:R])

    import os
    STOP = int(os.environ.get("K_STOP", "99"))
    for l in range(NL):
        if STOP >= 1:
            phase_norm(l, norm_g[l:l + 1, :], False)
        if STOP >= 2:
            phase_proj(l)
        if STOP >= 3:
            layer_consts(l)
        if STOP == 4:
            tile_call(l, 8, SEQ_P, True, True, 0, NRT - 1, 0)
        if STOP == 5:
            tile_call(l, 128, 0, True, NPT == 1, None, 0, 0)
        if STOP >= 6:
            phase_mix(l)
        if STOP >= 7:
            phase_out(l)
    if STOP >= 8:
        phase_norm(NL, fnorm_g, True)
    P.barrier()

    with nc.Block() as block:
        def emit(name, e):
            for waits, fn, inc in P.ops[name]:
                for sem, val in waits:
                    e.wait_ge(sems[sem], val)
                if fn is not None:
                    fn(e).then_inc(sems[inc[0]], inc[1])

        @block.tensor
        def _(e):
            emit("pe", e)

        @block.scalar
        def _(e):
            emit("act", e)

        @block.vector
        def _(e):
            emit("dve", e)

        @block.gpsimd
        def _(e):
            emit("pool", e)

        @block.sync
        def _(e):
            emit("sp", e)
    es.close()
    stats = {k: len(v) for k, v in P.ops.items()}
    return nc, stats


def _consts():
    s = np.arange(128)[:, None]
    t = np.arange(128)[None, :]
    su = (s < t).astype(np.float32)
    ui = (s <= t).astype(np.float32)
    sl = (s > t).astype(np.float32)
    c = {}
    c["cmask4"] = np.concatenate([-su, -ui, su, ui], axis=1).astype(np.float32)
    c["cnsl"] = -sl
    c["ctris"] = (-math.exp(-0.5) * ui).astype(np.float32)
    c["cidf"] = np.eye(128, dtype=np.float32)
    blk = np.zeros((128, 128), np.float32)
    blk[:64, :64] = 1.0
    blk[64:, 64:] = 1.0
    c["cblk"] = blk
    wins = (2, 4, 8, 16)
    ic0 = np.zeros((128, 4, 128), np.float32)
    ic1 = np.zeros((128, 4, 128), np.float32)
    pos = np.arange(128)
    for g, w in enumerate(wins):
        ic0[:, g, :] = (1.0 / np.minimum(pos + 1, w))[None, :]
        ic1[:, g, :] = 1.0 / w
    c["cic0"] = ic0.reshape(128, 512)
    c["cic1"] = ic1.reshape(128, 512)
    return c


def _chunked(vec, n):
    NL = vec.shape[0]
    return np.ascontiguousarray(vec.reshape(NL, n, 128).transpose(0, 2, 1))


def make_in_maps(inp, NL, SEQ_P, NSQ, ncores, nprompt):
    f = lambda a: np.ascontiguousarray(np.asarray(a, dtype=np.float32))
    pp = np.concatenate([
        _chunked(f(inp["shift_mu"])[:NL], 25), _chunked(f(inp["k_k"])[:NL], 8), _chunked(f(inp["k_a"])[:NL], 8),
        _chunked(f(inp["r_k"])[:NL].reshape(NL, 1024), 8), _chunked(f(inp["lnx_g"])[:NL], 8),
        _chunked(f(inp["lnx_b"])[:NL], 8), _chunked(f(inp["a0"])[:NL], 8), _chunked(f(inp["pool_scale"])[:NL], 4)],
        axis=2)
    assert pp.shape[2] == NPP
    shared = {
        "norm_g": f(inp["norm_g"])[:NL], "fnorm_g": f(inp["final_norm_g"]).reshape(1, D),
        "w_in": f(inp["w_in"])[:NL], "w_out": f(inp["w_out"])[:NL], "pp": np.ascontiguousarray(pp),
        "w0": f(inp["w0"])[:NL].reshape(NL, 1, 1024),
        "lw": np.ascontiguousarray(np.concatenate([f(inp["w_up"])[:NL], f(inp["a_up"])[:NL]], axis=1)),
        "pool_w": f(inp["pool_w"])[:NL], "lng": f(inp["gmlp_ln_g"])[:NL].reshape(NL, 1, 512),
        "wsT": np.ascontiguousarray(f(inp["gmlp_ws"])[:NL].transpose(0, 3, 1, 2)),
        "gb": f(inp["gmlp_b"])[:NL].reshape(NL, 1, 512),
    }
    shared.update(_consts())
    xpr = f(inp["x_prompt"])
    xsa = f(inp["x_sample"])
    sts, stw, stp = f(inp["state_shift"]), f(inp["state_wkv"]), f(inp["state_pool"])
    maps = []
    for c in range(ncores):
        b = c % nprompt
        m = dict(shared)
        m["xp"] = np.ascontiguousarray(xpr[b, :SEQ_P])
        sl = slice(c * NSQ, (c + 1) * NSQ)
        m["xs"] = np.ascontiguousarray(xsa[sl].reshape(NSQ * 8, D))
        m["st_shift"] = np.ascontiguousarray(sts[:NL, sl])
        m["st_wkv"] = np.ascontiguousarray(stw[:NL, sl])
        m["st_pool"] = np.ascontiguousarray(stp[:NL, sl])
        maps.append(m)
    return maps


_CACHE = {}


def run(inp, NL, SEQ_P, NSQ, ncores, nprompt):
    key = (NL, SEQ_P, NSQ)
    if key not in _CACHE:
        _CACHE[key] = build(NL, SEQ_P, NSQ)[0]
    nc = _CACHE[key]
    maps = make_in_maps(inp, NL, SEQ_P, NSQ, ncores, nprompt)
    res = run_bass_kernel_spmd(nc, maps, core_ids=list(range(ncores)))
    R = res.results
    g = lambda name, cores: np.stack([np.asarray(R[c][name], dtype=np.float32) for c in cores])
    pc = list(range(nprompt))
    ac = list(range(ncores))
    y_prompt = g("yp", pc)
    y_sample = g("ys", ac).reshape(ncores * NSQ, 8, D)
    p_shift = g("p_shift", pc).transpose(1, 0, 2)
    p_wkv = g("p_wkv", pc).transpose(1, 0, 2, 3, 4)
    p_pool = g("p_pool", pc).transpose(1, 0, 2, 3)
    cat = lambda name: np.concatenate([np.asarray(R[c][name], dtype=np.float32) for c in ac], axis=1)
    return (y_prompt, y_sample, np.ascontiguousarray(p_shift), np.ascontiguousarray(p_wkv),
            np.ascontiguousarray(p_pool), cat("s_shift"), cat("s_wkv"), cat("s_pool"), cat("s_cv"))


def kernel(**inputs):
    return run(inputs, 4, 2048, 16, 8, 4)
```

```python
import contextlib
import math
import numpy as np
import concourse.bass as bass
import concourse.mybir as mybir
from concourse.bass_utils import run_bass_kernel_spmd

F32 = mybir.dt.float32
BF = mybir.dt.bfloat16
AL = mybir.AluOpType
AF = mybir.ActivationFunctionType
AX = mybir.AxisListType

D = 2048
DIN = 6784
NCH = 53
EPS = 1e-6
GN_EPS = 64 * 1e-5
LN_EPS = 1e-5
NPP = 77
ENG = ("pe", "act", "dve", "pool", "sp")
KD = 16


class V:
    __slots__ = ("ap", "key")

    def __init__(s, ap, key):
        s.ap = ap
        s.key = key

    def __getitem__(s, i):
        return V(s.ap[i], s.key)

    def bc(s, shape):
        return V(s.ap.broadcast_to(list(shape)), s.key)

    def re(s, pat, **kw):
        return V(s.ap.rearrange(pat, **kw), s.key)

    def us(s, ax):
        return V(s.ap.unsqueeze(ax), s.key)

    def cast(s, dt):
        return V(s.ap.bitcast(dt), s.key)


class VK(V):
    __slots__ = ("keys",)

    def __init__(s, ap, keys):
        V.__init__(s, ap, "MULTI")
        s.keys = keys


def keys_of(v):
    if v is None or not isinstance(v, V) or v.key is None:
        return []
    if isinstance(v, VK):
        return list(v.keys)
    if v.key == "PSUM":
        ap = v.ap
        es = 2 if ap.dtype == BF else 4
        pstep = ap.ap[0][0]
        off = ap.offset % pstep
        dims = ap.ap[1:]
        starts = [off]
        for (st, cnt) in dims[:-1]:
            starts = [s0 + st * i for s0 in starts for i in range(cnt)]
        lst, lcnt = dims[-1]
        banks = set()
        for s0 in starts:
            banks.add((s0 * es) // 2048)
            banks.add(((s0 + lst * (lcnt - 1)) * es) // 2048)
        return [(ap.name, b) for b in banks]
    return [v.key]


class Prog:
    def __init__(s):
        s.ops = {e: [] for e in ENG}
        s.cnt = {e: 0 for e in ENG}
        s.lastw = {}
        s.readers = {}
        s.seen = {e: {} for e in ENG}
        s.dq = {"sp": 0, "pool": 0}

    def _deps(s, eng, rk, wk):
        need = {}

        def add(ev):
            sem, val, src = ev
            if eng == "pe" and src == "pe":
                return
            if need.get(sem, 0) < val:
                need[sem] = val

        for k in rk:
            if k in s.lastw:
                add(s.lastw[k])
        for k in wk:
            if k in s.lastw:
                add(s.lastw[k])
            for sem, (val, src) in s.readers.get(k, {}).items():
                add((sem, val, src))
        waits = []
        for sem, val in need.items():
            if s.seen[eng].get(sem, 0) >= val:
                continue
            s.seen[eng][sem] = val
            waits.append((sem, val))
        return waits

    def _commit(s, ev, rk, wk):
        sem, val, src = ev
        for k in rk:
            s.readers.setdefault(k, {})[sem] = (val, src)
        for k in wk:
            s.lastw[k] = ev
            s.readers[k] = {}

    def op(s, eng, fn, reads, writes):
        rk = [k for v in reads for k in keys_of(v)]
        wk = [k for v in writes for k in keys_of(v)]
        wk += [k for k in rk if isinstance(k, tuple) and isinstance(k[0], str) and k[0].startswith("pg")]
        waits = s._deps(eng, rk, wk)
        s.cnt[eng] += 1
        ev = ("E_" + eng, s.cnt[eng], eng)
        s.ops[eng].append((waits, fn, ("E_" + eng, 1)))
        s._commit(ev, rk, wk)

    def dma(s, out, in_, q="sp"):
        rk = keys_of(in_)
        wk = keys_of(out)
        i = s.dq[q]
        s.dq[q] += 1
        slot = i % KD
        val = 16 * (i // KD + 1)
        sem = "D_%s_%d" % (q, slot)
        waits = s._deps(q, rk, wk)
        if i >= KD and s.seen[q].get(sem, 0) < val - 16:
            s.seen[q][sem] = val - 16
            waits.append((sem, val - 16))
        o, a = out.ap, in_.ap
        s.ops[q].append((waits, lambda e: e.dma_start(out=o, in_=a), (sem, 16)))
        s._commit((sem, val, "dma"), rk, wk)

    def barrier(s):
        for e in ENG:
            waits = []
            for e2 in ENG:
                c = s.cnt[e2]
                sem = "E_" + e2
                if c > 0 and s.seen[e].get(sem, 0) < c:
                    s.seen[e][sem] = c
                    waits.append((sem, c))
            for q in ("sp", "pool"):
                n = s.dq[q]
                for slot in range(KD):
                    u = (n - slot + KD - 1) // KD if n > slot else 0
                    sem = "D_%s_%d" % (q, slot)
                    if u > 0 and s.seen[e].get(sem, 0) < 16 * u:
                        s.seen[e][sem] = 16 * u
                        waits.append((sem, 16 * u))
            if waits:
                s.ops[e].append((waits, None, None))

    def mm(s, out, lhsT, rhs, start=True, stop=True):
        o, l, r = out.ap, lhsT.ap, rhs.ap
        s.op("pe", lambda e: e.matmul(o, l, r, start=start, stop=stop), [lhsT, rhs], [out])

    def tr(s, out, in_, ident):
        o, i, d = out.ap, in_.ap, ident.ap
        s.op("pe", lambda e: e.transpose(o, i, d), [in_, ident], [out])

    def act(s, out, in_, func, bias=None, scale=None, accum=None):
        o, i = out.ap, in_.ap
        kw = {}
        rd = [in_]
        wr = [out]
        if bias is not None:
            kw["bias"] = bias.ap
            rd.append(bias)
        if scale is not None:
            kw["scale"] = scale
        if accum is not None:
            kw["accum_out"] = accum.ap
            wr.append(accum)
        s.op("act", lambda e: e.activation(out=o, in_=i, func=func, **kw), rd, wr)

    def tt(s, out, in0, in1, op, eng="dve"):
        o, a, b = out.ap, in0.ap, in1.ap
        s.op(eng, lambda e: e.tensor_tensor(out=o, in0=a, in1=b, op=op), [in0, in1], [out])

    def ts(s, out, in0, s1, s2, op0, op1=None, eng="dve"):
        o, a = out.ap, in0.ap
        rd = [in0]
        x1 = s1
        x2 = s2
        if isinstance(s1, V):
            rd.append(s1)
            x1 = s1.ap
        if isinstance(s2, V):
            rd.append(s2)
            x2 = s2.ap
        if op1 is None:
            s.op(eng, lambda e: e.tensor_scalar(out=o, in0=a, scalar1=x1, scalar2=None, op0=op0), rd, [out])
        else:
            s.op(eng, lambda e: e.tensor_scalar(out=o, in0=a, scalar1=x1, scalar2=x2, op0=op0, op1=op1), rd, [out])

    def stt(s, out, in0, scalar, in1, op0, op1):
        o, a, b = out.ap, in0.ap, in1.ap
        rd = [in0, in1]
        sc = scalar
        if isinstance(scalar, V):
            rd.append(scalar)
            sc = scalar.ap
        s.op("dve", lambda e: e.scalar_tensor_tensor(out=o, in0=a, scalar=sc, in1=b, op0=op0, op1=op1), rd, [out])

    def copy(s, out, in_, eng="act"):
        o, i = out.ap, in_.ap
        if eng == "act":
            s.op("act", lambda e: e.activation(out=o, in_=i, func=AF.Copy), [in_], [out])
        else:
            s.op(eng, lambda e: e.tensor_copy(out=o, in_=i), [in_], [out])

    def rsqrt(s, out, in_):
        o, i = out.ap, in_.ap
        s.op("act", lambda e: e.activation(out=o, in_=i, func=AF.Sqrt), [in_], [out])
        s.op("dve", lambda e: e.reciprocal(out=o, in_=o), [out], [out])

    def memset(s, out, val, eng="dve"):
        o = out.ap
        s.op(eng, lambda e: e.memset(o, val), [], [out])

    def reduce(s, out, in_, op=AL.add):
        o, i = out.ap, in_.ap
        s.op("dve", lambda e: e.tensor_reduce(out=o, in_=i, axis=AX.X, op=op), [in_], [out])


def build(NL, SEQ_P, NSQ):
    NPT = SEQ_P // 128
    SR = NSQ * 8
    NTOK = SEQ_P + SR
    rtiles = [(i * 128, 128) for i in range(NPT)] + [(SEQ_P, SR)]
    NRT = len(rtiles)
    tgroups = []
    t0 = 0
    while t0 < NTOK:
        n = min(512, NTOK - t0)
        tgroups.append((t0, n))
        t0 += n

    nc = bass.Bass("TRN2", target_bir_lowering=False)

    def dram(name, shape, dt=F32, kind="ExternalInput"):
        return nc.dram_tensor(name, list(shape), dt, kind=kind).ap()

    xp = dram("xp", [SEQ_P, D])
    xs = dram("xs", [SR, D])
    st_shift = dram("st_shift", [NL, NSQ, 3200])
    st_wkv = dram("st_wkv", [NL, NSQ, 16, 64, 64])
    st_pool = dram("st_pool", [NL, NSQ, 15, 512])
    norm_g = dram("norm_g", [NL, D])
    fnorm_g = dram("fnorm_g", [1, D])
    w_in = dram("w_in", [NL, D, DIN])
    w_out = dram("w_out", [NL, D, D])
    PPd = dram("pp", [NL, 128, NPP])
    w0d = dram("w0", [NL, 1, 1024])
    LWd = dram("lw", [NL, 128, 1024])
    pwd = dram("pool_w", [NL, 4, 128, 128])
    lngd = dram("lng", [NL, 1, 512])
    wsTd = dram("wsT", [NL, 128, 4, 128])
    gbd = dram("gb", [NL, 1, 512])
    cM4 = dram("cmask4", [128, 512])
    cNSL = dram("cnsl", [128, 128])
    cTRIS = dram("ctris", [128, 128])
    cIDF = dram("cidf", [128, 128])
    cBLK = dram("cblk", [128, 128])
    cIC0 = dram("cic0", [128, 512])
    cIC1 = dram("cic1", [128, 512])

    yp = dram("yp", [SEQ_P, D], kind="ExternalOutput")
    ys = dram("ys", [SR, D], kind="ExternalOutput")
    p_shift = dram("p_shift", [NL, 3200], kind="ExternalOutput")
    p_wkv = dram("p_wkv", [NL, 16, 64, 64], kind="ExternalOutput")
    p_pool = dram("p_pool", [NL, 15, 512], kind="ExternalOutput")
    s_shift = dram("s_shift", [NL, NSQ, 3200], kind="ExternalOutput")
    s_wkv = dram("s_wkv", [NL, NSQ, 16, 64, 64], kind="ExternalOutput")
    s_pool = dram("s_pool", [NL, NSQ, 15, 512], kind="ExternalOutput")
    s_cv = dram("s_cv", [NL, NSQ, 8, 512], kind="ExternalOutput")

    projT = dram("projT", [NCH * 128, NTOK], kind="Internal")
    vcTM = dram("vcTM", [NTOK, 512], kind="Internal")
    xbuf = dram("xbuf", [NTOK, D], kind="Internal")
    catT = dram("catT", [NRT, 128, 16, 128], BF, kind="Internal")

    projV = projT.rearrange("(ch p) t -> p ch t", p=128)

    P = Prog()
    es = contextlib.ExitStack()
    ARW = 47104
    AR = es.enter_context(nc.sbuf_tensor("arena", [128, ARW], F32))
    PG = [es.enter_context(nc.psum_tensor("pg%d" % i, [128, 4, 512], F32)) for i in range(2)]
    sems = {}
    for e in ENG:
        sems["E_" + e] = es.enter_context(nc.semaphore("E_" + e))
    for q in ("sp", "pool"):
        for i in range(KD):
            n = "D_%s_%d" % (q, i)
            sems[n] = es.enter_context(nc.semaphore(n))

    cur = [0]

    def carve(shape, dt, key, at=None):
        esz = 2 if dt == BF else 4
        n = 1
        for x in shape[1:]:
            n *= x
        words = (n * esz + 3) // 4
        words = (words + 7) // 8 * 8
        if at is None:
            at = cur[0]
            cur[0] += words
        assert at + words <= ARW, ("arena overflow", key, at, words)
        ap = AR[0:shape[0], at:at + words]
        if dt == BF:
            ap = ap.bitcast(BF)
        ap = ap[:, 0:n]
        if len(shape) == 3:
            ap = ap.rearrange("p (a b) -> p a b", b=shape[2])
        elif len(shape) == 4:
            ap = ap.rearrange("p (a b c) -> p a b c", b=shape[2], c=shape[3])
        v = V(ap, key)
        return v, at

    def psum(g, b0, nb=1):
        return V(PG[g][:, b0:b0 + nb, :].rearrange("p a b -> p (a b)"), "PSUM")

    def psum3(g, b0, nb, inner):
        return V(PG[g][:, b0:b0 + nb, :].rearrange("p a (b c) -> p (a b) c", c=inner), "PSUM")

    def psbf(g, b):
        return V(PG[g][:, b, :].bitcast(BF), "PSUM")

    MASK4, _ = carve([128, 4, 128], F32, "MASK4")
    NSL, _ = carve([128, 128], F32, "NSL")
    TRIS, _ = carve([128, 128], F32, "TRIS")
    IDF, _ = carve([128, 128], F32, "IDF")
    IDB, _ = carve([128, 128], BF, "IDB")
    BLK1, _ = carve([128, 128], BF, "BLK1")
    IC0, _ = carve([128, 4, 128], F32, "IC0")
    IC1, _ = carve([128, 4, 128], F32, "IC1")
    ONESF, _ = carve([128, 128], F32, "ONESF")
    PP, _ = carve([128, NPP], F32, "PP")
    OMK, _ = carve([128, 8], F32, "OMK")
    W0R, _ = carve([128, 1024], F32, "W0R")
    LW, _ = carve([128, 1024], BF, "LW")
    PW, _ = carve([128, 4, 128], BF, "PW")
    LNG, _ = carve([128, 512], F32, "LNG")
    BSB, _ = carve([128, 4, 128], F32, "BSB")
    WSTF, _ = carve([128, 4, 128], F32, "WSTF")
    WST, _ = carve([128, 4, 128], BF, "WST")
    base = cur[0]

    P.dma(MASK4, V(cM4.rearrange("p (a b) -> p a b", b=128), None))
    P.dma(NSL, V(cNSL, None))
    P.dma(TRIS, V(cTRIS, None))
    P.dma(IDF, V(cIDF, None))
    P.dma(IDB, V(cIDF, None), q="pool")
    P.dma(BLK1, V(cBLK, None), q="pool")
    P.dma(IC0, V(cIC0.rearrange("p (a b) -> p a b", b=128), None))
    P.dma(IC1, V(cIC1.rearrange("p (a b) -> p a b", b=128), None))
    P.memset(ONESF, 1.0)

    cur[0] = base
    HT, _ = carve([128, 16, NTOK], BF, "HT")
    abase = cur[0]
    XT = [carve([128, D], F32, "XT%d" % i)[0] for i in range(2)]
    HB = [carve([128, D], BF, "HB%d" % i)[0] for i in range(2)]
    NGB, _ = carve([128, D], F32, "NGB")
    SQJ, _ = carve([128, D], BF, "SQJ")
    SSA, _ = carve([128, 8], F32, "SSA")
    cur[0] = abase
    WCH = [carve([128, 16, 128], BF, "WCH%d" % i)[0] for i in range(3)]
    STG = [carve([128, 512], F32, "STG%d" % i)[0] for i in range(4)]
    cur[0] = base
    WO = [carve([128, 16, 512], BF, "WO%d" % i)[0] for i in range(4)]
    CT = [carve([128, 16, 128], BF, "CT%d" % i)[0] for i in range(2)]
    XO = [carve([128, D], F32, "XO%d" % i)[0] for i in range(2)]
    cur[0] = base
    WKO, _ = carve([128, 8, 128], F32, "WKO")
    SHO, _ = carve([128, 128], F32, "SHO")
    POUT, _ = carve([128, 512], F32, "POUT")

    NAMES = ("PS XS AM SPm AT EL ELI KK RN SIG KK2 KR KHf BHf VBF TW GAM KTtm KHtm BHtm Vtm Qm AV KPf U "
             "HTMP WKI GST SHI CATT EXT WA WB PLD PTM GB PIN VC VN32 VNB UC GC T1 CST "
             "Y32 VP GA YA SQT BON HS32 HSB").split()

    def make_set(tg, W):
        k = lambda n: tg + n
        big = W == 128
        d = {}
        d["PS"], _ = carve([128, 25, W + 1], F32, k("PS"))
        d["XS"], _ = carve([128, 25, W], F32, k("XS"))
        d["AM"] = [carve([128, 4, 4, W], BF, k("AM%d" % g))[0] for g in range(4)]
        d["SPm"] = [carve([128, 4, 2, W], BF, k("SP%d" % g))[0] for g in range(4)]
        d["AT"], _ = carve([128, 8, W], F32, k("AT"))
        d["EL"], _ = carve([128, 8, W + 1], F32, k("EL"))
        d["ELI"], _ = carve([128, 8, W], F32, k("ELI"))
        d["KK"], _ = carve([128, 8, W], F32, k("KK"))
        d["RN"], _ = carve([128, 8, W], F32, k("RN"))
        d["SIG"], _ = carve([128, 1024], F32, k("SIG"))
        d["KK2"], _ = carve([128, 8, W], BF, k("KK2"))
        d["KR"], _ = carve([128, 8, 2, W], BF, k("KR"))
        d["KHf"], _ = carve([128, 8, W], BF, k("KHf"))
        d["BHf"], _ = carve([128, 8, W], BF, k("BHf"))
        d["VBF"], _ = carve([128, 8, W], BF, k("VBF"))
        d["TW"], _ = carve([128, W], BF, k("TW"))
        d["GAM"], _ = carve([128, 8], F32, k("GAM"))
        for n in ("KTtm", "KHtm", "BHtm", "Vtm", "AV", "U"):
            d[n], _ = carve([128, 1024], BF, k(n))
        d["Qm"] = [carve([128, 4, W], BF, k("Q%d" % g))[0] for g in range(4)]
        d["KPf"], _ = carve([128, 8, W], BF, k("KPf"))
        d["HTMP"], _ = carve([128, 8, 64], F32, k("HTMP"))
        d["WKI"] = carve([128, 16, 64], F32, k("WKI"))[0]
        d["GST"], _ = carve([128, 64], F32, k("GST"))
        d["SHI"] = carve([128, 128], F32, k("SHI"))[0]
        d["CATT"], _ = carve([128, 16, W], BF, k("CATT"))
        for n in ("EXT", "WA", "WB"):
            d[n], _ = carve([128, 4, 15 + W], F32, k(n))
        d["PLD"], _ = carve([128, 4, W], BF, k("PLD"))
        d["PTM"], _ = carve([128, 4, W], F32, k("PTM"))
        d["GB"], _ = carve([128, 4, W], F32, k("GB"))
        d["PIN"] = carve([128, 512], F32, k("PIN"))[0]
        d["VC"], _ = carve([128, 512], F32, k("VC"))
        d["VN32"], _ = carve([128, 512], F32, k("VN32"))
        d["VNB"], _ = carve([128, 512], BF, k("VNB"))
        for n in ("UC", "GC", "T1"):
            d[n], _ = carve([128, 4, W], F32, k(n))
        d["CST"], _ = carve([128, 16], F32, k("CST"))
        d["HS32"], _ = carve([128, 8, 64], F32, k("HS32"))
        d["HSB"], _ = carve([128, 8, 64], BF, k("HSB"))
        if big:
            d["Y32"] = V(d["AT"].ap.rearrange("p a b -> p (a b)"), k("AT"))
            d["VP"] = V(d["EL"].ap.rearrange("p a b -> p (a b)")[:, 0:1024], k("EL"))
            d["GA"] = d["ELI"]
            d["YA"] = d["KK"]
            d["SQT"] = V(d["RN"].ap.rearrange("p a b -> p (a b)"), k("RN"))
            d["BON"] = V(d["SIG"].ap.rearrange("p (a b) -> p a b", b=128), k("SIG"))
        else:
            d["Y32"], _ = carve([128, 1024], F32, k("Y32"))
            d["VP"], _ = carve([128, 1024], F32, k("VP"))
            d["GA"] = d["ELI"]
            d["YA"] = d["KK"]
            d["SQT"] = d["SIG"]
            d["BON"], _ = carve([128, 8, W], F32, k("BON"))
        return tuple(d[n] for n in NAMES)

    BP = make_set("p.", 128)
    print("arena words used", cur[0], "of", ARW)

    def xsrc(l, r0, R):
        if l == 0:
            if r0 < SEQ_P:
                return V(xp[r0:r0 + R, :], None)
            return V(xs[0:R, :], None)
        return V(xbuf[r0:r0 + R, :], ("xb", r0))

    def phase_norm(l, gsrc, final):
        P.barrier()
        P.dma(NGB, V(gsrc.broadcast_to([128, D]), None))
        for ti, (r0, R) in enumerate(rtiles):
            xt = XT[ti % 2]
            hb = HB[ti % 2]
            P.dma(xt[0:R], xsrc(l, r0, R))
            ss = SSA[:, (ti % 2) * 2:(ti % 2) * 2 + 1]
            P.memset(ss[0:R], 0.0)
            P.act(SQJ[0:R], xt[0:R], AF.Square, accum=ss[0:R])
            rs = SSA[:, (ti % 2) * 2 + 1:(ti % 2) * 2 + 2]
            P.ts(rs[0:R], ss[0:R], 1.0 / D, EPS, AL.mult, AL.add)
            P.rsqrt(rs[0:R], rs[0:R])
            if final:
                P.stt(xt[0:R], xt[0:R], rs[0:R], NGB[0:R], AL.mult, AL.mult)
                if r0 < SEQ_P:
                    P.dma(V(yp[r0:r0 + R, :], None), xt[0:R])
                else:
                    P.dma(V(ys[0:R, :], None), xt[0:R])
                continue
            P.stt(hb[0:R], xt[0:R], rs[0:R], NGB[0:R], AL.mult, AL.mult)
            g = ti % 2
            for half in range(2):
                pb = psbf(g, half + 2 * ((ti // 2) % 2))
                for k in range(8):
                    kc = half * 8 + k
                    P.tr(pb[:, k * 128:k * 128 + R], hb[0:R, kc * 128:(kc + 1) * 128], IDB[0:R, 0:R])
                src = pb.re("p (a b) -> p a b", b=128)[:, :, 0:R]
                dst = HT[:, half * 8:half * 8 + 8, r0:r0 + R]
                if half == 0:
                    P.copy(dst, src, "act")
                else:
                    P.copy(dst, src, "dve")

    def phase_proj(l):
        P.barrier()
        wv = w_in[l].rearrange("(kc p) c -> p kc c", p=128)
        bank = 0
        sg = 0
        for ch in range(NCH):
            wch = WCH[ch % 3]
            P.dma(wch, V(wv[:, :, ch * 128:(ch + 1) * 128], None), q="pool")
            if 45 <= ch < 49:
                for ti, (r0, R) in enumerate(rtiles):
                    ps = psum(bank // 4, bank % 4)
                    for kc in range(16):
                        P.mm(ps[0:R, 0:128], HT[:, kc, r0:r0 + R], wch[:, kc, :], kc == 0, kc == 15)
                    st = STG[sg % 4]
                    if sg % 2 == 0:
                        P.copy(st[0:R, 0:128], ps[0:R, 0:128], "act")
                    else:
                        P.copy(st[0:R, 0:128], ps[0:R, 0:128], "dve")
                    P.dma(V(vcTM[r0:r0 + R, (ch - 45) * 128:(ch - 44) * 128], ("vc", ti, ch)), st[0:R, 0:128])
                    bank = (bank + 1) % 8
                    sg += 1
                continue
            for gi, (t0, n) in enumerate(tgroups):
                ps = psum(bank // 4, bank % 4)
                for kc in range(16):
                    P.mm(ps[:, 0:n], wch[:, kc, :], HT[:, kc, t0:t0 + n], kc == 0, kc == 15)
                st = STG[sg % 4]
                if sg % 2 == 0:
                    P.copy(st[:, 0:n], ps[:, 0:n], "act")
                else:
                    P.copy(st[:, 0:n], ps[:, 0:n], "dve")
                P.dma(V(projT[ch * 128:(ch + 1) * 128, t0:t0 + n], ("pj", ch, gi)), st[:, 0:n])
                bank = (bank + 1) % 8
                sg += 1

    def pj(c0, c1, a, b):
        ks = set()
        for gi, (t0, n) in enumerate(tgroups):
            if a < t0 + n and b > t0:
                for ch in range(c0, c1):
                    ks.add(("pj", ch, gi))
        return projV[:, c0:c1, a:b], sorted(ks)

    def dma_pj(dst, c0, c1, a, b):
        ap, ks = pj(c0, c1, a, b)
        P.dma(dst, VK(ap, ks))

    def layer_consts(l):
        P.barrier()
        P.dma(PP, V(PPd[l], None))
        P.dma(W0R[0:1], V(w0d[l], None))
        P.dma(LW, V(LWd[l], None), q="pool")
        P.dma(PW, V(pwd[l].rearrange("g c d -> c g d"), None), q="pool")
        P.dma(LNG, V(lngd[l].broadcast_to([128, 512]), None))
        P.dma(BSB, V(gbd[l].broadcast_to([128, 512]).rearrange("p (a b) -> p a b", b=128), None))
        P.dma(WSTF, V(wsTd[l], None))
        P.tt(WST, WSTF, MASK4[:, 3:4, :].bc([128, 4, 128]), AL.mult)
        P.ts(OMK, PP[:, 33:41], -1.0, 1.0, AL.mult, AL.add)

    def tile_call(B, l, T, tok0, first, last, sb, rt, col0):
        (PS, XS, AM, SPm, AT, EL, ELI, KK, RN, SIG, KK2, KR, KHf, BHf, VBF, TW, GAM, KTtm, KHtm, BHtm, Vtm, Qm,
         AV, KPf, U, HTMP, WKI, GST, SHI, CATT, EXT, WA, WB, PLD, PTM, GB, PIN, VC, VN32, VNB, UC, GC, T1, CST,
         Y32, VP, GA, YA, SQT, BON, HS32, HSB) = B
        nlev = int(math.log2(T)) - 1
        bc8 = lambda c0: PP[:, c0:c0 + 8].us(2).bc([128, 8, T])
        Lx = 15 + T

        def pool_load():
            if first and sb is None:
                P.memset(EXT[:, :, 0:15], 0.0)
                dma_pj(EXT[:, :, 15:15 + T], 33, 37, tok0, tok0 + T)
            elif first:
                P.dma(PIN[0:15, :], V(st_pool[l, sb], None))
                dma_pj(EXT[:, :, 15:15 + T], 33, 37, tok0, tok0 + T)
            else:
                dma_pj(EXT[:, :, 0:15 + T], 33, 37, tok0 - 15, tok0 + T)
            dma_pj(GB[:, :, 0:T], 37, 41, tok0, tok0 + T)

        def cg_load():
            vks = [("vc", rt, ch) for ch in range(45, 49)]
            P.dma(VC[0:T, :], VK(vcTM[tok0:tok0 + T, :], vks))
            dma_pj(UC[:, :, 0:T], 41, 45, tok0, tok0 + T)
            dma_pj(GC[:, :, 0:T], 49, 53, tok0, tok0 + T)

        def pool_dve():
            if first and sb is not None:
                pp_ = psum3(0, 2, 1, 128)
                for g4 in range(4):
                    P.tr(pp_[:, g4, 0:15], PIN[0:15, g4 * 128:(g4 + 1) * 128], IDF[0:15, 0:15])
                P.copy(EXT[:, :, 0:15], pp_[:, :, 0:15], "dve")
            P.tt(WA[:, 0:4, 1:Lx], EXT[:, 0:4, 1:Lx], EXT[:, 0:4, 0:Lx - 1], AL.add)
            P.tt(WB[:, 1:4, 3:Lx], WA[:, 1:4, 3:Lx], WA[:, 1:4, 1:Lx - 2], AL.add)
            P.tt(WA[:, 2:4, 7:Lx], WB[:, 2:4, 7:Lx], WB[:, 2:4, 3:Lx - 4], AL.add)
            P.tt(WB[:, 3:4, 15:Lx], WA[:, 3:4, 15:Lx], WA[:, 3:4, 7:Lx - 8], AL.add)
            ic = IC0 if (first and sb is None) else IC1
            for g4 in range(4):
                wsrc = (WA, WB, WA, WB)[g4]
                P.tt(PTM[:, g4, 0:T], wsrc[:, g4, 15:Lx], ic[:, g4, 0:T], AL.mult)
            P.tt(PLD[:, :, 0:T], PTM[:, :, 0:T], EXT[:, :, 15:Lx], AL.subtract)

        def cg_dve1():
            cm = CST[0:T, 0:1]
            P.reduce(cm, VC[0:T, :])
            P.ts(cm, cm, -1.0 / 512, None, AL.mult)
            P.ts(VN32[0:T, :], VC[0:T, :], cm, None, AL.add)

        def cg_mid():
            cv = CST[0:T, 1:2]
            P.memset(cv, 0.0)
            P.act(VC[0:T, :], VN32[0:T, :], AF.Square, accum=cv)
            P.ts(cv, cv, 1.0 / 512, LN_EPS, AL.mult, AL.add)
            P.rsqrt(cv, cv)
            P.stt(VN32[0:T, :], VN32[0:T, :], cv, LNG[0:T, :], AL.mult, AL.mult)
            P.copy(VNB[0:T, :], VN32[0:T, :], "act")
            if sb is not None:
                P.dma(V(s_cv[l, sb], None), VN32[0:T, :], q="pool")
            P.act(GB[:, :, 0:T], GB[:, :, 0:T], AF.Silu)
            P.act(GC[:, :, 0:T], GC[:, :, 0:T], AF.Silu)

        def pool_tail():
            mx = psum3(0, 3, 1, 128)
            for g4 in range(4):
                P.mm(mx[:, g4, 0:T], PW[:, g4, :], PLD[:, g4, 0:T])
            P.tt(PTM[:, :, 0:T], mx[:, :, 0:T], PP[:, 73:77].us(2).bc([128, 4, T]), AL.mult)
            P.tt(CATT[:, 8:12, 0:T], PTM[:, :, 0:T], GB[:, :, 0:T], AL.mult)
            if last:
                po_ = psum(0, 2)
                for g4 in range(4):
                    P.tr(po_[0:15, g4 * 128:(g4 + 1) * 128], EXT[:, g4, T:T + 15], IDF)
                P.copy(POUT[0:15, :], po_[0:15, :], "act")
                dst = p_pool[l] if sb is None else s_pool[l, sb]
                P.dma(V(dst, None), POUT[0:15, :], q="pool")

        def cg_tail():
            mxc = psum3(1, 3, 1, 128)
            for g4 in range(4):
                P.mm(mxc[:, g4, 0:T], VNB[0:T, g4 * 128:(g4 + 1) * 128], WST[0:T, g4, 0:T])
            t1 = T1[:, :, 0:T]
            P.tt(t1, mxc[:, :, 0:T], BSB[:, :, 0:T], AL.add)
            P.tt(t1, t1, UC[:, :, 0:T], AL.mult)
            P.tt(CATT[:, 12:16, 0:T], t1, GC[:, :, 0:T], AL.mult)

        if first and sb is None:
            P.memset(PS[:, :, 0:1], 0.0)
            dma_pj(PS[:, :, 1:T + 1], 0, 25, tok0, tok0 + T)
        elif first:
            P.dma(SHI[0:25, :], V(st_shift[l, sb].rearrange("(ch p) -> ch p", p=128), None))
            pt = psum(1, 3)
            P.tr(pt[:, 0:25], SHI[0:25, :], IDF[0:25, 0:25])
            P.copy(PS[:, :, 0], pt[:, 0:25], "dve")
            dma_pj(PS[:, :, 1:T + 1], 0, 25, tok0, tok0 + T)
        else:
            dma_pj(PS[:, :, 0:T + 1], 0, 25, tok0 - 1, tok0 + T)
        if last:
            pt = psum(1, 3)
            P.tr(pt[0:25, 0:128], PS[:, :, T], IDF)
            P.copy(SHO[0:25, :], pt[0:25, 0:128], "act")
            dst = p_shift[l] if sb is None else s_shift[l, sb]
            P.dma(V(dst.rearrange("(ch p) -> ch p", p=128), None), SHO[0:25, :], q="pool")
        pool_load()
        cg_load()
        P.memset(EL[:, :, 0:1], 1.0)
        xs_ = XS[:, :, 0:T]
        P.tt(xs_, PS[:, :, 0:T], PS[:, :, 1:T + 1], AL.subtract)
        P.tt(xs_, xs_, PP[:, 0:25].us(2).bc([128, 25, T]), AL.mult)
        P.tt(xs_, xs_, PS[:, :, 1:T + 1], AL.add)
        Xr = XS[:, 0:8, 0:T]
        Xk = XS[:, 8:16, 0:T]
        Xv = XS[:, 16:24, 0:T]
        pool_dve()
        cg_dve1()
        yield
        P.act(TW[0:64, 0:T], XS[0:64, 24, 0:T], AF.Tanh)
        P.act(TW[64:128, 0:T], XS[64:128, 24, 0:T], AF.Copy)
        zt = psum(0, 0, 2)
        for h2 in range(2):
            P.mm(zt[0:T, h2 * 512:(h2 + 1) * 512], TW[0:64, 0:T], LW[0:64, h2 * 512:(h2 + 1) * 512], True, False)
            P.mm(zt[0:T, h2 * 512:(h2 + 1) * 512], ONESF[0:1, 0:T], W0R[0:1, h2 * 512:(h2 + 1) * 512], False, True)
        P.act(SIG[0:T, :], zt[0:T, :], AF.Sigmoid)
        lt = psum3(0, 2, 2, 128)
        for ch in range(8):
            P.mm(lt[:, ch, 0:T], SIG[0:T, ch * 128:(ch + 1) * 128], TRIS[0:T, 0:T])
        P.act(EL[:, :, 1:T + 1], lt[:, :, 0:T], AF.Exp)
        P.act(ELI[:, :, 0:T], lt[:, :, 0:T], AF.Exp, scale=-1.0)
        at = psum3(1, 0, 2, 128)
        for ch in range(8):
            P.mm(at[:, ch, 0:T], LW[64:128, ch * 128:(ch + 1) * 128], TW[64:128, 0:T])
        a_ = AT[:, :, 0:T]
        P.tt(a_, at[:, :, 0:T], bc8(65), AL.add)
        P.act(a_, a_, AF.Sigmoid)
        yield
        kk = KK[:, :, 0:T]
        P.tt(kk, Xk, bc8(25), AL.mult)
        P.act(KK2[:, :, 0:T], kk, AF.Square)
        hs = psum3(1, 2, 2, 128)
        for ch in range(8):
            P.mm(hs[:, ch, 0:T], BLK1, KK2[:, ch, 0:T])
        rn = RN[:, :, 0:T]
        P.ts(rn, hs[:, :, 0:T], 1e-12, None, AL.max)
        P.rsqrt(rn, rn)
        P.tt(kk, kk, rn, AL.mult)
        P.tt(rn, a_, bc8(33), AL.mult)
        P.tt(rn, rn, OMK.us(2).bc([128, 8, T]), AL.add)
        P.tt(rn, rn, Xk, AL.mult)
        P.tt(a_, kk, a_, AL.mult)
        P.tt(KR[:, :, 0, 0:T], kk, EL[:, :, 0:T], AL.mult)
        P.tt(KR[:, :, 1, 0:T], Xr, EL[:, :, 1:T + 1], AL.mult)
        P.tt(KHf[:, :, 0:T], rn, ELI[:, :, 0:T], AL.mult)
        P.tt(BHf[:, :, 0:T], a_, ELI[:, :, 0:T], AL.mult)
        P.copy(GAM, EL[:, :, T], "dve")
        bon = BON[:, :, 0:T]
        P.tt(bon, Xr, bc8(41), AL.mult)
        P.tt(KK2[:, :, 0:T], bon, rn, AL.mult)
        bs = psum3(0, 0, 2, 128)
        for ch in range(8):
            P.mm(bs[:, ch, 0:T], BLK1, KK2[:, ch, 0:T])
        P.tt(bon, bs[:, :, 0:T], Xv, AL.mult)
        P.copy(VBF[:, :, 0:T], Xv, "act")
        cg_mid()
        yield
        for qi, (src, dst) in enumerate(((KR[:, :, 0, :], KTtm), (KHf, KHtm), (BHf, BHtm), (VBF, Vtm))):
            pb = psbf(1, qi)
            for ch in range(8):
                P.tr(pb[0:T, ch * 128:(ch + 1) * 128], src[:, ch, 0:T], IDB)
            if qi == 2:
                P.act(dst[0:T, :], pb[0:T, :], AF.Copy, scale=-1.0)
            else:
                P.copy(dst[0:T, :], pb[0:T, :], "act" if qi % 2 == 0 else "dve")
        pool_tail()
        cg_tail()
        yield "neu"
        m4 = MASK4[0:T, :, 0:T].us(1).bc([T, 4, 4, T])

        def pga(g):
            return V(PG[g % 2][:, :, :].rearrange("p h (a b) -> p h a b", b=128), "PSUM")

        def mm2(pg, hh, s0, lhsT, rhs3):
            if T == 128:
                P.mm(pg[0:T, hh, s0:s0 + 2, :].re("p a b -> p (a b)"), lhsT, rhs3.re("p a b -> p (a b)"))
            else:
                P.mm(pg[0:T, hh, s0, 0:T], lhsT, rhs3[:, 0, 0:T])
                P.mm(pg[0:T, hh, s0 + 1, 0:T], lhsT, rhs3[:, 1, 0:T])

        def a_mats(g):
            pg = pga(g)
            for hh in range(4):
                h = 4 * g + hh
                ch, po = h // 2, (h % 2) * 64
                mm2(pg, hh, 0, BHf[po:po + 64, ch, 0:T], KR[po:po + 64, ch])
                mm2(pg, hh, 2, KHf[po:po + 64, ch, 0:T], KR[po:po + 64, ch])
            P.tt(AM[g][0:T, :, :, 0:T], pg[0:T, :, :, 0:T], m4, AL.mult)
            for hh in range(4):
                h = 4 * g + hh
                ch, po = h // 2, (h % 2) * 64
                P.mm(pg[0:T, hh, 2, 0:T], KR[po:po + 64, ch, 0, 0:T], BHf[po:po + 64, ch, 0:T])
            P.tt(Qm[g][0:T, :, 0:T], pg[0:T, :, 2, 0:T], NSL[0:T, 0:T].us(1).bc([T, 4, T]), AL.mult)
            P.tt(SPm[g][0:T, :, 0, 0:T], AM[g][0:T, :, 0, 0:T], IDB[0:T, 0:T].us(1).bc([T, 4, T]), AL.add)

        def neu_pre_mm(g):
            pg = pga(g)
            for hh in range(4):
                P.mm(pg[0:T, hh, 1, 0:T], Qm[g][0:T, hh, 0:T], AM[g][0:T, hh, 0, 0:T])
                P.mm(pg[0:T, hh, 2, 0:T], AM[g][0:T, hh, 0, 0:T], Qm[g][0:T, hh, 0:T])

        def neu_pq_ev(g):
            pg = pga(g)
            P.copy(SPm[g][0:T, :, 1, 0:T], pg[0:T, :, 1, 0:T], "act")
            P.copy(Qm[g][0:T, :, 0:T], pg[0:T, :, 2, 0:T], "act")

        def neu_mm(g, lastlev):
            pg = pga(g)
            for hh in range(4):
                if lastlev:
                    P.mm(pg[0:T, hh, 0, 0:T], Qm[g][0:T, hh, 0:T], SPm[g][0:T, hh, 0, 0:T])
                else:
                    mm2(pg, hh, 0, Qm[g][0:T, hh, 0:T], SPm[g][0:T, hh])
                    P.mm(pg[0:T, hh, 2, 0:T], SPm[g][0:T, hh, 1, 0:T], Qm[g][0:T, hh, 0:T])

        def neu_ev(g, lastlev):
            pg = pga(g)
            P.tt(SPm[g][0:T, :, 0, 0:T], SPm[g][0:T, :, 0, 0:T], pg[0:T, :, 0, 0:T], AL.add)
            if not lastlev:
                neu_pq_ev(g)

        for pair in range(2):
            ga, gb = 2 * pair, 2 * pair + 1
            a_mats(ga)
            a_mats(gb)
            yield "neu"
            neu_pre_mm(ga)
            neu_pre_mm(gb)
            neu_pq_ev(ga)
            neu_pq_ev(gb)
            for lev in range(nlev):
                yield "neu"
                lastlev = lev == nlev - 1
                neu_mm(ga, lastlev)
                neu_mm(gb, lastlev)
                neu_ev(ga, lastlev)
                neu_ev(gb, lastlev)
            yield "neu" if pair == 0 else None
        av = psum(0, 0, 2)
        for h in range(16):
            P.mm(av[0:T, h * 64:(h + 1) * 64], AM[h // 4][0:T, h % 4, 2, 0:T], Vtm[0:T, h * 64:(h + 1) * 64])
        P.copy(AV[0:T, :], av[0:T, :], "act")
        yield
        kp = V(PG[1][:, :, :].rearrange("p a (b c) -> p (a b) c", c=128), "PSUM")
        for h in range(16):
            ch = h // 2
            P.mm(kp[:, h, 0:T], KTtm[0:T, ch * 128:(ch + 1) * 128], SPm[h // 4][0:T, h % 4, 0, 0:T])
        kp4 = V(PG[1][:, :, :].rearrange("p a (b two c) -> p (a b) two c", two=2, c=128), "PSUM")
        P.copy(KPf[0:64, :, 0:T], kp4[0:64, :, 0, 0:T], "act")
        P.copy(KPf[64:128, :, 0:T], kp4[64:128, :, 1, 0:T], "dve")
        yield
        vp = psum(0, 2, 2)
        for h in range(16):
            P.mm(vp[0:T, h * 64:(h + 1) * 64], SPm[h // 4][0:T, h % 4, 0, 0:T], AV[0:T, h * 64:(h + 1) * 64])
        P.copy(VP[0:T, :], vp[0:T, :], "act")
        yield
        if first:
            if sb is None:
                P.memset(HS32, 0.0)
                P.memset(HSB, 0.0)
            else:
                P.dma(WKI[0:64], V(st_wkv[l, sb].rearrange("h i j -> i h j"), None))
                hp = psum3(1, 3, 1, 64)
                for ch in range(8):
                    P.tr(hp[:, ch, :], WKI[0:64, 2 * ch:2 * ch + 2, :].re("p a b -> p (a b)"), IDF[0:64, 0:64])
                P.copy(HS32, hp, "act")
                P.copy(HSB, hp, "dve")
        um = psum(0, 0, 2)
        hpar = [2 * c for c in range(8)] + [2 * c + 1 for c in range(8)]
        for h in hpar:
            ch, po = h // 2, (h % 2) * 64
            c0 = (h % 2) * 512 + ch * 64
            P.mm(um[0:T, c0:c0 + 64], KPf[po:po + 64, ch, 0:T], HSB[po:po + 64, ch, :])
        hm = lambda v: v[0:T, :].re("p (c two i) -> p c two i", two=2, i=64)
        pm = lambda v: v[0:T, :].re("p (two c i) -> p c two i", two=2, i=64)
        P.tt(hm(U), hm(VP), pm(um), AL.add)
        yield
        yps = psum(0, 2, 2)
        for h in hpar:
            ch, po = h // 2, (h % 2) * 64
            c0 = (h % 2) * 512 + ch * 64
            P.mm(yps[0:T, c0:c0 + 64], KR[po:po + 64, ch, 1, 0:T], HSB[po:po + 64, ch, :])
        yps2 = psum(1, 2, 2)
        for h in range(16):
            o = yps2[0:T, h * 64:(h + 1) * 64]
            P.mm(o, AM[h // 4][0:T, h % 4, 1, 0:T], U[0:T, h * 64:(h + 1) * 64], True, False)
            P.mm(o, AM[h // 4][0:T, h % 4, 3, 0:T], Vtm[0:T, h * 64:(h + 1) * 64], False, True)
        P.copy(Y32[0:T, :], yps2[0:T, :], "act")
        P.tt(hm(Y32), hm(Y32), pm(yps), AL.add)
        yield
        hn = psum3(1, 0, 2, 64)
        for h in range(16):
            ch = h // 2
            P.mm(hn[:, h, :], BHtm[0:T, ch * 128:(ch + 1) * 128], U[0:T, h * 64:(h + 1) * 64], True, False)
            P.mm(hn[:, h, :], KHtm[0:T, ch * 128:(ch + 1) * 128], Vtm[0:T, h * 64:(h + 1) * 64], False, True)
        hn4 = V(PG[1][:, 0:2, :].rearrange("p a (b two c) -> p (a b) two c", two=2, c=64), "PSUM")
        P.tt(HTMP[0:64], HS32[0:64], hn4[0:64, :, 0, :], AL.add)
        P.tt(HTMP[64:128], HS32[64:128], hn4[64:128, :, 1, :], AL.add)
        P.tt(HS32, HTMP, GAM.us(2).bc([128, 8, 64]), AL.mult)
        P.copy(HSB, HS32, "act")
        yield
        if last:
            wk = psum3(1, 2, 2, 128)
            for ch in range(8):
                P.tr(wk[0:64, ch, :], HS32[:, ch, :], IDF)
            P.copy(WKO[0:64], wk[0:64], "act")
            dst = p_wkv[l] if sb is None else s_wkv[l, sb]
            P.dma(V(dst.rearrange("(c two) i j -> i c two j", two=2), None),
                  WKO[0:64].re("p c (two j) -> p c two j", two=2), q="pool")
        yield
        y3 = Y32[0:T, :].re("p (h i) -> p h i", i=64)
        mean = GST[0:T, 0:16]
        P.reduce(mean, y3)
        P.ts(mean, mean, -1.0 / 64, None, AL.mult)
        P.tt(y3, y3, mean.us(2).bc([T, 16, 64]), AL.add)
        P.act(SQT[0:T, :], Y32[0:T, :], AF.Square)
        var = GST[0:T, 16:32]
        P.reduce(var, SQT[0:T, :].re("p (h i) -> p h i", i=64))
        P.ts(var, var, 1.0 / 64, GN_EPS, AL.mult, AL.add)
        P.rsqrt(var, var)
        P.tt(y3, y3, var.us(2).bc([T, 16, 64]), AL.mult)
        ynT = psum3(0, 0, 2, 128)
        for ch in range(8):
            P.tr(ynT[:, ch, 0:T], Y32[0:T, ch * 128:(ch + 1) * 128], IDF[0:T, 0:T])
        ya = YA[:, :, 0:T]
        P.tt(ya, ynT[:, :, 0:T], bc8(49), AL.mult)
        P.tt(ya, ya, bc8(57), AL.add)
        P.tt(ya, ya, bon, AL.add)
        ga_ = GA[:, :, 0:T]
        dma_pj(ga_, 25, 33, tok0, tok0 + T)
        P.act(ga_, ga_, AF.Silu)
        P.tt(CATT[:, 0:8, 0:T], ya, ga_, AL.mult)
        yield
        P.dma(V(catT[rt, :, :, col0:col0 + T], ("cat", rt)), CATT[:, :, 0:T], q="pool")

    def phase_mix(l):
        layer_consts(l)

        def chain(gens):
            for g in gens:
                yield from g

        for b in range(NSQ):
            for _ in tile_call(BP, l, 8, SEQ_P + 8 * b, True, True, b, NRT - 1, 8 * b):
                pass
        for i in range(NPT):
            for _ in tile_call(BP, l, 128, i * 128, i == 0, i == NPT - 1, None, i, 0):
                pass

    def phase_out(l):
        P.barrier()
        wv = w_out[l].rearrange("(kc p) n -> p kc n", p=128)
        for n4 in range(4):
            P.dma(WO[n4], V(wv[:, :, n4 * 512:(n4 + 1) * 512], None), q="pool")
        for ti, (r0, R) in enumerate(rtiles):
            ct = CT[ti % 2]
            xo = XO[ti % 2]
            P.dma(ct[:, :, 0:R], V(catT[ti, :, :, 0:R], ("cat", ti)))
            P.dma(xo[0:R], xsrc(l, r0, R))
            for n4 in range(4):
                ps = psum(ti % 2, n4)
                for kc in range(16):
                    P.mm(ps[0:R, :], ct[:, kc, 0:R], WO[n4][:, kc, :], kc == 0, kc == 15)
                P.tt(xo[0:R, n4 * 512:(n4 + 1) * 512], xo[0:R, n4 * 512:(n4 + 1) * 512], ps[0:R, :], AL.add)
            P.dma(V(xbuf[r0:r0 + R, :], ("xb", r0)), xo[0:R])

    for l in range(NL):
        phase_norm(l, norm_g[l:l + 1, :], False)
        phase_proj(l)
        phase_mix(l)
        phase_out(l)
    phase_norm(NL, fnorm_g, True)
    P.barrier()

    with nc.Block() as block:
        def emit(name, e):
            for waits, fn, inc in P.ops[name]:
                for sem, val in waits:
                    e.wait_ge(sems[sem], val)
                if fn is not None:
                    fn(e).then_inc(sems[inc[0]], inc[1])

        @block.tensor
        def _(e):
            emit("pe", e)

        @block.scalar
        def _(e):
            emit("act", e)

        @block.vector
        def _(e):
            emit("dve", e)

        @block.gpsimd
        def _(e):
            emit("pool", e)

        @block.sync
        def _(e):
            emit("sp", e)
    es.close()
    stats = {k: len(v) for k, v in P.ops.items()}
    return nc, stats


def _consts():
    s = np.arange(128)[:, None]
    t = np.arange(128)[None, :]
    su = (s < t).astype(np.float32)
    ui = (s <= t).astype(np.float32)
    sl = (s > t).astype(np.float32)
    c = {}
    c["cmask4"] = np.concatenate([-su, -ui, su, ui], axis=1).astype(np.float32)
    c["cnsl"] = -sl
    c["ctris"] = (-math.exp(-0.5) * ui).astype(np.float32)
    c["cidf"] = np.eye(128, dtype=np.float32)
    blk = np.zeros((128, 128), np.float32)
    blk[:64, :64] = 1.0
    blk[64:, 64:] = 1.0
    c["cblk"] = blk
    wins = (2, 4, 8, 16)
    ic0 = np.zeros((128, 4, 128), np.float32)
    ic1 = np.zeros((128, 4, 128), np.float32)
    pos = np.arange(128)
    for g, w in enumerate(wins):
        ic0[:, g, :] = (1.0 / np.minimum(pos + 1, w))[None, :]
        ic1[:, g, :] = 1.0 / w
    c["cic0"] = ic0.reshape(128, 512)
    c["cic1"] = ic1.reshape(128, 512)
    return c


def _chunked(vec, n):
    NL = vec.shape[0]
    return np.ascontiguousarray(vec.reshape(NL, n, 128).transpose(0, 2, 1))


def make_in_maps(inp, NL, SEQ_P, NSQ, ncores, nprompt):
    f = lambda a: np.ascontiguousarray(np.asarray(a, dtype=np.float32))
    pp = np.concatenate([
        _chunked(f(inp["shift_mu"])[:NL], 25), _chunked(f(inp["k_k"])[:NL], 8), _chunked(f(inp["k_a"])[:NL], 8),
        _chunked(f(inp["r_k"])[:NL].reshape(NL, 1024), 8), _chunked(f(inp["lnx_g"])[:NL], 8),
        _chunked(f(inp["lnx_b"])[:NL], 8), _chunked(f(inp["a0"])[:NL], 8), _chunked(f(inp["pool_scale"])[:NL], 4)],
        axis=2)
    assert pp.shape[2] == NPP
    shared = {
        "norm_g": f(inp["norm_g"])[:NL], "fnorm_g": f(inp["final_norm_g"]).reshape(1, D),
        "w_in": f(inp["w_in"])[:NL], "w_out": f(inp["w_out"])[:NL], "pp": np.ascontiguousarray(pp),
        "w0": f(inp["w0"])[:NL].reshape(NL, 1, 1024),
        "lw": np.ascontiguousarray(np.concatenate([f(inp["w_up"])[:NL], f(inp["a_up"])[:NL]], axis=1)),
        "pool_w": f(inp["pool_w"])[:NL], "lng": f(inp["gmlp_ln_g"])[:NL].reshape(NL, 1, 512),
        "wsT": np.ascontiguousarray(f(inp["gmlp_ws"])[:NL].transpose(0, 3, 1, 2)),
        "gb": f(inp["gmlp_b"])[:NL].reshape(NL, 1, 512),
    }
    shared.update(_consts())
    xpr = f(inp["x_prompt"])
    xsa = f(inp["x_sample"])
    sts, stw, stp = f(inp["state_shift"]), f(inp["state_wkv"]), f(inp["state_pool"])
    maps = []
    for c in range(ncores):
        b = c % nprompt
        m = dict(shared)
        m["xp"] = np.ascontiguousarray(xpr[b, :SEQ_P])
        sl = slice(c * NSQ, (c + 1) * NSQ)
        m["xs"] = np.ascontiguousarray(xsa[sl].reshape(NSQ * 8, D))
        m["st_shift"] = np.ascontiguousarray(sts[:NL, sl])
        m["st_wkv"] = np.ascontiguousarray(stw[:NL, sl])
        m["st_pool"] = np.ascontiguousarray(stp[:NL, sl])
        maps.append(m)
    return maps


_CACHE = {}


def run(inp, NL, SEQ_P, NSQ, ncores, nprompt):
    key = (NL, SEQ_P, NSQ)
    if key not in _CACHE:
        _CACHE[key] = build(NL, SEQ_P, NSQ)[0]
    nc = _CACHE[key]
    maps = make_in_maps(inp, NL, SEQ_P, NSQ, ncores, nprompt)
    res = run_bass_kernel_spmd(nc, maps, core_ids=list(range(ncores)))
    R = res.results
    g = lambda name, cores: np.stack([np.asarray(R[c][name], dtype=np.float32) for c in cores])
    pc = list(range(nprompt))
    ac = list(range(ncores))
    y_prompt = g("yp", pc)
    y_sample = g("ys", ac).reshape(ncores * NSQ, 8, D)
    p_shift = g("p_shift", pc).transpose(1, 0, 2)
    p_wkv = g("p_wkv", pc).transpose(1, 0, 2, 3, 4)
    p_pool = g("p_pool", pc).transpose(1, 0, 2, 3)
    cat = lambda name: np.concatenate([np.asarray(R[c][name], dtype=np.float32) for c in ac], axis=1)
    return (y_prompt, y_sample, np.ascontiguousarray(p_shift), np.ascontiguousarray(p_wkv),
            np.ascontiguousarray(p_pool), cat("s_shift"), cat("s_wkv"), cat("s_pool"), cat("s_cv"))


def kernel(**inputs):
    return run(inputs, 4, 2048, 16, 8, 4)
```

```python
import contextlib
import math
import numpy as np
import concourse.bass as bass
import concourse.mybir as mybir
from concourse.bass_utils import run_bass_kernel_spmd

F32 = mybir.dt.float32
BF = mybir.dt.bfloat16
AL = mybir.AluOpType
AF = mybir.ActivationFunctionType
AX = mybir.AxisListType

D = 2048
DIN = 6784
NCH = 53
EPS = 1e-6
GN_EPS = 64 * 1e-5
LN_EPS = 1e-5
NPP = 77
ENG = ("pe", "act", "dve", "pool", "sp")
KD = 16


class V:
    __slots__ = ("ap", "key")

    def __init__(s, ap, key):
        s.ap = ap
        s.key = key

    def __getitem__(s, i):
        return V(s.ap[i], s.key)

    def bc(s, shape):
        return V(s.ap.broadcast_to(list(shape)), s.key)

    def re(s, pat, **kw):
        return V(s.ap.rearrange(pat, **kw), s.key)

    def us(s, ax):
        return V(s.ap.unsqueeze(ax), s.key)

    def cast(s, dt):
        return V(s.ap.bitcast(dt), s.key)


class VK(V):
    __slots__ = ("keys",)

    def __init__(s, ap, keys):
        V.__init__(s, ap, "MULTI")
        s.keys = keys


def keys_of(v):
    if v is None or not isinstance(v, V) or v.key is None:
        return []
    if isinstance(v, VK):
        return list(v.keys)
    if v.key == "PSUM":
        ap = v.ap
        es = 2 if ap.dtype == BF else 4
        pstep = ap.ap[0][0]
        off = ap.offset % pstep
        dims = ap.ap[1:]
        starts = [off]
        for (st, cnt) in dims[:-1]:
            starts = [s0 + st * i for s0 in starts for i in range(cnt)]
        lst, lcnt = dims[-1]
        banks = set()
        for s0 in starts:
            banks.add((s0 * es) // 2048)
            banks.add(((s0 + lst * (lcnt - 1)) * es) // 2048)
        return [(ap.name, b) for b in banks]
    return [v.key]


class Prog:
    def __init__(s):
        s.ops = {e: [] for e in ENG}
        s.cnt = {e: 0 for e in ENG}
        s.lastw = {}
        s.readers = {}
        s.seen = {e: {} for e in ENG}
        s.dq = {"sp": 0, "pool": 0, "act": 0}

    def _deps(s, eng, rk, wk):
        need = {}

        def add(ev):
            sem, val, src = ev
            if eng == "pe" and src == "pe":
                return
            if need.get(sem, 0) < val:
                need[sem] = val

        for k in rk:
            if k in s.lastw:
                add(s.lastw[k])
        for k in wk:
            if k in s.lastw:
                add(s.lastw[k])
            for sem, (val, src) in s.readers.get(k, {}).items():
                add((sem, val, src))
        waits = []
        for sem, val in need.items():
            if s.seen[eng].get(sem, 0) >= val:
                continue
            s.seen[eng][sem] = val
            waits.append((sem, val))
        return waits

    def _commit(s, ev, rk, wk):
        sem, val, src = ev
        for k in rk:
            s.readers.setdefault(k, {})[sem] = (val, src)
        for k in wk:
            s.lastw[k] = ev
            s.readers[k] = {}

    def op(s, eng, fn, reads, writes):
        rk = [k for v in reads for k in keys_of(v)]
        wk = [k for v in writes for k in keys_of(v)]
        wk += [k for k in rk if isinstance(k, tuple) and isinstance(k[0], str) and k[0].startswith("pg")]
        waits = s._deps(eng, rk, wk)
        s.cnt[eng] += 1
        ev = ("E_" + eng, s.cnt[eng], eng)
        s.ops[eng].append((waits, fn, ("E_" + eng, 1)))
        s._commit(ev, rk, wk)

    def dma(s, out, in_, q="sp"):
        rk = keys_of(in_)
        wk = keys_of(out)
        i = s.dq[q]
        s.dq[q] += 1
        slot = i % KD
        val = 16 * (i // KD + 1)
        sem = "D_%s_%d" % (q, slot)
        waits = s._deps(q, rk, wk)
        if i >= KD and s.seen[q].get(sem, 0) < val - 16:
            s.seen[q][sem] = val - 16
            waits.append((sem, val - 16))
        o, a = out.ap, in_.ap
        s.ops[q].append((waits, lambda e: e.dma_start(out=o, in_=a), (sem, 16)))
        s._commit((sem, val, "dma"), rk, wk)

    def barrier(s):
        for e in ENG:
            waits = []
            for e2 in ENG:
                c = s.cnt[e2]
                sem = "E_" + e2
                if c > 0 and s.seen[e].get(sem, 0) < c:
                    s.seen[e][sem] = c
                    waits.append((sem, c))
            for q in ("sp", "pool", "act"):
                n = s.dq[q]
                for slot in range(KD):
                    u = (n - slot + KD - 1) // KD if n > slot else 0
                    sem = "D_%s_%d" % (q, slot)
                    if u > 0 and s.seen[e].get(sem, 0) < 16 * u:
                        s.seen[e][sem] = 16 * u
                        waits.append((sem, 16 * u))
            if waits:
                s.ops[e].append((waits, None, None))

    def mm(s, out, lhsT, rhs, start=True, stop=True):
        o, l, r = out.ap, lhsT.ap, rhs.ap
        s.op("pe", lambda e: e.matmul(o, l, r, start=start, stop=stop), [lhsT, rhs], [out])

    def tr(s, out, in_, ident):
        o, i, d = out.ap, in_.ap, ident.ap
        s.op("pe", lambda e: e.transpose(o, i, d), [in_, ident], [out])

    def act(s, out, in_, func, bias=None, scale=None, accum=None):
        o, i = out.ap, in_.ap
        kw = {}
        rd = [in_]
        wr = [out]
        if bias is not None:
            kw["bias"] = bias.ap
            rd.append(bias)
        if scale is not None:
            kw["scale"] = scale
        if accum is not None:
            kw["accum_out"] = accum.ap
            wr.append(accum)
        s.op("act", lambda e: e.activation(out=o, in_=i, func=func, **kw), rd, wr)

    def tt(s, out, in0, in1, op, eng="dve"):
        o, a, b = out.ap, in0.ap, in1.ap
        s.op(eng, lambda e: e.tensor_tensor(out=o, in0=a, in1=b, op=op), [in0, in1], [out])

    def ts(s, out, in0, s1, s2, op0, op1=None, eng="dve"):
        o, a = out.ap, in0.ap
        rd = [in0]
        x1 = s1
        x2 = s2
        if isinstance(s1, V):
            rd.append(s1)
            x1 = s1.ap
        if isinstance(s2, V):
            rd.append(s2)
            x2 = s2.ap
        if op1 is None:
            s.op(eng, lambda e: e.tensor_scalar(out=o, in0=a, scalar1=x1, scalar2=None, op0=op0), rd, [out])
        else:
            s.op(eng, lambda e: e.tensor_scalar(out=o, in0=a, scalar1=x1, scalar2=x2, op0=op0, op1=op1), rd, [out])

    def stt(s, out, in0, scalar, in1, op0, op1):
        o, a, b = out.ap, in0.ap, in1.ap
        rd = [in0, in1]
        sc = scalar
        if isinstance(scalar, V):
            rd.append(scalar)
            sc = scalar.ap
        s.op("dve", lambda e: e.scalar_tensor_tensor(out=o, in0=a, scalar=sc, in1=b, op0=op0, op1=op1), rd, [out])

    def copy(s, out, in_, eng="act"):
        o, i = out.ap, in_.ap
        if eng == "act":
            s.op("act", lambda e: e.activation(out=o, in_=i, func=AF.Copy), [in_], [out])
        else:
            s.op(eng, lambda e: e.tensor_copy(out=o, in_=i), [in_], [out])

    def rsqrt(s, out, in_):
        o, i = out.ap, in_.ap
        s.op("act", lambda e: e.activation(out=o, in_=i, func=AF.Sqrt), [in_], [out])
        s.op("dve", lambda e: e.reciprocal(out=o, in_=o), [out], [out])

    def memset(s, out, val, eng="dve"):
        o = out.ap
        s.op(eng, lambda e: e.memset(o, val), [], [out])

    def reduce(s, out, in_, op=AL.add):
        o, i = out.ap, in_.ap
        s.op("dve", lambda e: e.tensor_reduce(out=o, in_=i, axis=AX.X, op=op), [in_], [out])


def build(NL, SEQ_P, NSQ):
    NPT = SEQ_P // 128
    SR = NSQ * 8
    NTOK = SEQ_P + SR
    rtiles = [(i * 128, 128) for i in range(NPT)] + [(SEQ_P, SR)]
    NRT = len(rtiles)
    tgroups = []
    t0 = 0
    while t0 < NTOK:
        n = min(512, NTOK - t0)
        tgroups.append((t0, n))
        t0 += n

    nc = bass.Bass("TRN2", target_bir_lowering=False)

    def dram(name, shape, dt=F32, kind="ExternalInput"):
        return nc.dram_tensor(name, list(shape), dt, kind=kind).ap()

    xp = dram("xp", [SEQ_P, D])
    xs = dram("xs", [SR, D])
    st_shift = dram("st_shift", [NL, NSQ, 3200])
    st_wkv = dram("st_wkv", [NL, NSQ, 16, 64, 64])
    st_pool = dram("st_pool", [NL, NSQ, 15, 512])
    norm_g = dram("norm_g", [NL, D])
    fnorm_g = dram("fnorm_g", [1, D])
    w_in = dram("w_in", [NL, D, DIN])
    w_out = dram("w_out", [NL, D, D])
    PPd = dram("pp", [NL, 128, NPP])
    w0d = dram("w0", [NL, 1, 1024])
    LWd = dram("lw", [NL, 128, 1024])
    pwd = dram("pool_w", [NL, 4, 128, 128])
    lngd = dram("lng", [NL, 1, 512])
    wsTd = dram("wsT", [NL, 128, 4, 128])
    gbd = dram("gb", [NL, 1, 512])
    cM4 = dram("cmask4", [128, 512])
    cNSL = dram("cnsl", [128, 128])
    cTRIS = dram("ctris", [128, 128])
    cIDF = dram("cidf", [128, 128])
    cBLK = dram("cblk", [128, 128])
    cIC0 = dram("cic0", [128, 512])
    cIC1 = dram("cic1", [128, 512])

    yp = dram("yp", [SEQ_P, D], kind="ExternalOutput")
    ys = dram("ys", [SR, D], kind="ExternalOutput")
    p_shift = dram("p_shift", [NL, 3200], kind="ExternalOutput")
    p_wkv = dram("p_wkv", [NL, 16, 64, 64], kind="ExternalOutput")
    p_pool = dram("p_pool", [NL, 15, 512], kind="ExternalOutput")
    s_shift = dram("s_shift", [NL, NSQ, 3200], kind="ExternalOutput")
    s_wkv = dram("s_wkv", [NL, NSQ, 16, 64, 64], kind="ExternalOutput")
    s_pool = dram("s_pool", [NL, NSQ, 15, 512], kind="ExternalOutput")
    s_cv = dram("s_cv", [NL, NSQ, 8, 512], kind="ExternalOutput")

    projT = dram("projT", [NCH * 128, NTOK], kind="Internal")
    vcTM = dram("vcTM", [NTOK, 512], kind="Internal")
    xbuf = dram("xbuf", [NTOK, D], kind="Internal")
    catT = dram("catT", [NRT, 128, 16, 128], BF, kind="Internal")

    projV = projT.rearrange("(ch p) t -> p ch t", p=128)

    P = Prog()
    es = contextlib.ExitStack()
    ARW = 47104
    AR = es.enter_context(nc.sbuf_tensor("arena", [128, ARW], F32))
    PG = [es.enter_context(nc.psum_tensor("pg%d" % i, [128, 4, 512], F32)) for i in range(2)]
    sems = {}
    for e in ENG:
        sems["E_" + e] = es.enter_context(nc.semaphore("E_" + e))
    for q in ("sp", "pool", "act"):
        for i in range(KD):
            n = "D_%s_%d" % (q, i)
            sems[n] = es.enter_context(nc.semaphore(n))

    cur = [0]

    def carve(shape, dt, key, at=None):
        esz = 2 if dt == BF else 4
        n = 1
        for x in shape[1:]:
            n *= x
        words = (n * esz + 3) // 4
        words = (words + 7) // 8 * 8
        if at is None:
            at = cur[0]
            cur[0] += words
        assert at + words <= ARW, ("arena overflow", key, at, words)
        ap = AR[0:shape[0], at:at + words]
        if dt == BF:
            ap = ap.bitcast(BF)
        ap = ap[:, 0:n]
        if len(shape) == 3:
            ap = ap.rearrange("p (a b) -> p a b", b=shape[2])
        elif len(shape) == 4:
            ap = ap.rearrange("p (a b c) -> p a b c", b=shape[2], c=shape[3])
        v = V(ap, key)
        return v, at

    def psum(g, b0, nb=1):
        return V(PG[g][:, b0:b0 + nb, :].rearrange("p a b -> p (a b)"), "PSUM")

    def psum3(g, b0, nb, inner):
        return V(PG[g][:, b0:b0 + nb, :].rearrange("p a (b c) -> p (a b) c", c=inner), "PSUM")

    def psbf(g, b):
        return V(PG[g][:, b, :].bitcast(BF), "PSUM")

    MASK4, _ = carve([128, 4, 128], F32, "MASK4")
    NSL, _ = carve([128, 128], F32, "NSL")
    TRIS, _ = carve([128, 128], F32, "TRIS")
    IDF, _ = carve([128, 128], F32, "IDF")
    IDB, _ = carve([128, 128], BF, "IDB")
    BLK1, _ = carve([128, 128], BF, "BLK1")
    IC0, _ = carve([128, 4, 128], F32, "IC0")
    IC1, _ = carve([128, 4, 128], F32, "IC1")
    ONESF, _ = carve([128, 128], F32, "ONESF")
    PP, _ = carve([128, NPP], F32, "PP")
    OMK, _ = carve([128, 8], F32, "OMK")
    W0R, _ = carve([128, 1024], F32, "W0R")
    LW, _ = carve([128, 1024], BF, "LW")
    PW, _ = carve([128, 4, 128], BF, "PW")
    LNG, _ = carve([128, 512], F32, "LNG")
    BSB, _ = carve([128, 4, 128], F32, "BSB")
    WSTF, _ = carve([128, 4, 128], F32, "WSTF")
    WST, _ = carve([128, 4, 128], BF, "WST")
    base = cur[0]

    P.dma(MASK4, V(cM4.rearrange("p (a b) -> p a b", b=128), None))
    P.dma(NSL, V(cNSL, None))
    P.dma(TRIS, V(cTRIS, None))
    P.dma(IDF, V(cIDF, None))
    P.dma(IDB, V(cIDF, None), q="pool")
    P.dma(BLK1, V(cBLK, None), q="pool")
    P.dma(IC0, V(cIC0.rearrange("p (a b) -> p a b", b=128), None))
    P.dma(IC1, V(cIC1.rearrange("p (a b) -> p a b", b=128), None))
    P.memset(ONESF, 1.0)

    cur[0] = base
    HT, _ = carve([128, 16, NTOK], BF, "HT")
    abase = cur[0]
    XT = [carve([128, D], F32, "XT%d" % i)[0] for i in range(2)]
    HB = [carve([128, D], BF, "HB%d" % i)[0] for i in range(2)]
    NGB, _ = carve([128, D], F32, "NGB")
    SQJ, _ = carve([128, D], BF, "SQJ")
    SSA, _ = carve([128, 8], F32, "SSA")
    cur[0] = abase
    WCH = [carve([128, 16, 512], BF, "WCH%d" % i)[0] for i in range(3)]
    STG = [carve([128, 512], F32, "STG%d" % i)[0] for i in range(4)]
    cur[0] = base
    WO = [carve([128, 16, 512], BF, "WO%d" % i)[0] for i in range(4)]
    CT = [carve([128, 16, 128], BF, "CT%d" % i)[0] for i in range(2)]
    XO = [carve([128, D], F32, "XO%d" % i)[0] for i in range(2)]
    cur[0] = base
    WKO, _ = carve([128, 8, 128], F32, "WKO")
    SHO, _ = carve([128, 128], F32, "SHO")
    POUT, _ = carve([128, 512], F32, "POUT")

    NAMES = ("PS XS AM SPm AT EL ELI KK RN SIG KK2 KR KHf BHf VBF TW GAM KTtm KHtm BHtm Vtm Qm AV KPf U "
             "HTMP WKI GST SHI CATT EXT WA WB PLD PTM GB PIN VC VN32 VNB UC GC T1 CST "
             "Y32 VP GA YA SQT BON HS32 HSB").split()

    def make_set(tg, W):
        k = lambda n: tg + n
        big = W == 128
        d = {}
        d["PS"], _ = carve([128, 25, W + 1], F32, k("PS"))
        d["XS"], _ = carve([128, 25, W], F32, k("XS"))
        d["AM"] = [carve([128, 4, 4, W], BF, k("AM%d" % g))[0] for g in range(4)]
        d["SPm"] = [carve([128, 4, 2, W], BF, k("SP%d" % g))[0] for g in range(4)]
        d["AT"], _ = carve([128, 8, W], F32, k("AT"))
        d["EL"], _ = carve([128, 8, W + 1], F32, k("EL"))
        d["ELI"], _ = carve([128, 8, W], F32, k("ELI"))
        d["KK"], _ = carve([128, 8, W], F32, k("KK"))
        d["RN"], _ = carve([128, 8, W], F32, k("RN"))
        d["SIG"], _ = carve([128, 1024], F32, k("SIG"))
        d["KK2"], _ = carve([128, 8, W], BF, k("KK2"))
        d["KR"], _ = carve([128, 8, 2, W], BF, k("KR"))
        d["KHf"], _ = carve([128, 8, W], BF, k("KHf"))
        d["BHf"], _ = carve([128, 8, W], BF, k("BHf"))
        d["VBF"], _ = carve([128, 8, W], BF, k("VBF"))
        d["TW"], _ = carve([128, W], BF, k("TW"))
        d["GAM"], _ = carve([128, 8], F32, k("GAM"))
        for n in ("KTtm", "KHtm", "BHtm", "Vtm", "AV", "U"):
            d[n], _ = carve([128, 1024], BF, k(n))
        d["Qm"] = [carve([128, 4, W], BF, k("Q%d" % g))[0] for g in range(4)]
        d["KPf"], _ = carve([128, 8, W], BF, k("KPf"))
        d["HTMP"], _ = carve([128, 8, 64], F32, k("HTMP"))
        d["WKI"] = carve([128, 16, 64], F32, k("WKI"))[0]
        d["GST"], _ = carve([128, 64], F32, k("GST"))
        d["SHI"] = carve([128, 128], F32, k("SHI"))[0]
        d["CATT"], _ = carve([128, 16, W], BF, k("CATT"))
        for n in ("EXT", "WA", "WB"):
            d[n], _ = carve([128, 4, 15 + W], F32, k(n))
        d["PLD"], _ = carve([128, 4, W], BF, k("PLD"))
        d["PTM"], _ = carve([128, 4, W], F32, k("PTM"))
        d["GB"], _ = carve([128, 4, W], F32, k("GB"))
        d["PIN"] = carve([128, 512], F32, k("PIN"))[0]
        d["VC"], _ = carve([128, 512], F32, k("VC"))
        d["VN32"], _ = carve([128, 512], F32, k("VN32"))
        d["VNB"], _ = carve([128, 512], BF, k("VNB"))
        for n in ("UC", "GC", "T1"):
            d[n], _ = carve([128, 4, W], F32, k(n))
        d["CST"], _ = carve([128, 16], F32, k("CST"))
        d["HS32"], _ = carve([128, 8, 64], F32, k("HS32"))
        d["HSB"], _ = carve([128, 8, 64], BF, k("HSB"))
        if big:
            d["Y32"] = V(d["AT"].ap.rearrange("p a b -> p (a b)"), k("AT"))
            d["VP"] = V(d["EL"].ap.rearrange("p a b -> p (a b)")[:, 0:1024], k("EL"))
            d["GA"] = d["ELI"]
            d["YA"] = d["KK"]
            d["SQT"] = V(d["RN"].ap.rearrange("p a b -> p (a b)"), k("RN"))
            d["BON"] = V(d["SIG"].ap.rearrange("p (a b) -> p a b", b=128), k("SIG"))
        else:
            d["Y32"], _ = carve([128, 1024], F32, k("Y32"))
            d["VP"], _ = carve([128, 1024], F32, k("VP"))
            d["GA"] = d["ELI"]
            d["YA"] = d["KK"]
            d["SQT"] = d["SIG"]
            d["BON"], _ = carve([128, 8, W], F32, k("BON"))
        return tuple(d[n] for n in NAMES)

    BP = make_set("p.", 128)
    print("arena words used", cur[0], "of", ARW)

    def xsrc(l, r0, R):
        if l == 0:
            if r0 < SEQ_P:
                return V(xp[r0:r0 + R, :], None)
            return V(xs[0:R, :], None)
        return V(xbuf[r0:r0 + R, :], ("xb", r0))

    def phase_norm(l, gsrc, final):
        P.barrier()
        P.dma(NGB, V(gsrc.broadcast_to([128, D]), None))
        for ti, (r0, R) in enumerate(rtiles):
            xt = XT[ti % 2]
            hb = HB[ti % 2]
            P.dma(xt[0:R], xsrc(l, r0, R), q=("sp" if ti % 2 == 0 else "pool"))
            ss = SSA[:, (ti % 2) * 2:(ti % 2) * 2 + 1]
            P.memset(ss[0:R], 0.0)
            P.act(SQJ[0:R], xt[0:R], AF.Square, accum=ss[0:R])
            rs = SSA[:, (ti % 2) * 2 + 1:(ti % 2) * 2 + 2]
            P.ts(rs[0:R], ss[0:R], 1.0 / D, EPS, AL.mult, AL.add)
            P.rsqrt(rs[0:R], rs[0:R])
            if final:
                P.stt(xt[0:R], xt[0:R], rs[0:R], NGB[0:R], AL.mult, AL.mult)
                if r0 < SEQ_P:
                    P.dma(V(yp[r0:r0 + R, :], None), xt[0:R], q="act")
                else:
                    P.dma(V(ys[0:R, :], None), xt[0:R], q="act")
                continue
            P.stt(hb[0:R], xt[0:R], rs[0:R], NGB[0:R], AL.mult, AL.mult)
            g = ti % 2
            for half in range(2):
                pb = psbf(g, half + 2 * ((ti // 2) % 2))
                for k in range(8):
                    kc = half * 8 + k
                    P.tr(pb[:, k * 128:k * 128 + R], hb[0:R, kc * 128:(kc + 1) * 128], IDB[0:R, 0:R])
                src = pb.re("p (a b) -> p a b", b=128)[:, :, 0:R]
                dst = HT[:, half * 8:half * 8 + 8, r0:r0 + R]
                if half == 0:
                    P.copy(dst, src, "act")
                else:
                    P.copy(dst, src, "dve")

    def phase_proj(l):
        P.barrier()
        wv = w_in[l].rearrange("(kc p) c -> p kc c", p=128)
        bank = 0
        sg = 0
        for ch in range(NCH):
            sc, sub = ch // 4, ch % 4
            if sub == 0:
                ncol = min(512, DIN - sc * 512)
                P.dma(WCH[sc % 3][:, :, 0:ncol], V(wv[:, :, sc * 512:sc * 512 + ncol], None), q="pool")
            wch = WCH[sc % 3][:, :, sub * 128:(sub + 1) * 128]
            if 45 <= ch < 49:
                for ti, (r0, R) in enumerate(rtiles):
                    ps = psum(bank // 4, bank % 4)
                    for kc in range(16):
                        P.mm(ps[0:R, 0:128], HT[:, kc, r0:r0 + R], wch[:, kc, :], kc == 0, kc == 15)
                    st = STG[sg % 4]
                    if sg % 2 == 0:
                        P.copy(st[0:R, 0:128], ps[0:R, 0:128], "act")
                    else:
                        P.copy(st[0:R, 0:128], ps[0:R, 0:128], "dve")
                    P.dma(V(vcTM[r0:r0 + R, (ch - 45) * 128:(ch - 44) * 128], ("vc", ti, ch)), st[0:R, 0:128])
                    bank = (bank + 1) % 8
                    sg += 1
                continue
            for gi, (t0, n) in enumerate(tgroups):
                ps = psum(bank // 4, bank % 4)
                for kc in range(16):
                    P.mm(ps[:, 0:n], wch[:, kc, :], HT[:, kc, t0:t0 + n], kc == 0, kc == 15)
                st = STG[sg % 4]
                if sg % 2 == 0:
                    P.copy(st[:, 0:n], ps[:, 0:n], "act")
                else:
                    P.copy(st[:, 0:n], ps[:, 0:n], "dve")
                P.dma(V(projT[ch * 128:(ch + 1) * 128, t0:t0 + n], ("pj", ch, gi)), st[:, 0:n])
                bank = (bank + 1) % 8
                sg += 1

    def pj(c0, c1, a, b):
        ks = set()
        for gi, (t0, n) in enumerate(tgroups):
            if a < t0 + n and b > t0:
                for ch in range(c0, c1):
                    ks.add(("pj", ch, gi))
        return projV[:, c0:c1, a:b], sorted(ks)

    def dma_pj(dst, c0, c1, a, b):
        ap, ks = pj(c0, c1, a, b)
        P.dma(dst, VK(ap, ks))

    def layer_consts(l):
        P.barrier()
        P.dma(PP, V(PPd[l], None))
        P.dma(W0R[0:1], V(w0d[l], None))
        P.dma(LW, V(LWd[l], None), q="pool")
        P.dma(PW, V(pwd[l].rearrange("g c d -> c g d"), None), q="pool")
        P.dma(LNG, V(lngd[l].broadcast_to([128, 512]), None))
        P.dma(BSB, V(gbd[l].broadcast_to([128, 512]).rearrange("p (a b) -> p a b", b=128), None))
        P.dma(WSTF, V(wsTd[l], None))
        P.tt(WST, WSTF, MASK4[:, 3:4, :].bc([128, 4, 128]), AL.mult)
        P.ts(OMK, PP[:, 33:41], -1.0, 1.0, AL.mult, AL.add)

    def tile_call(B, l, T, tok0, first, last, sb, rt, col0):
        (PS, XS, AM, SPm, AT, EL, ELI, KK, RN, SIG, KK2, KR, KHf, BHf, VBF, TW, GAM, KTtm, KHtm, BHtm, Vtm, Qm,
         AV, KPf, U, HTMP, WKI, GST, SHI, CATT, EXT, WA, WB, PLD, PTM, GB, PIN, VC, VN32, VNB, UC, GC, T1, CST,
         Y32, VP, GA, YA, SQT, BON, HS32, HSB) = B
        nlev = int(math.log2(T)) - 1
        bc8 = lambda c0: PP[:, c0:c0 + 8].us(2).bc([128, 8, T])
        Lx = 15 + T

        def pool_load():
            if first and sb is None:
                P.memset(EXT[:, :, 0:15], 0.0)
                dma_pj(EXT[:, :, 15:15 + T], 33, 37, tok0, tok0 + T)
            elif first:
                P.dma(PIN[0:15, :], V(st_pool[l, sb], None))
                dma_pj(EXT[:, :, 15:15 + T], 33, 37, tok0, tok0 + T)
            else:
                dma_pj(EXT[:, :, 0:15 + T], 33, 37, tok0 - 15, tok0 + T)
            dma_pj(GB[:, :, 0:T], 37, 41, tok0, tok0 + T)

        def cg_load():
            vks = [("vc", rt, ch) for ch in range(45, 49)]
            P.dma(VC[0:T, :], VK(vcTM[tok0:tok0 + T, :], vks))
            dma_pj(UC[:, :, 0:T], 41, 45, tok0, tok0 + T)
            dma_pj(GC[:, :, 0:T], 49, 53, tok0, tok0 + T)

        def pool_dve():
            if first and sb is not None:
                pp_ = psum3(0, 2, 1, 128)
                for g4 in range(4):
                    P.tr(pp_[:, g4, 0:15], PIN[0:15, g4 * 128:(g4 + 1) * 128], IDF[0:15, 0:15])
                P.copy(EXT[:, :, 0:15], pp_[:, :, 0:15], "dve")
            P.tt(WA[:, 0:4, 1:Lx], EXT[:, 0:4, 1:Lx], EXT[:, 0:4, 0:Lx - 1], AL.add)
            P.tt(WB[:, 1:4, 3:Lx], WA[:, 1:4, 3:Lx], WA[:, 1:4, 1:Lx - 2], AL.add)
            P.tt(WA[:, 2:4, 7:Lx], WB[:, 2:4, 7:Lx], WB[:, 2:4, 3:Lx - 4], AL.add)
            P.tt(WB[:, 3:4, 15:Lx], WA[:, 3:4, 15:Lx], WA[:, 3:4, 7:Lx - 8], AL.add)
            ic = IC0 if (first and sb is None) else IC1
            for g4 in range(4):
                wsrc = (WA, WB, WA, WB)[g4]
                P.tt(PTM[:, g4, 0:T], wsrc[:, g4, 15:Lx], ic[:, g4, 0:T], AL.mult)
            P.tt(PLD[:, :, 0:T], PTM[:, :, 0:T], EXT[:, :, 15:Lx], AL.subtract)

        def cg_dve1():
            cm = CST[0:T, 0:1]
            P.reduce(cm, VC[0:T, :])
            P.ts(cm, cm, -1.0 / 512, None, AL.mult)
            P.ts(VN32[0:T, :], VC[0:T, :], cm, None, AL.add)

        def cg_mid():
            cv = CST[0:T, 1:2]
            P.memset(cv, 0.0)
            P.act(VC[0:T, :], VN32[0:T, :], AF.Square, accum=cv)
            P.ts(cv, cv, 1.0 / 512, LN_EPS, AL.mult, AL.add)
            P.rsqrt(cv, cv)
            P.stt(VN32[0:T, :], VN32[0:T, :], cv, LNG[0:T, :], AL.mult, AL.mult)
            P.copy(VNB[0:T, :], VN32[0:T, :], "act")
            if sb is not None:
                P.dma(V(s_cv[l, sb], None), VN32[0:T, :], q="pool")
            P.act(GB[:, :, 0:T], GB[:, :, 0:T], AF.Silu)
            P.act(GC[:, :, 0:T], GC[:, :, 0:T], AF.Silu)

        def pool_tail():
            mx = psum3(0, 3, 1, 128)
            for g4 in range(4):
                P.mm(mx[:, g4, 0:T], PW[:, g4, :], PLD[:, g4, 0:T])
            P.tt(PTM[:, :, 0:T], mx[:, :, 0:T], PP[:, 73:77].us(2).bc([128, 4, T]), AL.mult)
            P.tt(CATT[:, 8:12, 0:T], PTM[:, :, 0:T], GB[:, :, 0:T], AL.mult)
            if last:
                po_ = psum(0, 2)
                for g4 in range(4):
                    P.tr(po_[0:15, g4 * 128:(g4 + 1) * 128], EXT[:, g4, T:T + 15], IDF)
                P.copy(POUT[0:15, :], po_[0:15, :], "act")
                dst = p_pool[l] if sb is None else s_pool[l, sb]
                P.dma(V(dst, None), POUT[0:15, :], q="pool")

        def cg_tail():
            mxc = psum3(1, 3, 1, 128)
            for g4 in range(4):
                P.mm(mxc[:, g4, 0:T], VNB[0:T, g4 * 128:(g4 + 1) * 128], WST[0:T, g4, 0:T])
            t1 = T1[:, :, 0:T]
            P.tt(t1, mxc[:, :, 0:T], BSB[:, :, 0:T], AL.add)
            P.tt(t1, t1, UC[:, :, 0:T], AL.mult)
            P.tt(CATT[:, 12:16, 0:T], t1, GC[:, :, 0:T], AL.mult)

        if first and sb is None:
            P.memset(PS[:, :, 0:1], 0.0)
            dma_pj(PS[:, :, 1:T + 1], 0, 25, tok0, tok0 + T)
        elif first:
            P.dma(SHI[0:25, :], V(st_shift[l, sb].rearrange("(ch p) -> ch p", p=128), None))
            pt = psum(1, 3)
            P.tr(pt[:, 0:25], SHI[0:25, :], IDF[0:25, 0:25])
            P.copy(PS[:, :, 0], pt[:, 0:25], "dve")
            dma_pj(PS[:, :, 1:T + 1], 0, 25, tok0, tok0 + T)
        else:
            dma_pj(PS[:, :, 0:T + 1], 0, 25, tok0 - 1, tok0 + T)
        if last:
            pt = psum(1, 3)
            P.tr(pt[0:25, 0:128], PS[:, :, T], IDF)
            P.copy(SHO[0:25, :], pt[0:25, 0:128], "act")
            dst = p_shift[l] if sb is None else s_shift[l, sb]
            P.dma(V(dst.rearrange("(ch p) -> ch p", p=128), None), SHO[0:25, :], q="pool")
        pool_load()
        cg_load()
        P.memset(EL[:, :, 0:1], 1.0)
        XSa = V(XS.ap[:, 0:16], XS.key + "a")
        XSb = V(XS.ap[:, 16:25], XS.key + "b")
        for (c0, c1, eng, xv) in ((0, 16, "dve", XSa), (16, 25, "dve", XSb)):
            xx = xv[:, :, 0:T]
            P.tt(xx, PS[:, c0:c1, 0:T], PS[:, c0:c1, 1:T + 1], AL.subtract, eng=eng)
            P.tt(xx, xx, PP[:, c0:c1].us(2).bc([128, c1 - c0, T]), AL.mult, eng=eng)
            P.tt(xx, xx, PS[:, c0:c1, 1:T + 1], AL.add, eng=eng)
        Xr = XSa[:, 0:8, 0:T]
        Xk = XSa[:, 8:16, 0:T]
        Xv = XSb[:, 0:8, 0:T]
        pool_dve()
        cg_dve1()
        yield
        P.act(TW[0:64, 0:T], XSb[0:64, 8, 0:T], AF.Tanh)
        P.act(TW[64:128, 0:T], XSb[64:128, 8, 0:T], AF.Copy)
        zt = psum(0, 0, 2)
        for h2 in range(2):
            P.mm(zt[0:T, h2 * 512:(h2 + 1) * 512], TW[0:64, 0:T], LW[0:64, h2 * 512:(h2 + 1) * 512], True, False)
            P.mm(zt[0:T, h2 * 512:(h2 + 1) * 512], ONESF[0:1, 0:T], W0R[0:1, h2 * 512:(h2 + 1) * 512], False, True)
        P.act(SIG[0:T, :], zt[0:T, :], AF.Sigmoid)
        lt = psum3(0, 2, 2, 128)
        for ch in range(8):
            P.mm(lt[:, ch, 0:T], SIG[0:T, ch * 128:(ch + 1) * 128], TRIS[0:T, 0:T])
        P.act(EL[:, :, 1:T + 1], lt[:, :, 0:T], AF.Exp)
        P.act(ELI[:, :, 0:T], lt[:, :, 0:T], AF.Exp, scale=-1.0)
        at = psum3(1, 0, 2, 128)
        for ch in range(8):
            P.mm(at[:, ch, 0:T], LW[64:128, ch * 128:(ch + 1) * 128], TW[64:128, 0:T])
        a_ = AT[:, :, 0:T]
        P.tt(a_, at[:, :, 0:T], bc8(65), AL.add)
        P.act(a_, a_, AF.Sigmoid)
        yield
        kk = KK[:, :, 0:T]
        P.tt(kk, Xk, bc8(25), AL.mult)
        P.act(KK2[:, :, 0:T], kk, AF.Square)
        hs = psum3(1, 2, 2, 128)
        for ch in range(8):
            P.mm(hs[:, ch, 0:T], BLK1, KK2[:, ch, 0:T])
        rn = RN[:, :, 0:T]
        P.ts(rn, hs[:, :, 0:T], 1e-12, None, AL.max)
        P.rsqrt(rn, rn)
        P.tt(kk, kk, rn, AL.mult)
        P.tt(rn, a_, bc8(33), AL.mult)
        P.tt(rn, rn, OMK.us(2).bc([128, 8, T]), AL.add)
        P.tt(rn, rn, Xk, AL.mult)
        P.tt(a_, kk, a_, AL.mult)
        P.tt(KR[:, :, 0, 0:T], kk, EL[:, :, 0:T], AL.mult)
        P.tt(KR[:, :, 1, 0:T], Xr, EL[:, :, 1:T + 1], AL.mult)
        P.tt(KHf[:, :, 0:T], rn, ELI[:, :, 0:T], AL.mult)
        P.tt(BHf[:, :, 0:T], a_, ELI[:, :, 0:T], AL.mult)
        P.copy(GAM, EL[:, :, T], "dve")
        bon = BON[:, :, 0:T]
        P.tt(bon, Xr, bc8(41), AL.mult)
        P.tt(KK2[:, :, 0:T], bon, rn, AL.mult)
        bs = psum3(0, 0, 2, 128)
        for ch in range(8):
            P.mm(bs[:, ch, 0:T], BLK1, KK2[:, ch, 0:T])
        P.tt(bon, bs[:, :, 0:T], Xv, AL.mult)
        P.copy(VBF[:, :, 0:T], Xv, "act")
        cg_mid()
        yield
        for qi, (src, dst) in enumerate(((KR[:, :, 0, :], KTtm), (KHf, KHtm), (BHf, BHtm), (VBF, Vtm))):
            pb = psbf(1, qi)
            for ch in range(8):
                P.tr(pb[0:T, ch * 128:(ch + 1) * 128], src[:, ch, 0:T], IDB)
            if qi == 2:
                P.act(dst[0:T, :], pb[0:T, :], AF.Copy, scale=-1.0)
            else:
                P.copy(dst[0:T, :], pb[0:T, :], "act" if qi % 2 == 0 else "dve")
        pool_tail()
        cg_tail()
        yield "neu"
        m4 = MASK4[0:T, :, 0:T].us(1).bc([T, 4, 4, T])

        def pga(g):
            return V(PG[g % 2][:, :, :].rearrange("p h (a b) -> p h a b", b=128), "PSUM")

        def mm2(pg, hh, s0, lhsT, rhs3):
            if T == 128:
                P.mm(pg[0:T, hh, s0:s0 + 2, :].re("p a b -> p (a b)"), lhsT, rhs3.re("p a b -> p (a b)"))
            else:
                P.mm(pg[0:T, hh, s0, 0:T], lhsT, rhs3[:, 0, 0:T])
                P.mm(pg[0:T, hh, s0 + 1, 0:T], lhsT, rhs3[:, 1, 0:T])

        def a_mats(g):
            pg = pga(g)
            for hh in range(4):
                h = 4 * g + hh
                ch, po = h // 2, (h % 2) * 64
                mm2(pg, hh, 0, BHf[po:po + 64, ch, 0:T], KR[po:po + 64, ch])
                mm2(pg, hh, 2, KHf[po:po + 64, ch, 0:T], KR[po:po + 64, ch])
            P.tt(AM[g][0:T, :, :, 0:T], pg[0:T, :, :, 0:T], m4, AL.mult)
            for hh in range(4):
                h = 4 * g + hh
                ch, po = h // 2, (h % 2) * 64
                P.mm(pg[0:T, hh, 2, 0:T], KR[po:po + 64, ch, 0, 0:T], BHf[po:po + 64, ch, 0:T])
            P.tt(Qm[g][0:T, :, 0:T], pg[0:T, :, 2, 0:T], NSL[0:T, 0:T].us(1).bc([T, 4, T]), AL.mult)
            P.tt(SPm[g][0:T, :, 0, 0:T], AM[g][0:T, :, 0, 0:T], IDB[0:T, 0:T].us(1).bc([T, 4, T]), AL.add)

        def neu_pre_mm(g):
            pg = pga(g)
            for hh in range(4):
                P.mm(pg[0:T, hh, 1, 0:T], Qm[g][0:T, hh, 0:T], AM[g][0:T, hh, 0, 0:T])
                P.mm(pg[0:T, hh, 2, 0:T], AM[g][0:T, hh, 0, 0:T], Qm[g][0:T, hh, 0:T])

        def neu_pq_ev(g):
            pg = pga(g)
            P.copy(SPm[g][0:T, :, 1, 0:T], pg[0:T, :, 1, 0:T], "act")
            P.copy(Qm[g][0:T, :, 0:T], pg[0:T, :, 2, 0:T], "act")

        def neu_mm(g, lastlev):
            pg = pga(g)
            for hh in range(4):
                if lastlev:
                    P.mm(pg[0:T, hh, 0, 0:T], Qm[g][0:T, hh, 0:T], SPm[g][0:T, hh, 0, 0:T])
                else:
                    mm2(pg, hh, 0, Qm[g][0:T, hh, 0:T], SPm[g][0:T, hh])
                    P.mm(pg[0:T, hh, 2, 0:T], SPm[g][0:T, hh, 1, 0:T], Qm[g][0:T, hh, 0:T])

        def neu_ev(g, lastlev):
            pg = pga(g)
            P.tt(SPm[g][0:T, :, 0, 0:T], SPm[g][0:T, :, 0, 0:T], pg[0:T, :, 0, 0:T], AL.add)
            if not lastlev:
                neu_pq_ev(g)

        for pair in range(2):
            ga, gb = 2 * pair, 2 * pair + 1
            a_mats(ga)
            a_mats(gb)
            yield "neu"
            neu_pre_mm(ga)
            neu_pre_mm(gb)
            neu_pq_ev(ga)
            neu_pq_ev(gb)
            for lev in range(nlev):
                yield "neu"
                lastlev = lev == nlev - 1
                neu_mm(ga, lastlev)
                neu_mm(gb, lastlev)
                neu_ev(ga, lastlev)
                neu_ev(gb, lastlev)
            yield "neu" if pair == 0 else None
        av = psum(0, 0, 2)
        for h in range(16):
            P.mm(av[0:T, h * 64:(h + 1) * 64], AM[h // 4][0:T, h % 4, 2, 0:T], Vtm[0:T, h * 64:(h + 1) * 64])
        P.copy(AV[0:T, :], av[0:T, :], "act")
        yield
        kp = V(PG[1][:, :, :].rearrange("p a (b c) -> p (a b) c", c=128), "PSUM")
        for h in range(16):
            ch = h // 2
            P.mm(kp[:, h, 0:T], KTtm[0:T, ch * 128:(ch + 1) * 128], SPm[h // 4][0:T, h % 4, 0, 0:T])
        kp4 = V(PG[1][:, :, :].rearrange("p a (b two c) -> p (a b) two c", two=2, c=128), "PSUM")
        P.copy(KPf[0:64, :, 0:T], kp4[0:64, :, 0, 0:T], "act")
        P.copy(KPf[64:128, :, 0:T], kp4[64:128, :, 1, 0:T], "dve")
        yield
        vp = psum(0, 2, 2)
        for h in range(16):
            P.mm(vp[0:T, h * 64:(h + 1) * 64], SPm[h // 4][0:T, h % 4, 0, 0:T], AV[0:T, h * 64:(h + 1) * 64])
        P.copy(VP[0:T, :], vp[0:T, :], "act")
        yield
        if first:
            if sb is None:
                P.memset(HS32, 0.0)
                P.memset(HSB, 0.0)
            else:
                P.dma(WKI[0:64], V(st_wkv[l, sb].rearrange("h i j -> i h j"), None))
                hp = psum3(1, 3, 1, 64)
                for ch in range(8):
                    P.tr(hp[:, ch, :], WKI[0:64, 2 * ch:2 * ch + 2, :].re("p a b -> p (a b)"), IDF[0:64, 0:64])
                P.copy(HS32, hp, "act")
                P.copy(HSB, hp, "dve")
        um = psum(0, 0, 2)
        hpar = [2 * c for c in range(8)] + [2 * c + 1 for c in range(8)]
        for h in hpar:
            ch, po = h // 2, (h % 2) * 64
            c0 = (h % 2) * 512 + ch * 64
            P.mm(um[0:T, c0:c0 + 64], KPf[po:po + 64, ch, 0:T], HSB[po:po + 64, ch, :])
        hm = lambda v: v[0:T, :].re("p (c two i) -> p c two i", two=2, i=64)
        pm = lambda v: v[0:T, :].re("p (two c i) -> p c two i", two=2, i=64)
        P.tt(hm(U), hm(VP), pm(um), AL.add)
        yield
        yps = psum(0, 2, 2)
        for h in hpar:
            ch, po = h // 2, (h % 2) * 64
            c0 = (h % 2) * 512 + ch * 64
            P.mm(yps[0:T, c0:c0 + 64], KR[po:po + 64, ch, 1, 0:T], HSB[po:po + 64, ch, :])
        yps2 = psum(1, 2, 2)
        for h in range(16):
            o = yps2[0:T, h * 64:(h + 1) * 64]
            P.mm(o, AM[h // 4][0:T, h % 4, 1, 0:T], U[0:T, h * 64:(h + 1) * 64], True, False)
            P.mm(o, AM[h // 4][0:T, h % 4, 3, 0:T], Vtm[0:T, h * 64:(h + 1) * 64], False, True)
        P.copy(Y32[0:T, :], yps2[0:T, :], "act")
        P.tt(hm(Y32), hm(Y32), pm(yps), AL.add)
        yield
        hn = psum3(1, 0, 2, 64)
        for h in range(16):
            ch = h // 2
            P.mm(hn[:, h, :], BHtm[0:T, ch * 128:(ch + 1) * 128], U[0:T, h * 64:(h + 1) * 64], True, False)
            P.mm(hn[:, h, :], KHtm[0:T, ch * 128:(ch + 1) * 128], Vtm[0:T, h * 64:(h + 1) * 64], False, True)
        hn4 = V(PG[1][:, 0:2, :].rearrange("p a (b two c) -> p (a b) two c", two=2, c=64), "PSUM")
        P.tt(HTMP[0:64], HS32[0:64], hn4[0:64, :, 0, :], AL.add)
        P.tt(HTMP[64:128], HS32[64:128], hn4[64:128, :, 1, :], AL.add)
        P.tt(HS32, HTMP, GAM.us(2).bc([128, 8, 64]), AL.mult)
        P.copy(HSB, HS32, "act")
        yield
        if last:
            wk = psum3(1, 2, 2, 128)
            for ch in range(8):
                P.tr(wk[0:64, ch, :], HS32[:, ch, :], IDF)
            P.copy(WKO[0:64], wk[0:64], "act")
            dst = p_wkv[l] if sb is None else s_wkv[l, sb]
            P.dma(V(dst.rearrange("(c two) i j -> i c two j", two=2), None),
                  WKO[0:64].re("p c (two j) -> p c two j", two=2), q="pool")
        yield
        y3 = Y32[0:T, :].re("p (h i) -> p h i", i=64)
        mean = GST[0:T, 0:16]
        P.reduce(mean, y3)
        P.ts(mean, mean, -1.0 / 64, None, AL.mult)
        P.tt(y3, y3, mean.us(2).bc([T, 16, 64]), AL.add)
        P.act(SQT[0:T, :], Y32[0:T, :], AF.Square)
        var = GST[0:T, 16:32]
        P.reduce(var, SQT[0:T, :].re("p (h i) -> p h i", i=64))
        P.ts(var, var, 1.0 / 64, GN_EPS, AL.mult, AL.add)
        P.rsqrt(var, var)
        P.tt(y3, y3, var.us(2).bc([T, 16, 64]), AL.mult)
        ynT = psum3(0, 0, 2, 128)
        for ch in range(8):
            P.tr(ynT[:, ch, 0:T], Y32[0:T, ch * 128:(ch + 1) * 128], IDF[0:T, 0:T])
        ya = YA[:, :, 0:T]
        P.tt(ya, ynT[:, :, 0:T], bc8(49), AL.mult)
        P.tt(ya, ya, bc8(57), AL.add)
        P.tt(ya, ya, bon, AL.add)
        ga_ = GA[:, :, 0:T]
        dma_pj(ga_, 25, 33, tok0, tok0 + T)
        P.act(ga_, ga_, AF.Silu)
        P.tt(CATT[:, 0:8, 0:T], ya, ga_, AL.mult)
        yield
        P.dma(V(catT[rt, :, :, col0:col0 + T], ("cat", rt)), CATT[:, :, 0:T], q="pool")

    def phase_mix(l):
        layer_consts(l)

        def chain(gens):
            for g in gens:
                yield from g

        for b in range(NSQ):
            for _ in tile_call(BP, l, 8, SEQ_P + 8 * b, True, True, b, NRT - 1, 8 * b):
                pass
        for i in range(NPT):
            for _ in tile_call(BP, l, 128, i * 128, i == 0, i == NPT - 1, None, i, 0):
                pass

    def phase_out(l):
        P.barrier()
        wv = w_out[l].rearrange("(kc p) n -> p kc n", p=128)
        for n4 in range(4):
            P.dma(WO[n4], V(wv[:, :, n4 * 512:(n4 + 1) * 512], None), q="pool")
        for ti, (r0, R) in enumerate(rtiles):
            ct = CT[ti % 2]
            xo = XO[ti % 2]
            P.dma(ct[:, :, 0:R], V(catT[ti, :, :, 0:R], ("cat", ti)), q="pool")
            P.dma(xo[0:R], xsrc(l, r0, R))
            for n4 in range(4):
                ps = psum(ti % 2, n4)
                for kc in range(16):
                    P.mm(ps[0:R, :], ct[:, kc, 0:R], WO[n4][:, kc, :], kc == 0, kc == 15)
                P.tt(xo[0:R, n4 * 512:(n4 + 1) * 512], xo[0:R, n4 * 512:(n4 + 1) * 512], ps[0:R, :], AL.add)
            P.dma(V(xbuf[r0:r0 + R, :], ("xb", r0)), xo[0:R], q="act")

    for l in range(NL):
        phase_norm(l, norm_g[l:l + 1, :], False)
        phase_proj(l)
        phase_mix(l)
        phase_out(l)
    phase_norm(NL, fnorm_g, True)
    P.barrier()

    with nc.Block() as block:
        def emit(name, e):
            for waits, fn, inc in P.ops[name]:
                for sem, val in waits:
                    e.wait_ge(sems[sem], val)
                if fn is not None:
                    fn(e).then_inc(sems[inc[0]], inc[1])

        @block.tensor
        def _(e):
            emit("pe", e)

        @block.scalar
        def _(e):
            emit("act", e)

        @block.vector
        def _(e):
            emit("dve", e)

        @block.gpsimd
        def _(e):
            emit("pool", e)

        @block.sync
        def _(e):
            emit("sp", e)
    es.close()
    stats = {k: len(v) for k, v in P.ops.items()}
    return nc, stats


def _consts():
    s = np.arange(128)[:, None]
    t = np.arange(128)[None, :]
    su = (s < t).astype(np.float32)
    ui = (s <= t).astype(np.float32)
    sl = (s > t).astype(np.float32)
    c = {}
    c["cmask4"] = np.concatenate([-su, -ui, su, ui], axis=1).astype(np.float32)
    c["cnsl"] = -sl
    c["ctris"] = (-math.exp(-0.5) * ui).astype(np.float32)
    c["cidf"] = np.eye(128, dtype=np.float32)
    blk = np.zeros((128, 128), np.float32)
    blk[:64, :64] = 1.0
    blk[64:, 64:] = 1.0
    c["cblk"] = blk
    wins = (2, 4, 8, 16)
    ic0 = np.zeros((128, 4, 128), np.float32)
    ic1 = np.zeros((128, 4, 128), np.float32)
    pos = np.arange(128)
    for g, w in enumerate(wins):
        ic0[:, g, :] = (1.0 / np.minimum(pos + 1, w))[None, :]
        ic1[:, g, :] = 1.0 / w
    c["cic0"] = ic0.reshape(128, 512)
    c["cic1"] = ic1.reshape(128, 512)
    return c


def _chunked(vec, n):
    NL = vec.shape[0]
    return np.ascontiguousarray(vec.reshape(NL, n, 128).transpose(0, 2, 1))


def make_in_maps(inp, NL, SEQ_P, NSQ, ncores, nprompt):
    f = lambda a: np.ascontiguousarray(np.asarray(a, dtype=np.float32))
    pp = np.concatenate([
        _chunked(f(inp["shift_mu"])[:NL], 25), _chunked(f(inp["k_k"])[:NL], 8), _chunked(f(inp["k_a"])[:NL], 8),
        _chunked(f(inp["r_k"])[:NL].reshape(NL, 1024), 8), _chunked(f(inp["lnx_g"])[:NL], 8),
        _chunked(f(inp["lnx_b"])[:NL], 8), _chunked(f(inp["a0"])[:NL], 8), _chunked(f(inp["pool_scale"])[:NL], 4)],
        axis=2)
    assert pp.shape[2] == NPP
    shared = {
        "norm_g": f(inp["norm_g"])[:NL], "fnorm_g": f(inp["final_norm_g"]).reshape(1, D),
        "w_in": f(inp["w_in"])[:NL], "w_out": f(inp["w_out"])[:NL], "pp": np.ascontiguousarray(pp),
        "w0": f(inp["w0"])[:NL].reshape(NL, 1, 1024),
        "lw": np.ascontiguousarray(np.concatenate([f(inp["w_up"])[:NL], f(inp["a_up"])[:NL]], axis=1)),
        "pool_w": f(inp["pool_w"])[:NL], "lng": f(inp["gmlp_ln_g"])[:NL].reshape(NL, 1, 512),
        "wsT": np.ascontiguousarray(f(inp["gmlp_ws"])[:NL].transpose(0, 3, 1, 2)),
        "gb": f(inp["gmlp_b"])[:NL].reshape(NL, 1, 512),
    }
    shared.update(_consts())
    xpr = f(inp["x_prompt"])
    xsa = f(inp["x_sample"])
    sts, stw, stp = f(inp["state_shift"]), f(inp["state_wkv"]), f(inp["state_pool"])
    maps = []
    for c in range(ncores):
        b = c % nprompt
        m = dict(shared)
        m["xp"] = np.ascontiguousarray(xpr[b, :SEQ_P])
        sl = slice(c * NSQ, (c + 1) * NSQ)
        m["xs"] = np.ascontiguousarray(xsa[sl].reshape(NSQ * 8, D))
        m["st_shift"] = np.ascontiguousarray(sts[:NL, sl])
        m["st_wkv"] = np.ascontiguousarray(stw[:NL, sl])
        m["st_pool"] = np.ascontiguousarray(stp[:NL, sl])
        maps.append(m)
    return maps


_CACHE = {}


def run(inp, NL, SEQ_P, NSQ, ncores, nprompt):
    key = (NL, SEQ_P, NSQ)
    if key not in _CACHE:
        _CACHE[key] = build(NL, SEQ_P, NSQ)[0]
    nc = _CACHE[key]
    maps = make_in_maps(inp, NL, SEQ_P, NSQ, ncores, nprompt)
    res = run_bass_kernel_spmd(nc, maps, core_ids=list(range(ncores)))
    R = res.results
    g = lambda name, cores: np.stack([np.asarray(R[c][name], dtype=np.float32) for c in cores])
    pc = list(range(nprompt))
    ac = list(range(ncores))
    y_prompt = g("yp", pc)
    y_sample = g("ys", ac).reshape(ncores * NSQ, 8, D)
    p_shift = g("p_shift", pc).transpose(1, 0, 2)
    p_wkv = g("p_wkv", pc).transpose(1, 0, 2, 3, 4)
    p_pool = g("p_pool", pc).transpose(1, 0, 2, 3)
    cat = lambda name: np.concatenate([np.asarray(R[c][name], dtype=np.float32) for c in ac], axis=1)
    return (y_prompt, y_sample, np.ascontiguousarray(p_shift), np.ascontiguousarray(p_wkv),
            np.ascontiguousarray(p_pool), cat("s_shift"), cat("s_wkv"), cat("s_pool"), cat("s_cv"))


def kernel(**inputs):
    return run(inputs, 4, 2048, 16, 8, 4)
```

```python
import contextlib
import math
import numpy as np
import concourse.bass as bass
import concourse.mybir as mybir
from concourse.bass_utils import run_bass_kernel_spmd

F32 = mybir.dt.float32
BF = mybir.dt.bfloat16
AL = mybir.AluOpType
AF = mybir.ActivationFunctionType
AX = mybir.AxisListType

D = 2048
DIN = 6784
NCH = 53
EPS = 1e-6
GN_EPS = 64 * 1e-5
LN_EPS = 1e-5
NPP = 77
ENG = ("pe", "act", "dve", "pool", "sp")
KD = 16


class V:
    __slots__ = ("ap", "key")

    def __init__(s, ap, key):
        s.ap = ap
        s.key = key

    def __getitem__(s, i):
        return V(s.ap[i], s.key)

    def bc(s, shape):
        return V(s.ap.broadcast_to(list(shape)), s.key)

    def re(s, pat, **kw):
        return V(s.ap.rearrange(pat, **kw), s.key)

    def us(s, ax):
        return V(s.ap.unsqueeze(ax), s.key)

    def cast(s, dt):
        return V(s.ap.bitcast(dt), s.key)


class VK(V):
    __slots__ = ("keys",)

    def __init__(s, ap, keys):
        V.__init__(s, ap, "MULTI")
        s.keys = keys


def keys_of(v):
    if v is None or not isinstance(v, V) or v.key is None:
        return []
    if isinstance(v, VK):
        return list(v.keys)
    if v.key == "PSUM":
        ap = v.ap
        es = 2 if ap.dtype == BF else 4
        pstep = ap.ap[0][0]
        off = ap.offset % pstep
        dims = ap.ap[1:]
        starts = [off]
        for (st, cnt) in dims[:-1]:
            starts = [s0 + st * i for s0 in starts for i in range(cnt)]
        lst, lcnt = dims[-1]
        banks = set()
        for s0 in starts:
            banks.add((s0 * es) // 2048)
            banks.add(((s0 + lst * (lcnt - 1)) * es) // 2048)
        return [(ap.name, b) for b in banks]
    return [v.key]


class Prog:
    def __init__(s):
        s.ops = {e: [] for e in ENG}
        s.cnt = {e: 0 for e in ENG}
        s.lastw = {}
        s.readers = {}
        s.seen = {e: {} for e in ENG}
        s.dq = {"sp": 0, "pool": 0, "act": 0}

    def _deps(s, eng, rk, wk):
        need = {}

        def add(ev):
            sem, val, src = ev
            if eng == "pe" and src == "pe":
                return
            if need.get(sem, 0) < val:
                need[sem] = val

        for k in rk:
            if k in s.lastw:
                add(s.lastw[k])
        for k in wk:
            if k in s.lastw:
                add(s.lastw[k])
            for sem, (val, src) in s.readers.get(k, {}).items():
                add((sem, val, src))
        waits = []
        for sem, val in need.items():
            if s.seen[eng].get(sem, 0) >= val:
                continue
            s.seen[eng][sem] = val
            waits.append((sem, val))
        return waits

    def _commit(s, ev, rk, wk):
        sem, val, src = ev
        for k in rk:
            s.readers.setdefault(k, {})[sem] = (val, src)
        for k in wk:
            s.lastw[k] = ev
            s.readers[k] = {}

    def op(s, eng, fn, reads, writes):
        rk = [k for v in reads for k in keys_of(v)]
        wk = [k for v in writes for k in keys_of(v)]
        wk += [k for k in rk if isinstance(k, tuple) and isinstance(k[0], str) and k[0].startswith("pg")]
        waits = s._deps(eng, rk, wk)
        s.cnt[eng] += 1
        ev = ("E_" + eng, s.cnt[eng], eng)
        s.ops[eng].append((waits, fn, ("E_" + eng, 1)))
        s._commit(ev, rk, wk)

    def dma(s, out, in_, q="sp"):
        rk = keys_of(in_)
        wk = keys_of(out)
        i = s.dq[q]
        s.dq[q] += 1
        slot = i % KD
        val = 16 * (i // KD + 1)
        sem = "D_%s_%d" % (q, slot)
        waits = s._deps(q, rk, wk)
        if i >= KD and s.seen[q].get(sem, 0) < val - 16:
            s.seen[q][sem] = val - 16
            waits.append((sem, val - 16))
        o, a = out.ap, in_.ap
        s.ops[q].append((waits, lambda e: e.dma_start(out=o, in_=a), (sem, 16)))
        s._commit((sem, val, "dma"), rk, wk)

    def barrier(s):
        for e in ENG:
            waits = []
            for e2 in ENG:
                c = s.cnt[e2]
                sem = "E_" + e2
                if c > 0 and s.seen[e].get(sem, 0) < c:
                    s.seen[e][sem] = c
                    waits.append((sem, c))
            for q in ("sp", "pool", "act"):
                n = s.dq[q]
                for slot in range(KD):
                    u = (n - slot + KD - 1) // KD if n > slot else 0
                    sem = "D_%s_%d" % (q, slot)
                    if u > 0 and s.seen[e].get(sem, 0) < 16 * u:
                        s.seen[e][sem] = 16 * u
                        waits.append((sem, 16 * u))
            if waits:
                s.ops[e].append((waits, None, None))

    def mm(s, out, lhsT, rhs, start=True, stop=True):
        o, l, r = out.ap, lhsT.ap, rhs.ap
        s.op("pe", lambda e: e.matmul(o, l, r, start=start, stop=stop), [lhsT, rhs], [out])

    def tr(s, out, in_, ident):
        o, i, d = out.ap, in_.ap, ident.ap
        s.op("pe", lambda e: e.transpose(o, i, d), [in_, ident], [out])

    def act(s, out, in_, func, bias=None, scale=None, accum=None):
        o, i = out.ap, in_.ap
        kw = {}
        rd = [in_]
        wr = [out]
        if bias is not None:
            kw["bias"] = bias.ap
            rd.append(bias)
        if scale is not None:
            kw["scale"] = scale
        if accum is not None:
            kw["accum_out"] = accum.ap
            wr.append(accum)
        s.op("act", lambda e: e.activation(out=o, in_=i, func=func, **kw), rd, wr)

    def tt(s, out, in0, in1, op, eng="dve"):
        o, a, b = out.ap, in0.ap, in1.ap
        s.op(eng, lambda e: e.tensor_tensor(out=o, in0=a, in1=b, op=op), [in0, in1], [out])

    def ts(s, out, in0, s1, s2, op0, op1=None, eng="dve"):
        o, a = out.ap, in0.ap
        rd = [in0]
        x1 = s1
        x2 = s2
        if isinstance(s1, V):
            rd.append(s1)
            x1 = s1.ap
        if isinstance(s2, V):
            rd.append(s2)
            x2 = s2.ap
        if op1 is None:
            s.op(eng, lambda e: e.tensor_scalar(out=o, in0=a, scalar1=x1, scalar2=None, op0=op0), rd, [out])
        else:
            s.op(eng, lambda e: e.tensor_scalar(out=o, in0=a, scalar1=x1, scalar2=x2, op0=op0, op1=op1), rd, [out])

    def stt(s, out, in0, scalar, in1, op0, op1):
        o, a, b = out.ap, in0.ap, in1.ap
        rd = [in0, in1]
        sc = scalar
        if isinstance(scalar, V):
            rd.append(scalar)
            sc = scalar.ap
        s.op("dve", lambda e: e.scalar_tensor_tensor(out=o, in0=a, scalar=sc, in1=b, op0=op0, op1=op1), rd, [out])

    def copy(s, out, in_, eng="act"):
        o, i = out.ap, in_.ap
        if eng == "act":
            s.op("act", lambda e: e.activation(out=o, in_=i, func=AF.Copy), [in_], [out])
        else:
            s.op(eng, lambda e: e.tensor_copy(out=o, in_=i), [in_], [out])

    def rsqrt(s, out, in_):
        o, i = out.ap, in_.ap
        s.op("act", lambda e: e.activation(out=o, in_=i, func=AF.Sqrt), [in_], [out])
        s.op("dve", lambda e: e.reciprocal(out=o, in_=o), [out], [out])

    def memset(s, out, val, eng="dve"):
        o = out.ap
        s.op(eng, lambda e: e.memset(o, val), [], [out])

    def reduce(s, out, in_, op=AL.add):
        o, i = out.ap, in_.ap
        s.op("dve", lambda e: e.tensor_reduce(out=o, in_=i, axis=AX.X, op=op), [in_], [out])


def build(NL, SEQ_P, NSQ):
    NPT = SEQ_P // 128
    SR = NSQ * 8
    NTOK = SEQ_P + SR
    rtiles = [(i * 128, 128) for i in range(NPT)] + [(SEQ_P, SR)]
    NRT = len(rtiles)
    tgroups = []
    t0 = 0
    while t0 < NTOK:
        n = min(512, NTOK - t0)
        tgroups.append((t0, n))
        t0 += n

    nc = bass.Bass("TRN2", target_bir_lowering=False)

    def dram(name, shape, dt=F32, kind="ExternalInput"):
        return nc.dram_tensor(name, list(shape), dt, kind=kind).ap()

    xp = dram("xp", [SEQ_P, D])
    xs = dram("xs", [SR, D])
    st_shift = dram("st_shift", [NL, NSQ, 3200])
    st_wkv = dram("st_wkv", [NL, NSQ, 16, 64, 64])
    st_pool = dram("st_pool", [NL, NSQ, 15, 512])
    norm_g = dram("norm_g", [NL, D])
    fnorm_g = dram("fnorm_g", [1, D])
    w_in = dram("w_in", [NL, D, DIN])
    w_out = dram("w_out", [NL, D, D])
    PPd = dram("pp", [NL, 128, NPP])
    w0d = dram("w0", [NL, 1, 1024])
    LWd = dram("lw", [NL, 128, 1024])
    pwd = dram("pool_w", [NL, 4, 128, 128])
    lngd = dram("lng", [NL, 1, 512])
    wsTd = dram("wsT", [NL, 128, 4, 128])
    gbd = dram("gb", [NL, 1, 512])
    cM4 = dram("cmask4", [128, 512])
    cNSL = dram("cnsl", [128, 128])
    cTRIS = dram("ctris", [128, 128])
    cIDF = dram("cidf", [128, 128])
    cBLK = dram("cblk", [128, 128])
    cIC0 = dram("cic0", [128, 512])
    cIC1 = dram("cic1", [128, 512])

    yp = dram("yp", [SEQ_P, D], kind="ExternalOutput")
    ys = dram("ys", [SR, D], kind="ExternalOutput")
    p_shift = dram("p_shift", [NL, 3200], kind="ExternalOutput")
    p_wkv = dram("p_wkv", [NL, 16, 64, 64], kind="ExternalOutput")
    p_pool = dram("p_pool", [NL, 15, 512], kind="ExternalOutput")
    s_shift = dram("s_shift", [NL, NSQ, 3200], kind="ExternalOutput")
    s_wkv = dram("s_wkv", [NL, NSQ, 16, 64, 64], kind="ExternalOutput")
    s_pool = dram("s_pool", [NL, NSQ, 15, 512], kind="ExternalOutput")
    s_cv = dram("s_cv", [NL, NSQ, 8, 512], kind="ExternalOutput")

    projT = dram("projT", [NCH * 128, NTOK], kind="Internal")
    vcTM = dram("vcTM", [NTOK, 512], kind="Internal")
    xbuf = dram("xbuf", [NTOK, D], kind="Internal")
    catT = dram("catT", [NRT, 128, 16, 128], BF, kind="Internal")

    projV = projT.rearrange("(ch p) t -> p ch t", p=128)

    P = Prog()
    es = contextlib.ExitStack()
    ARW = 47104
    AR = es.enter_context(nc.sbuf_tensor("arena", [128, ARW], F32))
    PG = [es.enter_context(nc.psum_tensor("pg%d" % i, [128, 4, 512], F32)) for i in range(2)]
    sems = {}
    for e in ENG:
        sems["E_" + e] = es.enter_context(nc.semaphore("E_" + e))
    for q in ("sp", "pool", "act"):
        for i in range(KD):
            n = "D_%s_%d" % (q, i)
            sems[n] = es.enter_context(nc.semaphore(n))

    cur = [0]

    def carve(shape, dt, key, at=None):
        esz = 2 if dt == BF else 4
        n = 1
        for x in shape[1:]:
            n *= x
        words = (n * esz + 3) // 4
        words = (words + 7) // 8 * 8
        if at is None:
            at = cur[0]
            cur[0] += words
        assert at + words <= ARW, ("arena overflow", key, at, words)
        ap = AR[0:shape[0], at:at + words]
        if dt == BF:
            ap = ap.bitcast(BF)
        ap = ap[:, 0:n]
        if len(shape) == 3:
            ap = ap.rearrange("p (a b) -> p a b", b=shape[2])
        elif len(shape) == 4:
            ap = ap.rearrange("p (a b c) -> p a b c", b=shape[2], c=shape[3])
        v = V(ap, key)
        return v, at

    def psum(g, b0, nb=1):
        return V(PG[g][:, b0:b0 + nb, :].rearrange("p a b -> p (a b)"), "PSUM")

    def psum3(g, b0, nb, inner):
        return V(PG[g][:, b0:b0 + nb, :].rearrange("p a (b c) -> p (a b) c", c=inner), "PSUM")

    def psbf(g, b):
        return V(PG[g][:, b, :].bitcast(BF), "PSUM")

    MASK4, _ = carve([128, 4, 128], F32, "MASK4")
    NSL, _ = carve([128, 128], F32, "NSL")
    TRIS, _ = carve([128, 128], F32, "TRIS")
    IDF, _ = carve([128, 128], F32, "IDF")
    IDB, _ = carve([128, 128], BF, "IDB")
    BLK1, _ = carve([128, 128], BF, "BLK1")
    IC0, _ = carve([128, 4, 128], F32, "IC0")
    IC1, _ = carve([128, 4, 128], F32, "IC1")
    ONESF, _ = carve([128, 128], F32, "ONESF")
    PP, _ = carve([128, NPP], F32, "PP")
    OMK, _ = carve([128, 8], F32, "OMK")
    W0R, _ = carve([128, 1024], F32, "W0R")
    LW, _ = carve([128, 1024], BF, "LW")
    PW, _ = carve([128, 4, 128], BF, "PW")
    LNG, _ = carve([128, 512], F32, "LNG")
    BSB, _ = carve([128, 4, 128], F32, "BSB")
    WSTF, _ = carve([128, 4, 128], F32, "WSTF")
    WST, _ = carve([128, 4, 128], BF, "WST")
    base = cur[0]

    P.dma(MASK4, V(cM4.rearrange("p (a b) -> p a b", b=128), None))
    P.dma(NSL, V(cNSL, None))
    P.dma(TRIS, V(cTRIS, None))
    P.dma(IDF, V(cIDF, None))
    P.dma(IDB, V(cIDF, None), q="pool")
    P.dma(BLK1, V(cBLK, None), q="pool")
    P.dma(IC0, V(cIC0.rearrange("p (a b) -> p a b", b=128), None))
    P.dma(IC1, V(cIC1.rearrange("p (a b) -> p a b", b=128), None))
    P.memset(ONESF, 1.0)

    cur[0] = base
    HT, _ = carve([128, 16, NTOK], BF, "HT")
    abase = cur[0]
    XT = [carve([128, D], F32, "XT%d" % i)[0] for i in range(2)]
    HB = [carve([128, D], BF, "HB%d" % i)[0] for i in range(2)]
    NGB, _ = carve([128, D], F32, "NGB")
    SQJ, _ = carve([128, D], BF, "SQJ")
    SSA, _ = carve([128, 8], F32, "SSA")
    cur[0] = abase
    WCH = [carve([128, 16, 512], BF, "WCH%d" % i)[0] for i in range(3)]
    STG = [carve([128, 512], F32, "STG%d" % i)[0] for i in range(4)]
    cur[0] = base
    WO = [carve([128, 16, 512], BF, "WO%d" % i)[0] for i in range(4)]
    CT = [carve([128, 16, 128], BF, "CT%d" % i)[0] for i in range(2)]
    XO = [carve([128, D], F32, "XO%d" % i)[0] for i in range(2)]
    cur[0] = base
    WKO, _ = carve([128, 8, 128], F32, "WKO")
    SHO, _ = carve([128, 128], F32, "SHO")
    POUT, _ = carve([128, 512], F32, "POUT")

    NAMES = ("PS XS AM SPm AT EL ELI KK RN SIG KK2 KR KHf BHf VBF TW GAM KTtm KHtm BHtm Vtm Qm AV KPf U "
             "HTMP WKI GST SHI CATT EXT WA WB PLD PTM GB PIN VC VN32 VNB UC GC T1 CST "
             "Y32 VP GA YA SQT BON HS32 HSB").split()

    def make_set(tg, W):
        k = lambda n: tg + n
        big = W == 128
        d = {}
        d["PS"], _ = carve([128, 25, W + 1], F32, k("PS"))
        d["XS"], _ = carve([128, 25, W], F32, k("XS"))
        d["AM"] = [carve([128, 4, 4, W], BF, k("AM%d" % g))[0] for g in range(4)]
        d["SPm"] = [carve([128, 4, 2, W], BF, k("SP%d" % g))[0] for g in range(4)]
        d["AT"], _ = carve([128, 8, W], F32, k("AT"))
        d["EL"], _ = carve([128, 8, W + 1], F32, k("EL"))
        d["ELI"], _ = carve([128, 8, W], F32, k("ELI"))
        d["KK"], _ = carve([128, 8, W], F32, k("KK"))
        d["RN"], _ = carve([128, 8, W], F32, k("RN"))
        d["SIG"], _ = carve([128, 1024], F32, k("SIG"))
        d["KK2"], _ = carve([128, 8, W], BF, k("KK2"))
        d["KR"], _ = carve([128, 8, 2, W], BF, k("KR"))
        d["KHf"], _ = carve([128, 8, W], BF, k("KHf"))
        d["BHf"], _ = carve([128, 8, W], BF, k("BHf"))
        d["VBF"], _ = carve([128, 8, W], BF, k("VBF"))
        d["TW"], _ = carve([128, W], BF, k("TW"))
        d["GAM"], _ = carve([128, 8], F32, k("GAM"))
        for n in ("KTtm", "KHtm", "BHtm", "Vtm", "AV", "U"):
            d[n], _ = carve([128, 1024], BF, k(n))
        d["Qm"] = [carve([128, 4, W], BF, k("Q%d" % g))[0] for g in range(4)]
        d["KPf"], _ = carve([128, 8, W], BF, k("KPf"))
        d["HTMP"], _ = carve([128, 8, 64], F32, k("HTMP"))
        d["WKI"] = carve([128, 16, 64], F32, k("WKI"))[0]
        d["GST"], _ = carve([128, 64], F32, k("GST"))
        d["SHI"] = carve([128, 128], F32, k("SHI"))[0]
        d["CATT"], _ = carve([128, 16, W], BF, k("CATT"))
        for n in ("EXT", "WA", "WB"):
            d[n], _ = carve([128, 4, 15 + W], F32, k(n))
        d["PLD"], _ = carve([128, 4, W], BF, k("PLD"))
        d["PTM"], _ = carve([128, 4, W], F32, k("PTM"))
        d["GB"], _ = carve([128, 4, W], F32, k("GB"))
        d["PIN"] = carve([128, 512], F32, k("PIN"))[0]
        d["VC"], _ = carve([128, 512], F32, k("VC"))
        d["VN32"], _ = carve([128, 512], F32, k("VN32"))
        d["VNB"], _ = carve([128, 512], BF, k("VNB"))
        for n in ("UC", "GC", "T1"):
            d[n], _ = carve([128, 4, W], F32, k(n))
        d["CST"], _ = carve([128, 16], F32, k("CST"))
        d["HS32"], _ = carve([128, 8, 64], F32, k("HS32"))
        d["HSB"], _ = carve([128, 8, 64], BF, k("HSB"))
        if big:
            d["Y32"] = V(d["AT"].ap.rearrange("p a b -> p (a b)"), k("AT"))
            d["VP"] = V(d["EL"].ap.rearrange("p a b -> p (a b)")[:, 0:1024], k("EL"))
            d["GA"] = d["ELI"]
            d["YA"] = d["KK"]
            d["SQT"] = V(d["RN"].ap.rearrange("p a b -> p (a b)"), k("RN"))
            d["BON"] = V(d["SIG"].ap.rearrange("p (a b) -> p a b", b=128), k("SIG"))
        else:
            d["Y32"], _ = carve([128, 1024], F32, k("Y32"))
            d["VP"], _ = carve([128, 1024], F32, k("VP"))
            d["GA"] = d["ELI"]
            d["YA"] = d["KK"]
            d["SQT"] = d["SIG"]
            d["BON"], _ = carve([128, 8, W], F32, k("BON"))
        return tuple(d[n] for n in NAMES)

    BP = make_set("p.", 128)
    print("arena words used", cur[0], "of", ARW)

    def xsrc(l, r0, R):
        if l == 0:
            if r0 < SEQ_P:
                return V(xp[r0:r0 + R, :], None)
            return V(xs[0:R, :], None)
        return V(xbuf[r0:r0 + R, :], ("xb", r0))

    def phase_norm(l, gsrc, final):
        P.barrier()
        P.dma(NGB, V(gsrc.broadcast_to([128, D]), None))
        for ti, (r0, R) in enumerate(rtiles):
            xt = XT[ti % 2]
            hb = HB[ti % 2]
            P.dma(xt[0:R], xsrc(l, r0, R), q=("sp" if ti % 2 == 0 else "pool"))
            ss = SSA[:, (ti % 2) * 2:(ti % 2) * 2 + 1]
            P.memset(ss[0:R], 0.0)
            P.act(SQJ[0:R], xt[0:R], AF.Square, accum=ss[0:R])
            rs = SSA[:, (ti % 2) * 2 + 1:(ti % 2) * 2 + 2]
            P.ts(rs[0:R], ss[0:R], 1.0 / D, EPS, AL.mult, AL.add)
            P.rsqrt(rs[0:R], rs[0:R])
            if final:
                P.stt(xt[0:R], xt[0:R], rs[0:R], NGB[0:R], AL.mult, AL.mult)
                if r0 < SEQ_P:
                    P.dma(V(yp[r0:r0 + R, :], None), xt[0:R], q="act")
                else:
                    P.dma(V(ys[0:R, :], None), xt[0:R], q="act")
                continue
            P.stt(hb[0:R], xt[0:R], rs[0:R], NGB[0:R], AL.mult, AL.mult)
            g = ti % 2
            for half in range(2):
                pb = psbf(g, half + 2 * ((ti // 2) % 2))
                for k in range(8):
                    kc = half * 8 + k
                    P.tr(pb[:, k * 128:k * 128 + R], hb[0:R, kc * 128:(kc + 1) * 128], IDB[0:R, 0:R])
                src = pb.re("p (a b) -> p a b", b=128)[:, :, 0:R]
                dst = HT[:, half * 8:half * 8 + 8, r0:r0 + R]
                if half == 0:
                    P.copy(dst, src, "act")
                else:
                    P.copy(dst, src, "dve")

    def phase_proj(l):
        P.barrier()
        wv = w_in[l].rearrange("(kc p) c -> p kc c", p=128)
        bank = 0
        sg = 0
        for ch in range(NCH):
            sc, sub = ch // 4, ch % 4
            if sub == 0:
                ncol = min(512, DIN - sc * 512)
                P.dma(WCH[sc % 3][:, :, 0:ncol], V(wv[:, :, sc * 512:sc * 512 + ncol], None), q="pool")
            wch = WCH[sc % 3][:, :, sub * 128:(sub + 1) * 128]
            if 45 <= ch < 49:
                for ti, (r0, R) in enumerate(rtiles):
                    ps = psum(bank // 4, bank % 4)
                    for kc in range(16):
                        P.mm(ps[0:R, 0:128], HT[:, kc, r0:r0 + R], wch[:, kc, :], kc == 0, kc == 15)
                    st = STG[sg % 4]
                    if sg % 2 == 0:
                        P.copy(st[0:R, 0:128], ps[0:R, 0:128], "act")
                    else:
                        P.copy(st[0:R, 0:128], ps[0:R, 0:128], "dve")
                    P.dma(V(vcTM[r0:r0 + R, (ch - 45) * 128:(ch - 44) * 128], ("vc", ti, ch)), st[0:R, 0:128])
                    bank = (bank + 1) % 8
                    sg += 1
                continue
            for gi, (t0, n) in enumerate(tgroups):
                ps = psum(bank // 4, bank % 4)
                for kc in range(16):
                    P.mm(ps[:, 0:n], wch[:, kc, :], HT[:, kc, t0:t0 + n], kc == 0, kc == 15)
                st = STG[sg % 4]
                if sg % 2 == 0:
                    P.copy(st[:, 0:n], ps[:, 0:n], "act")
                else:
                    P.copy(st[:, 0:n], ps[:, 0:n], "dve")
                P.dma(V(projT[ch * 128:(ch + 1) * 128, t0:t0 + n], ("pj", ch, gi)), st[:, 0:n])
                bank = (bank + 1) % 8
                sg += 1

    def pj(c0, c1, a, b):
        ks = set()
        for gi, (t0, n) in enumerate(tgroups):
            if a < t0 + n and b > t0:
                for ch in range(c0, c1):
                    ks.add(("pj", ch, gi))
        return projV[:, c0:c1, a:b], sorted(ks)

    def dma_pj(dst, c0, c1, a, b):
        ap, ks = pj(c0, c1, a, b)
        P.dma(dst, VK(ap, ks))

    def layer_consts(l):
        P.barrier()
        P.dma(PP, V(PPd[l], None))
        P.dma(W0R[0:1], V(w0d[l], None))
        P.dma(LW, V(LWd[l], None), q="pool")
        P.dma(PW, V(pwd[l].rearrange("g c d -> c g d"), None), q="pool")
        P.dma(LNG, V(lngd[l].broadcast_to([128, 512]), None))
        P.dma(BSB, V(gbd[l].broadcast_to([128, 512]).rearrange("p (a b) -> p a b", b=128), None))
        P.dma(WSTF, V(wsTd[l], None))
        P.tt(WST, WSTF, MASK4[:, 3:4, :].bc([128, 4, 128]), AL.mult)
        P.ts(OMK, PP[:, 33:41], -1.0, 1.0, AL.mult, AL.add)

    def tile_call(B, l, T, tok0, first, last, sb, rt, col0):
        (PS, XS, AM, SPm, AT, EL, ELI, KK, RN, SIG, KK2, KR, KHf, BHf, VBF, TW, GAM, KTtm, KHtm, BHtm, Vtm, Qm,
         AV, KPf, U, HTMP, WKI, GST, SHI, CATT, EXT, WA, WB, PLD, PTM, GB, PIN, VC, VN32, VNB, UC, GC, T1, CST,
         Y32, VP, GA, YA, SQT, BON, HS32, HSB) = B
        nlev = int(math.log2(T)) - 1
        bc8 = lambda c0: PP[:, c0:c0 + 8].us(2).bc([128, 8, T])
        Lx = 15 + T

        def pool_load():
            if first and sb is None:
                P.memset(EXT[:, :, 0:15], 0.0)
                dma_pj(EXT[:, :, 15:15 + T], 33, 37, tok0, tok0 + T)
            elif first:
                P.dma(PIN[0:15, :], V(st_pool[l, sb], None))
                dma_pj(EXT[:, :, 15:15 + T], 33, 37, tok0, tok0 + T)
            else:
                dma_pj(EXT[:, :, 0:15 + T], 33, 37, tok0 - 15, tok0 + T)
            dma_pj(GB[:, :, 0:T], 37, 41, tok0, tok0 + T)

        def cg_load():
            vks = [("vc", rt, ch) for ch in range(45, 49)]
            P.dma(VC[0:T, :], VK(vcTM[tok0:tok0 + T, :], vks))
            dma_pj(UC[:, :, 0:T], 41, 45, tok0, tok0 + T)
            dma_pj(GC[:, :, 0:T], 49, 53, tok0, tok0 + T)

        def pool_dve():
            if first and sb is not None:
                pp_ = psum3(0, 2, 1, 128)
                for g4 in range(4):
                    P.tr(pp_[:, g4, 0:15], PIN[0:15, g4 * 128:(g4 + 1) * 128], IDF[0:15, 0:15])
                P.copy(EXT[:, :, 0:15], pp_[:, :, 0:15], "dve")
            P.tt(WA[:, 0:4, 1:Lx], EXT[:, 0:4, 1:Lx], EXT[:, 0:4, 0:Lx - 1], AL.add)
            P.tt(WB[:, 1:4, 3:Lx], WA[:, 1:4, 3:Lx], WA[:, 1:4, 1:Lx - 2], AL.add)
            P.tt(WA[:, 2:4, 7:Lx], WB[:, 2:4, 7:Lx], WB[:, 2:4, 3:Lx - 4], AL.add)
            P.tt(WB[:, 3:4, 15:Lx], WA[:, 3:4, 15:Lx], WA[:, 3:4, 7:Lx - 8], AL.add)
            ic = IC0 if (first and sb is None) else IC1
            for g4 in range(4):
                wsrc = (WA, WB, WA, WB)[g4]
                P.tt(PTM[:, g4, 0:T], wsrc[:, g4, 15:Lx], ic[:, g4, 0:T], AL.mult)
            P.tt(PLD[:, :, 0:T], PTM[:, :, 0:T], EXT[:, :, 15:Lx], AL.subtract)

        def cg_dve1():
            cm = CST[0:T, 0:1]
            P.reduce(cm, VC[0:T, :])
            P.ts(cm, cm, -1.0 / 512, None, AL.mult)
            P.ts(VN32[0:T, :], VC[0:T, :], cm, None, AL.add)

        def cg_mid():
            cv = CST[0:T, 1:2]
            P.memset(cv, 0.0)
            P.act(VC[0:T, :], VN32[0:T, :], AF.Square, accum=cv)
            P.ts(cv, cv, 1.0 / 512, LN_EPS, AL.mult, AL.add)
            P.rsqrt(cv, cv)
            P.stt(VN32[0:T, :], VN32[0:T, :], cv, LNG[0:T, :], AL.mult, AL.mult)
            P.copy(VNB[0:T, :], VN32[0:T, :], "act")
            if sb is not None:
                P.dma(V(s_cv[l, sb], None), VN32[0:T, :], q="pool")
            P.act(GB[:, :, 0:T], GB[:, :, 0:T], AF.Silu)
            P.act(GC[:, :, 0:T], GC[:, :, 0:T], AF.Silu)

        def pool_tail():
            mx = psum3(0, 3, 1, 128)
            for g4 in range(4):
                P.mm(mx[:, g4, 0:T], PW[:, g4, :], PLD[:, g4, 0:T])
            P.tt(PTM[:, :, 0:T], mx[:, :, 0:T], PP[:, 73:77].us(2).bc([128, 4, T]), AL.mult)
            P.tt(CATT[:, 8:12, 0:T], PTM[:, :, 0:T], GB[:, :, 0:T], AL.mult)
            if last:
                po_ = psum(0, 2)
                for g4 in range(4):
                    P.tr(po_[0:15, g4 * 128:(g4 + 1) * 128], EXT[:, g4, T:T + 15], IDF)
                P.copy(POUT[0:15, :], po_[0:15, :], "act")
                dst = p_pool[l] if sb is None else s_pool[l, sb]
                P.dma(V(dst, None), POUT[0:15, :], q="pool")

        def cg_tail():
            mxc = psum3(1, 3, 1, 128)
            for g4 in range(4):
                P.mm(mxc[:, g4, 0:T], VNB[0:T, g4 * 128:(g4 + 1) * 128], WST[0:T, g4, 0:T])
            t1 = T1[:, :, 0:T]
            P.tt(t1, mxc[:, :, 0:T], BSB[:, :, 0:T], AL.add)
            P.tt(t1, t1, UC[:, :, 0:T], AL.mult)
            P.tt(CATT[:, 12:16, 0:T], t1, GC[:, :, 0:T], AL.mult)

        if first and sb is None:
            P.memset(PS[:, :, 0:1], 0.0)
            dma_pj(PS[:, :, 1:T + 1], 0, 25, tok0, tok0 + T)
        elif first:
            P.dma(SHI[0:25, :], V(st_shift[l, sb].rearrange("(ch p) -> ch p", p=128), None))
            pt = psum(1, 3)
            P.tr(pt[:, 0:25], SHI[0:25, :], IDF[0:25, 0:25])
            P.copy(PS[:, :, 0], pt[:, 0:25], "dve")
            dma_pj(PS[:, :, 1:T + 1], 0, 25, tok0, tok0 + T)
        else:
            dma_pj(PS[:, :, 0:T + 1], 0, 25, tok0 - 1, tok0 + T)
        if last:
            pt = psum(1, 3)
            P.tr(pt[0:25, 0:128], PS[:, :, T], IDF)
            P.copy(SHO[0:25, :], pt[0:25, 0:128], "act")
            dst = p_shift[l] if sb is None else s_shift[l, sb]
            P.dma(V(dst.rearrange("(ch p) -> ch p", p=128), None), SHO[0:25, :], q="pool")
        pool_load()
        cg_load()
        P.memset(EL[:, :, 0:1], 1.0)
        XSa = V(XS.ap[:, 0:16], XS.key + "a")
        XSb = V(XS.ap[:, 16:24], XS.key + "b")
        XSl = V(XS.ap[:, 24:25], XS.key + "l")
        for (c0, c1, eng, xv) in ((24, 25, "dve", XSl), (0, 16, "dve", XSa), (16, 24, "dve", XSb)):
            xx = xv[:, :, 0:T]
            P.tt(xx, PS[:, c0:c1, 0:T], PS[:, c0:c1, 1:T + 1], AL.subtract, eng=eng)
            P.tt(xx, xx, PP[:, c0:c1].us(2).bc([128, c1 - c0, T]), AL.mult, eng=eng)
            P.tt(xx, xx, PS[:, c0:c1, 1:T + 1], AL.add, eng=eng)
        Xr = XSa[:, 0:8, 0:T]
        Xk = XSa[:, 8:16, 0:T]
        Xv = XSb[:, 0:8, 0:T]
        yield
        P.act(TW[0:64, 0:T], XSl[0:64, 0, 0:T], AF.Tanh)
        P.act(TW[64:128, 0:T], XSl[64:128, 0, 0:T], AF.Copy)
        zt = psum(0, 0, 2)
        for h2 in range(2):
            P.mm(zt[0:T, h2 * 512:(h2 + 1) * 512], TW[0:64, 0:T], LW[0:64, h2 * 512:(h2 + 1) * 512], True, False)
            P.mm(zt[0:T, h2 * 512:(h2 + 1) * 512], ONESF[0:1, 0:T], W0R[0:1, h2 * 512:(h2 + 1) * 512], False, True)
        at = psum3(1, 0, 2, 128)
        for ch in range(8):
            P.mm(at[:, ch, 0:T], LW[64:128, ch * 128:(ch + 1) * 128], TW[64:128, 0:T])
        kk = KK[:, :, 0:T]
        P.tt(kk, Xk, bc8(25), AL.mult)
        P.act(KK2[:, :, 0:T], kk, AF.Square)
        hs = psum3(1, 2, 2, 128)
        for ch in range(8):
            P.mm(hs[:, ch, 0:T], BLK1, KK2[:, ch, 0:T])
        pool_dve()
        cg_dve1()
        P.act(SIG[0:T, :], zt[0:T, :], AF.Sigmoid)
        a_ = AT[:, :, 0:T]
        P.tt(a_, at[:, :, 0:T], bc8(65), AL.add)
        P.act(a_, a_, AF.Sigmoid)
        lt = psum3(0, 2, 2, 128)
        for ch in range(8):
            P.mm(lt[:, ch, 0:T], SIG[0:T, ch * 128:(ch + 1) * 128], TRIS[0:T, 0:T])
        P.act(EL[:, :, 1:T + 1], lt[:, :, 0:T], AF.Exp)
        P.act(ELI[:, :, 0:T], lt[:, :, 0:T], AF.Exp, scale=-1.0)
        rn = RN[:, :, 0:T]
        P.ts(rn, hs[:, :, 0:T], 1e-12, None, AL.max)
        P.rsqrt(rn, rn)
        P.tt(kk, kk, rn, AL.mult)
        yield
        P.tt(rn, a_, bc8(33), AL.mult)
        P.tt(rn, rn, OMK.us(2).bc([128, 8, T]), AL.add)
        P.tt(rn, rn, Xk, AL.mult)
        P.tt(a_, kk, a_, AL.mult)
        P.tt(KR[:, :, 0, 0:T], kk, EL[:, :, 0:T], AL.mult)
        P.tt(KR[:, :, 1, 0:T], Xr, EL[:, :, 1:T + 1], AL.mult)
        P.tt(KHf[:, :, 0:T], rn, ELI[:, :, 0:T], AL.mult)
        P.tt(BHf[:, :, 0:T], a_, ELI[:, :, 0:T], AL.mult)
        P.copy(GAM, EL[:, :, T], "dve")
        bon = BON[:, :, 0:T]
        P.tt(bon, Xr, bc8(41), AL.mult)
        P.tt(KK2[:, :, 0:T], bon, rn, AL.mult)
        bs = psum3(0, 0, 2, 128)
        for ch in range(8):
            P.mm(bs[:, ch, 0:T], BLK1, KK2[:, ch, 0:T])
        P.tt(bon, bs[:, :, 0:T], Xv, AL.mult)
        P.copy(VBF[:, :, 0:T], Xv, "act")
        cg_mid()
        yield
        for qi, (src, dst) in enumerate(((KR[:, :, 0, :], KTtm), (KHf, KHtm), (BHf, BHtm), (VBF, Vtm))):
            pb = psbf(1, qi)
            for ch in range(8):
                P.tr(pb[0:T, ch * 128:(ch + 1) * 128], src[:, ch, 0:T], IDB)
            if qi == 2:
                P.act(dst[0:T, :], pb[0:T, :], AF.Copy, scale=-1.0)
            else:
                P.copy(dst[0:T, :], pb[0:T, :], "act" if qi % 2 == 0 else "dve")
        pool_tail()
        cg_tail()
        dma_pj(GA[:, :, 0:T], 25, 33, tok0, tok0 + T)
        P.act(GA[:, :, 0:T], GA[:, :, 0:T], AF.Silu)
        yield "neu"
        m4 = MASK4[0:T, :, 0:T].us(1).bc([T, 4, 4, T])

        def pga(g):
            return V(PG[g % 2][:, :, :].rearrange("p h (a b) -> p h a b", b=128), "PSUM")

        def mm2(pg, hh, s0, lhsT, rhs3):
            if T == 128:
                P.mm(pg[0:T, hh, s0:s0 + 2, :].re("p a b -> p (a b)"), lhsT, rhs3.re("p a b -> p (a b)"))
            else:
                P.mm(pg[0:T, hh, s0, 0:T], lhsT, rhs3[:, 0, 0:T])
                P.mm(pg[0:T, hh, s0 + 1, 0:T], lhsT, rhs3[:, 1, 0:T])

        def a_mats(g):
            pg = pga(g)
            for hh in range(4):
                h = 4 * g + hh
                ch, po = h // 2, (h % 2) * 64
                mm2(pg, hh, 0, BHf[po:po + 64, ch, 0:T], KR[po:po + 64, ch])
                mm2(pg, hh, 2, KHf[po:po + 64, ch, 0:T], KR[po:po + 64, ch])
            P.tt(AM[g][0:T, :, :, 0:T], pg[0:T, :, :, 0:T], m4, AL.mult)
            for hh in range(4):
                h = 4 * g + hh
                ch, po = h // 2, (h % 2) * 64
                P.mm(pg[0:T, hh, 2, 0:T], KR[po:po + 64, ch, 0, 0:T], BHf[po:po + 64, ch, 0:T])
            P.tt(Qm[g][0:T, :, 0:T], pg[0:T, :, 2, 0:T], NSL[0:T, 0:T].us(1).bc([T, 4, T]), AL.mult)
            P.tt(SPm[g][0:T, :, 0, 0:T], AM[g][0:T, :, 0, 0:T], IDB[0:T, 0:T].us(1).bc([T, 4, T]), AL.add)

        def neu_pre_mm(g):
            pg = pga(g)
            for hh in range(4):
                P.mm(pg[0:T, hh, 1, 0:T], Qm[g][0:T, hh, 0:T], AM[g][0:T, hh, 0, 0:T])
                P.mm(pg[0:T, hh, 2, 0:T], AM[g][0:T, hh, 0, 0:T], Qm[g][0:T, hh, 0:T])

        def neu_pq_ev(g):
            pg = pga(g)
            P.copy(SPm[g][0:T, :, 1, 0:T], pg[0:T, :, 1, 0:T], "act")
            P.copy(Qm[g][0:T, :, 0:T], pg[0:T, :, 2, 0:T], "act")

        def neu_mm(g, lastlev):
            pg = pga(g)
            for hh in range(4):
                if lastlev:
                    P.mm(pg[0:T, hh, 0, 0:T], Qm[g][0:T, hh, 0:T], SPm[g][0:T, hh, 0, 0:T])
                else:
                    mm2(pg, hh, 0, Qm[g][0:T, hh, 0:T], SPm[g][0:T, hh])
                    P.mm(pg[0:T, hh, 2, 0:T], SPm[g][0:T, hh, 1, 0:T], Qm[g][0:T, hh, 0:T])

        def neu_ev(g, lastlev):
            pg = pga(g)
            P.tt(SPm[g][0:T, :, 0, 0:T], SPm[g][0:T, :, 0, 0:T], pg[0:T, :, 0, 0:T], AL.add)
            if not lastlev:
                neu_pq_ev(g)

        for pair in range(2):
            ga, gb = 2 * pair, 2 * pair + 1
            a_mats(ga)
            a_mats(gb)
            yield "neu"
            neu_pre_mm(ga)
            neu_pre_mm(gb)
            neu_pq_ev(ga)
            neu_pq_ev(gb)
            for lev in range(nlev):
                yield "neu"
                lastlev = lev == nlev - 1
                neu_mm(ga, lastlev)
                neu_mm(gb, lastlev)
                neu_ev(ga, lastlev)
                neu_ev(gb, lastlev)
            yield "neu" if pair == 0 else None
        av = psum(0, 0, 2)
        for h in range(16):
            P.mm(av[0:T, h * 64:(h + 1) * 64], AM[h // 4][0:T, h % 4, 2, 0:T], Vtm[0:T, h * 64:(h + 1) * 64])
        P.copy(AV[0:T, :], av[0:T, :], "act")
        yield
        kp = V(PG[1][:, :, :].rearrange("p a (b c) -> p (a b) c", c=128), "PSUM")
        for h in range(16):
            ch = h // 2
            P.mm(kp[:, h, 0:T], KTtm[0:T, ch * 128:(ch + 1) * 128], SPm[h // 4][0:T, h % 4, 0, 0:T])
        kp4 = V(PG[1][:, :, :].rearrange("p a (b two c) -> p (a b) two c", two=2, c=128), "PSUM")
        P.copy(KPf[0:64, :, 0:T], kp4[0:64, :, 0, 0:T], "act")
        P.copy(KPf[64:128, :, 0:T], kp4[64:128, :, 1, 0:T], "dve")
        yield
        vp = psum(0, 2, 2)
        for h in range(16):
            P.mm(vp[0:T, h * 64:(h + 1) * 64], SPm[h // 4][0:T, h % 4, 0, 0:T], AV[0:T, h * 64:(h + 1) * 64])
        P.copy(VP[0:T, :], vp[0:T, :], "act")
        yield
        if first:
            if sb is None:
                P.memset(HS32, 0.0)
                P.memset(HSB, 0.0)
            else:
                P.dma(WKI[0:64], V(st_wkv[l, sb].rearrange("h i j -> i h j"), None))
                hp = psum3(1, 3, 1, 64)
                for ch in range(8):
                    P.tr(hp[:, ch, :], WKI[0:64, 2 * ch:2 * ch + 2, :].re("p a b -> p (a b)"), IDF[0:64, 0:64])
                P.copy(HS32, hp, "act")
                P.copy(HSB, hp, "dve")
        um = psum(0, 0, 2)
        hpar = [2 * c for c in range(8)] + [2 * c + 1 for c in range(8)]
        for h in hpar:
            ch, po = h // 2, (h % 2) * 64
            c0 = (h % 2) * 512 + ch * 64
            P.mm(um[0:T, c0:c0 + 64], KPf[po:po + 64, ch, 0:T], HSB[po:po + 64, ch, :])
        hm = lambda v: v[0:T, :].re("p (c two i) -> p c two i", two=2, i=64)
        pm = lambda v: v[0:T, :].re("p (two c i) -> p c two i", two=2, i=64)
        P.tt(hm(U), hm(VP), pm(um), AL.add)
        yield
        yps = psum(0, 2, 2)
        for h in hpar:
            ch, po = h // 2, (h % 2) * 64
            c0 = (h % 2) * 512 + ch * 64
            P.mm(yps[0:T, c0:c0 + 64], KR[po:po + 64, ch, 1, 0:T], HSB[po:po + 64, ch, :])
        yps2 = psum(1, 2, 2)
        for h in range(16):
            o = yps2[0:T, h * 64:(h + 1) * 64]
            P.mm(o, AM[h // 4][0:T, h % 4, 1, 0:T], U[0:T, h * 64:(h + 1) * 64], True, False)
            P.mm(o, AM[h // 4][0:T, h % 4, 3, 0:T], Vtm[0:T, h * 64:(h + 1) * 64], False, True)
        P.copy(Y32[0:T, :], yps2[0:T, :], "act")
        P.tt(hm(Y32), hm(Y32), pm(yps), AL.add)
        yield
        hn = psum3(1, 0, 2, 64)
        for h in range(16):
            ch = h // 2
            P.mm(hn[:, h, :], BHtm[0:T, ch * 128:(ch + 1) * 128], U[0:T, h * 64:(h + 1) * 64], True, False)
            P.mm(hn[:, h, :], KHtm[0:T, ch * 128:(ch + 1) * 128], Vtm[0:T, h * 64:(h + 1) * 64], False, True)
        hn4 = V(PG[1][:, 0:2, :].rearrange("p a (b two c) -> p (a b) two c", two=2, c=64), "PSUM")
        P.tt(HTMP[0:64], HS32[0:64], hn4[0:64, :, 0, :], AL.add)
        P.tt(HTMP[64:128], HS32[64:128], hn4[64:128, :, 1, :], AL.add)
        P.tt(HS32, HTMP, GAM.us(2).bc([128, 8, 64]), AL.mult)
        P.copy(HSB, HS32, "act")
        yield
        if last:
            wk = psum3(1, 2, 2, 128)
            for ch in range(8):
                P.tr(wk[0:64, ch, :], HS32[:, ch, :], IDF)
            P.copy(WKO[0:64], wk[0:64], "act")
            dst = p_wkv[l] if sb is None else s_wkv[l, sb]
            P.dma(V(dst.rearrange("(c two) i j -> i c two j", two=2), None),
                  WKO[0:64].re("p c (two j) -> p c two j", two=2), q="pool")
        yield
        y3 = Y32[0:T, :].re("p (h i) -> p h i", i=64)
        mean = GST[0:T, 0:16]
        P.reduce(mean, y3)
        P.ts(mean, mean, -1.0 / 64, None, AL.mult)
        P.tt(y3, y3, mean.us(2).bc([T, 16, 64]), AL.add)
        P.act(SQT[0:T, :], Y32[0:T, :], AF.Square)
        var = GST[0:T, 16:32]
        P.reduce(var, SQT[0:T, :].re("p (h i) -> p h i", i=64))
        P.ts(var, var, 1.0 / 64, GN_EPS, AL.mult, AL.add)
        P.rsqrt(var, var)
        P.tt(y3, y3, var.us(2).bc([T, 16, 64]), AL.mult)
        ynT = psum3(0, 0, 2, 128)
        for ch in range(8):
            P.tr(ynT[:, ch, 0:T], Y32[0:T, ch * 128:(ch + 1) * 128], IDF[0:T, 0:T])
        ya = YA[:, :, 0:T]
        P.tt(ya, ynT[:, :, 0:T], bc8(49), AL.mult)
        P.tt(ya, ya, bc8(57), AL.add)
        P.tt(ya, ya, bon, AL.add)
        ga_ = GA[:, :, 0:T]
        P.tt(CATT[:, 0:8, 0:T], ya, ga_, AL.mult)
        yield
        P.dma(V(catT[rt, :, :, col0:col0 + T], ("cat", rt)), CATT[:, :, 0:T], q="pool")

    def phase_mix(l):
        layer_consts(l)

        def chain(gens):
            for g in gens:
                yield from g

        for b in range(NSQ):
            for _ in tile_call(BP, l, 8, SEQ_P + 8 * b, True, True, b, NRT - 1, 8 * b):
                pass
        for i in range(NPT):
            for _ in tile_call(BP, l, 128, i * 128, i == 0, i == NPT - 1, None, i, 0):
                pass

    def phase_out(l):
        P.barrier()
        wv = w_out[l].rearrange("(kc p) n -> p kc n", p=128)
        for n4 in range(4):
            P.dma(WO[n4], V(wv[:, :, n4 * 512:(n4 + 1) * 512], None), q="pool")
        for ti, (r0, R) in enumerate(rtiles):
            ct = CT[ti % 2]
            xo = XO[ti % 2]
            P.dma(ct[:, :, 0:R], V(catT[ti, :, :, 0:R], ("cat", ti)), q="pool")
            P.dma(xo[0:R], xsrc(l, r0, R))
            for n4 in range(4):
                ps = psum(ti % 2, n4)
                for kc in range(16):
                    P.mm(ps[0:R, :], ct[:, kc, 0:R], WO[n4][:, kc, :], kc == 0, kc == 15)
                P.tt(xo[0:R, n4 * 512:(n4 + 1) * 512], xo[0:R, n4 * 512:(n4 + 1) * 512], ps[0:R, :], AL.add)
            P.dma(V(xbuf[r0:r0 + R, :], ("xb", r0)), xo[0:R], q="act")

    for l in range(NL):
        phase_norm(l, norm_g[l:l + 1, :], False)
        phase_proj(l)
        phase_mix(l)
        phase_out(l)
    phase_norm(NL, fnorm_g, True)
    P.barrier()

    with nc.Block() as block:
        def emit(name, e):
            for waits, fn, inc in P.ops[name]:
                for sem, val in waits:
                    e.wait_ge(sems[sem], val)
                if fn is not None:
                    fn(e).then_inc(sems[inc[0]], inc[1])

        @block.tensor
        def _(e):
            emit("pe", e)

        @block.scalar
        def _(e):
            emit("act", e)

        @block.vector
        def _(e):
            emit("dve", e)

        @block.gpsimd
        def _(e):
            emit("pool", e)

        @block.sync
        def _(e):
            emit("sp", e)
    es.close()
    stats = {k: len(v) for k, v in P.ops.items()}
    return nc, stats


def _consts():
    s = np.arange(128)[:, None]
    t = np.arange(128)[None, :]
    su = (s < t).astype(np.float32)
    ui = (s <= t).astype(np.float32)
    sl = (s > t).astype(np.float32)
    c = {}
    c["cmask4"] = np.concatenate([-su, -ui, su, ui], axis=1).astype(np.float32)
    c["cnsl"] = -sl
    c["ctris"] = (-math.exp(-0.5) * ui).astype(np.float32)
    c["cidf"] = np.eye(128, dtype=np.float32)
    blk = np.zeros((128, 128), np.float32)
    blk[:64, :64] = 1.0
    blk[64:, 64:] = 1.0
    c["cblk"] = blk
    wins = (2, 4, 8, 16)
    ic0 = np.zeros((128, 4, 128), np.float32)
    ic1 = np.zeros((128, 4, 128), np.float32)
    pos = np.arange(128)
    for g, w in enumerate(wins):
        ic0[:, g, :] = (1.0 / np.minimum(pos + 1, w))[None, :]
        ic1[:, g, :] = 1.0 / w
    c["cic0"] = ic0.reshape(128, 512)
    c["cic1"] = ic1.reshape(128, 512)
    return c


def _chunked(vec, n):
    NL = vec.shape[0]
    return np.ascontiguousarray(vec.reshape(NL, n, 128).transpose(0, 2, 1))


def make_in_maps(inp, NL, SEQ_P, NSQ, ncores, nprompt):
    f = lambda a: np.ascontiguousarray(np.asarray(a, dtype=np.float32))
    pp = np.concatenate([
        _chunked(f(inp["shift_mu"])[:NL], 25), _chunked(f(inp["k_k"])[:NL], 8), _chunked(f(inp["k_a"])[:NL], 8),
        _chunked(f(inp["r_k"])[:NL].reshape(NL, 1024), 8), _chunked(f(inp["lnx_g"])[:NL], 8),
        _chunked(f(inp["lnx_b"])[:NL], 8), _chunked(f(inp["a0"])[:NL], 8), _chunked(f(inp["pool_scale"])[:NL], 4)],
        axis=2)
    assert pp.shape[2] == NPP
    shared = {
        "norm_g": f(inp["norm_g"])[:NL], "fnorm_g": f(inp["final_norm_g"]).reshape(1, D),
        "w_in": f(inp["w_in"])[:NL], "w_out": f(inp["w_out"])[:NL], "pp": np.ascontiguousarray(pp),
        "w0": f(inp["w0"])[:NL].reshape(NL, 1, 1024),
        "lw": np.ascontiguousarray(np.concatenate([f(inp["w_up"])[:NL], f(inp["a_up"])[:NL]], axis=1)),
        "pool_w": f(inp["pool_w"])[:NL], "lng": f(inp["gmlp_ln_g"])[:NL].reshape(NL, 1, 512),
        "wsT": np.ascontiguousarray(f(inp["gmlp_ws"])[:NL].transpose(0, 3, 1, 2)),
        "gb": f(inp["gmlp_b"])[:NL].reshape(NL, 1, 512),
    }
    shared.update(_consts())
    xpr = f(inp["x_prompt"])
    xsa = f(inp["x_sample"])
    sts, stw, stp = f(inp["state_shift"]), f(inp["state_wkv"]), f(inp["state_pool"])
    maps = []
    for c in range(ncores):
        b = c % nprompt
        m = dict(shared)
        m["xp"] = np.ascontiguousarray(xpr[b, :SEQ_P])
        sl = slice(c * NSQ, (c + 1) * NSQ)
        m["xs"] = np.ascontiguousarray(xsa[sl].reshape(NSQ * 8, D))
        m["st_shift"] = np.ascontiguousarray(sts[:NL, sl])
        m["st_wkv"] = np.ascontiguousarray(stw[:NL, sl])
        m["st_pool"] = np.ascontiguousarray(stp[:NL, sl])
        maps.append(m)
    return maps


_CACHE = {}


def run(inp, NL, SEQ_P, NSQ, ncores, nprompt):
    key = (NL, SEQ_P, NSQ)
    if key not in _CACHE:
        _CACHE[key] = build(NL, SEQ_P, NSQ)[0]
    nc = _CACHE[key]
    maps = make_in_maps(inp, NL, SEQ_P, NSQ, ncores, nprompt)
    res = run_bass_kernel_spmd(nc, maps, core_ids=list(range(ncores)))
    R = res.results
    g = lambda name, cores: np.stack([np.asarray(R[c][name], dtype=np.float32) for c in cores])
    pc = list(range(nprompt))
    ac = list(range(ncores))
    y_prompt = g("yp", pc)
    y_sample = g("ys", ac).reshape(ncores * NSQ, 8, D)
    p_shift = g("p_shift", pc).transpose(1, 0, 2)
    p_wkv = g("p_wkv", pc).transpose(1, 0, 2, 3, 4)
    p_pool = g("p_pool", pc).transpose(1, 0, 2, 3)
    cat = lambda name: np.concatenate([np.asarray(R[c][name], dtype=np.float32) for c in ac], axis=1)
    return (y_prompt, y_sample, np.ascontiguousarray(p_shift), np.ascontiguousarray(p_wkv),
            np.ascontiguousarray(p_pool), cat("s_shift"), cat("s_wkv"), cat("s_pool"), cat("s_cv"))


def kernel(**inputs):
    return run(inputs, 4, 2048, 16, 8, 4)
```

```python
import contextlib
import math
import numpy as np
import concourse.bass as bass
import concourse.mybir as mybir
from concourse.bass_utils import run_bass_kernel_spmd

F32 = mybir.dt.float32
BF = mybir.dt.bfloat16
AL = mybir.AluOpType
AF = mybir.ActivationFunctionType
AX = mybir.AxisListType

D = 2048
DIN = 6784
NCH = 53
EPS = 1e-6
GN_EPS = 64 * 1e-5
LN_EPS = 1e-5
NPP = 77
ENG = ("pe", "act", "dve", "pool", "sp")
KD = 16


class V:
    __slots__ = ("ap", "key")

    def __init__(s, ap, key):
        s.ap = ap
        s.key = key

    def __getitem__(s, i):
        return V(s.ap[i], s.key)

    def bc(s, shape):
        return V(s.ap.broadcast_to(list(shape)), s.key)

    def re(s, pat, **kw):
        return V(s.ap.rearrange(pat, **kw), s.key)

    def us(s, ax):
        return V(s.ap.unsqueeze(ax), s.key)

    def cast(s, dt):
        return V(s.ap.bitcast(dt), s.key)


class VK(V):
    __slots__ = ("keys",)

    def __init__(s, ap, keys):
        V.__init__(s, ap, "MULTI")
        s.keys = keys


def keys_of(v):
    if v is None or not isinstance(v, V) or v.key is None:
        return []
    if isinstance(v, VK):
        return list(v.keys)
    if v.key == "PSUM":
        ap = v.ap
        es = 2 if ap.dtype == BF else 4
        pstep = ap.ap[0][0]
        off = ap.offset % pstep
        dims = ap.ap[1:]
        starts = [off]
        for (st, cnt) in dims[:-1]:
            starts = [s0 + st * i for s0 in starts for i in range(cnt)]
        lst, lcnt = dims[-1]
        banks = set()
        for s0 in starts:
            banks.add((s0 * es) // 2048)
            banks.add(((s0 + lst * (lcnt - 1)) * es) // 2048)
        return [(ap.name, b) for b in banks]
    return [v.key]


class Prog:
    def __init__(s):
        s.ops = {e: [] for e in ENG}
        s.cnt = {e: 0 for e in ENG}
        s.lastw = {}
        s.readers = {}
        s.seen = {e: {} for e in ENG}
        s.dq = {"sp": 0, "pool": 0, "act": 0}

    def _deps(s, eng, rk, wk):
        need = {}

        def add(ev):
            sem, val, src = ev
            if eng == "pe" and src == "pe":
                return
            if need.get(sem, 0) < val:
                need[sem] = val

        for k in rk:
            if k in s.lastw:
                add(s.lastw[k])
        for k in wk:
            if k in s.lastw:
                add(s.lastw[k])
            for sem, (val, src) in s.readers.get(k, {}).items():
                add((sem, val, src))
        waits = []
        for sem, val in need.items():
            if s.seen[eng].get(sem, 0) >= val:
                continue
            s.seen[eng][sem] = val
            waits.append((sem, val))
        return waits

    def _commit(s, ev, rk, wk):
        sem, val, src = ev
        for k in rk:
            s.readers.setdefault(k, {})[sem] = (val, src)
        for k in wk:
            s.lastw[k] = ev
            s.readers[k] = {}

    def op(s, eng, fn, reads, writes):
        rk = [k for v in reads for k in keys_of(v)]
        wk = [k for v in writes for k in keys_of(v)]
        wk += [k for k in rk if isinstance(k, tuple) and isinstance(k[0], str) and k[0].startswith("pg")]
        waits = s._deps(eng, rk, wk)
        s.cnt[eng] += 1
        ev = ("E_" + eng, s.cnt[eng], eng)
        s.ops[eng].append((waits, fn, ("E_" + eng, 1)))
        s._commit(ev, rk, wk)

    def dma(s, out, in_, q="sp"):
        rk = keys_of(in_)
        wk = keys_of(out)
        i = s.dq[q]
        s.dq[q] += 1
        slot = i % KD
        val = 16 * (i // KD + 1)
        sem = "D_%s_%d" % (q, slot)
        waits = s._deps(q, rk, wk)
        if i >= KD and s.seen[q].get(sem, 0) < val - 16:
            s.seen[q][sem] = val - 16
            waits.append((sem, val - 16))
        o, a = out.ap, in_.ap
        s.ops[q].append((waits, lambda e: e.dma_start(out=o, in_=a), (sem, 16)))
        s._commit((sem, val, "dma"), rk, wk)

    def barrier(s):
        for e in ENG:
            waits = []
            for e2 in ENG:
                c = s.cnt[e2]
                sem = "E_" + e2
                if c > 0 and s.seen[e].get(sem, 0) < c:
                    s.seen[e][sem] = c
                    waits.append((sem, c))
            for q in ("sp", "pool", "act"):
                n = s.dq[q]
                for slot in range(KD):
                    u = (n - slot + KD - 1) // KD if n > slot else 0
                    sem = "D_%s_%d" % (q, slot)
                    if u > 0 and s.seen[e].get(sem, 0) < 16 * u:
                        s.seen[e][sem] = 16 * u
                        waits.append((sem, 16 * u))
            if waits:
                s.ops[e].append((waits, None, None))

    def mm(s, out, lhsT, rhs, start=True, stop=True):
        o, l, r = out.ap, lhsT.ap, rhs.ap
        s.op("pe", lambda e: e.matmul(o, l, r, start=start, stop=stop), [lhsT, rhs], [out])

    def tr(s, out, in_, ident):
        o, i, d = out.ap, in_.ap, ident.ap
        s.op("pe", lambda e: e.transpose(o, i, d), [in_, ident], [out])

    def act(s, out, in_, func, bias=None, scale=None, accum=None):
        o, i = out.ap, in_.ap
        kw = {}
        rd = [in_]
        wr = [out]
        if bias is not None:
            kw["bias"] = bias.ap
            rd.append(bias)
        if scale is not None:
            kw["scale"] = scale
        if accum is not None:
            kw["accum_out"] = accum.ap
            wr.append(accum)
        s.op("act", lambda e: e.activation(out=o, in_=i, func=func, **kw), rd, wr)

    def tt(s, out, in0, in1, op, eng="dve"):
        o, a, b = out.ap, in0.ap, in1.ap
        s.op(eng, lambda e: e.tensor_tensor(out=o, in0=a, in1=b, op=op), [in0, in1], [out])

    def ts(s, out, in0, s1, s2, op0, op1=None, eng="dve"):
        o, a = out.ap, in0.ap
        rd = [in0]
        x1 = s1
        x2 = s2
        if isinstance(s1, V):
            rd.append(s1)
            x1 = s1.ap
        if isinstance(s2, V):
            rd.append(s2)
            x2 = s2.ap
        if op1 is None:
            s.op(eng, lambda e: e.tensor_scalar(out=o, in0=a, scalar1=x1, scalar2=None, op0=op0), rd, [out])
        else:
            s.op(eng, lambda e: e.tensor_scalar(out=o, in0=a, scalar1=x1, scalar2=x2, op0=op0, op1=op1), rd, [out])

    def stt(s, out, in0, scalar, in1, op0, op1):
        o, a, b = out.ap, in0.ap, in1.ap
        rd = [in0, in1]
        sc = scalar
        if isinstance(scalar, V):
            rd.append(scalar)
            sc = scalar.ap
        s.op("dve", lambda e: e.scalar_tensor_tensor(out=o, in0=a, scalar=sc, in1=b, op0=op0, op1=op1), rd, [out])

    def copy(s, out, in_, eng="act"):
        o, i = out.ap, in_.ap
        if eng == "act":
            s.op("act", lambda e: e.activation(out=o, in_=i, func=AF.Copy), [in_], [out])
        else:
            s.op(eng, lambda e: e.tensor_copy(out=o, in_=i), [in_], [out])

    def rsqrt(s, out, in_):
        o, i = out.ap, in_.ap
        s.op("act", lambda e: e.activation(out=o, in_=i, func=AF.Sqrt), [in_], [out])
        s.op("dve", lambda e: e.reciprocal(out=o, in_=o), [out], [out])

    def memset(s, out, val, eng="dve"):
        o = out.ap
        s.op(eng, lambda e: e.memset(o, val), [], [out])

    def reduce(s, out, in_, op=AL.add):
        o, i = out.ap, in_.ap
        s.op("dve", lambda e: e.tensor_reduce(out=o, in_=i, axis=AX.X, op=op), [in_], [out])


def build(NL, SEQ_P, NSQ):
    NPT = SEQ_P // 128
    SR = NSQ * 8
    NTOK = SEQ_P + SR
    rtiles = [(i * 128, 128) for i in range(NPT)] + [(SEQ_P, SR)]
    NRT = len(rtiles)
    tgroups = []
    t0 = 0
    while t0 < NTOK:
        n = min(512, NTOK - t0)
        tgroups.append((t0, n))
        t0 += n

    nc = bass.Bass("TRN2", target_bir_lowering=False)

    def dram(name, shape, dt=F32, kind="ExternalInput"):
        return nc.dram_tensor(name, list(shape), dt, kind=kind).ap()

    xp = dram("xp", [SEQ_P, D])
    xs = dram("xs", [SR, D])
    st_shift = dram("st_shift", [NL, NSQ, 3200])
    st_wkv = dram("st_wkv", [NL, NSQ, 16, 64, 64])
    st_pool = dram("st_pool", [NL, NSQ, 15, 512])
    norm_g = dram("norm_g", [NL, D])
    fnorm_g = dram("fnorm_g", [1, D])
    w_in = dram("w_in", [NL, D, DIN])
    w_out = dram("w_out", [NL, D, D])
    PPd = dram("pp", [NL, 128, NPP])
    w0d = dram("w0", [NL, 1, 1024])
    LWd = dram("lw", [NL, 128, 1024])
    pwd = dram("pool_w", [NL, 4, 128, 128])
    lngd = dram("lng", [NL, 1, 512])
    wsTd = dram("wsT", [NL, 128, 4, 128])
    gbd = dram("gb", [NL, 1, 512])
    cM4 = dram("cmask4", [128, 512])
    cNSL = dram("cnsl", [128, 128])
    cTRIS = dram("ctris", [128, 128])
    cIDF = dram("cidf", [128, 128])
    cBLK = dram("cblk", [128, 128])
    cIC0 = dram("cic0", [128, 512])
    cIC1 = dram("cic1", [128, 512])

    yp = dram("yp", [SEQ_P, D], kind="ExternalOutput")
    ys = dram("ys", [SR, D], kind="ExternalOutput")
    p_shift = dram("p_shift", [NL, 3200], kind="ExternalOutput")
    p_wkv = dram("p_wkv", [NL, 16, 64, 64], kind="ExternalOutput")
    p_pool = dram("p_pool", [NL, 15, 512], kind="ExternalOutput")
    s_shift = dram("s_shift", [NL, NSQ, 3200], kind="ExternalOutput")
    s_wkv = dram("s_wkv", [NL, NSQ, 16, 64, 64], kind="ExternalOutput")
    s_pool = dram("s_pool", [NL, NSQ, 15, 512], kind="ExternalOutput")
    s_cv = dram("s_cv", [NL, NSQ, 8, 512], kind="ExternalOutput")

    projT = dram("projT", [NCH * 128, NTOK], kind="Internal")
    vcTM = dram("vcTM", [NTOK, 512], kind="Internal")
    xbuf = dram("xbuf", [NTOK, D], kind="Internal")
    catT = dram("catT", [NRT, 128, 16, 128], BF, kind="Internal")

    projV = projT.rearrange("(ch p) t -> p ch t", p=128)

    P = Prog()
    es = contextlib.ExitStack()
    ARW = 47104
    AR = es.enter_context(nc.sbuf_tensor("arena", [128, ARW], F32))
    PG = [es.enter_context(nc.psum_tensor("pg%d" % i, [128, 4, 512], F32)) for i in range(2)]
    sems = {}
    for e in ENG:
        sems["E_" + e] = es.enter_context(nc.semaphore("E_" + e))
    for q in ("sp", "pool", "act"):
        for i in range(KD):
            n = "D_%s_%d" % (q, i)
            sems[n] = es.enter_context(nc.semaphore(n))

    cur = [0]

    def carve(shape, dt, key, at=None):
        esz = 2 if dt == BF else 4
        n = 1
        for x in shape[1:]:
            n *= x
        words = (n * esz + 3) // 4
        words = (words + 7) // 8 * 8
        if at is None:
            at = cur[0]
            cur[0] += words
        assert at + words <= ARW, ("arena overflow", key, at, words)
        ap = AR[0:shape[0], at:at + words]
        if dt == BF:
            ap = ap.bitcast(BF)
        ap = ap[:, 0:n]
        if len(shape) == 3:
            ap = ap.rearrange("p (a b) -> p a b", b=shape[2])
        elif len(shape) == 4:
            ap = ap.rearrange("p (a b c) -> p a b c", b=shape[2], c=shape[3])
        v = V(ap, key)
        return v, at

    def psum(g, b0, nb=1):
        return V(PG[g][:, b0:b0 + nb, :].rearrange("p a b -> p (a b)"), "PSUM")

    def psum3(g, b0, nb, inner):
        return V(PG[g][:, b0:b0 + nb, :].rearrange("p a (b c) -> p (a b) c", c=inner), "PSUM")

    def psbf(g, b):
        return V(PG[g][:, b, :].bitcast(BF), "PSUM")

    MASK4, _ = carve([128, 4, 128], F32, "MASK4")
    NSL, _ = carve([128, 128], F32, "NSL")
    TRIS, _ = carve([128, 128], F32, "TRIS")
    IDF, _ = carve([128, 128], F32, "IDF")
    IDB, _ = carve([128, 128], BF, "IDB")
    BLK1, _ = carve([128, 128], BF, "BLK1")
    IC0, _ = carve([128, 4, 128], F32, "IC0")
    IC1, _ = carve([128, 4, 128], F32, "IC1")
    ONESF, _ = carve([128, 128], F32, "ONESF")
    PP, _ = carve([128, NPP], F32, "PP")
    OMK, _ = carve([128, 8], F32, "OMK")
    W0R, _ = carve([128, 1024], F32, "W0R")
    LW, _ = carve([128, 1024], BF, "LW")
    PW, _ = carve([128, 4, 128], BF, "PW")
    LNG, _ = carve([128, 512], F32, "LNG")
    BSB, _ = carve([128, 4, 128], F32, "BSB")
    WSTF, _ = carve([128, 4, 128], F32, "WSTF")
    WST, _ = carve([128, 4, 128], BF, "WST")
    base = cur[0]

    P.dma(MASK4, V(cM4.rearrange("p (a b) -> p a b", b=128), None))
    P.dma(NSL, V(cNSL, None))
    P.dma(TRIS, V(cTRIS, None))
    P.dma(IDF, V(cIDF, None))
    P.dma(IDB, V(cIDF, None), q="pool")
    P.dma(BLK1, V(cBLK, None), q="pool")
    P.dma(IC0, V(cIC0.rearrange("p (a b) -> p a b", b=128), None))
    P.dma(IC1, V(cIC1.rearrange("p (a b) -> p a b", b=128), None))
    P.memset(ONESF, 1.0)

    cur[0] = base
    HT, _ = carve([128, 16, NTOK], BF, "HT")
    abase = cur[0]
    XT = [carve([128, D], F32, "XT%d" % i)[0] for i in range(2)]
    HB = [carve([128, D], BF, "HB%d" % i)[0] for i in range(2)]
    NGB, _ = carve([128, D], F32, "NGB")
    SQJ, _ = carve([128, D], BF, "SQJ")
    SSA, _ = carve([128, 8], F32, "SSA")
    cur[0] = abase
    WCH = [carve([128, 16, 512], BF, "WCH%d" % i)[0] for i in range(3)]
    STG = [carve([128, 512], F32, "STG%d" % i)[0] for i in range(4)]
    cur[0] = base
    WO = [carve([128, 16, 512], BF, "WO%d" % i)[0] for i in range(4)]
    CT = [carve([128, 16, 128], BF, "CT%d" % i)[0] for i in range(2)]
    XO = [carve([128, D], F32, "XO%d" % i)[0] for i in range(2)]
    cur[0] = base
    WKO, _ = carve([128, 8, 128], F32, "WKO")
    SHO, _ = carve([128, 128], F32, "SHO")
    POUT, _ = carve([128, 512], F32, "POUT")

    NAMES = ("PS XS AM SPm AT EL ELI KK RN SIG KK2 KR KHf BHf VBF TW GAM KTtm KHtm BHtm Vtm Qm AV KPf U "
             "HTMP WKI GST SHI CATT EXT WA WB PLD PTM GB PIN VC VN32 VNB UC GC T1 CST "
             "Y32 VP GA YA SQT BON HS32 HSB").split()

    def make_set(tg, W):
        k = lambda n: tg + n
        big = W == 128
        d = {}
        d["PS"], _ = carve([128, 25, W + 1], F32, k("PS"))
        d["XS"], _ = carve([128, 25, W], F32, k("XS"))
        d["AM"] = [carve([128, 4, 4, W], BF, k("AM%d" % g))[0] for g in range(4)]
        d["SPm"] = [carve([128, 4, 3, W], BF, k("SP%d" % g))[0] for g in range(4)]
        d["AT"], _ = carve([128, 8, W], F32, k("AT"))
        d["EL"], _ = carve([128, 8, W + 1], F32, k("EL"))
        d["ELI"], _ = carve([128, 8, W], F32, k("ELI"))
        d["KK"], _ = carve([128, 8, W], F32, k("KK"))
        d["RN"], _ = carve([128, 8, W], F32, k("RN"))
        d["SIG"], _ = carve([128, 1024], F32, k("SIG"))
        d["KK2"], _ = carve([128, 8, W], BF, k("KK2"))
        d["KR"], _ = carve([128, 8, 2, W], BF, k("KR"))
        d["KHf"], _ = carve([128, 8, W], BF, k("KHf"))
        d["BHf"], _ = carve([128, 8, W], BF, k("BHf"))
        d["VBF"], _ = carve([128, 8, W], BF, k("VBF"))
        d["TW"], _ = carve([128, W], BF, k("TW"))
        d["GAM"], _ = carve([128, 8], F32, k("GAM"))
        for n in ("KTtm", "KHtm", "BHtm", "Vtm", "AV", "U"):
            d[n], _ = carve([128, 1024], BF, k(n))
        d["Qm"] = [V(d["SPm"][g].ap[:, :, 2, :], k("SP%d" % g)) for g in range(4)]
        d["KPf"], _ = carve([128, 8, W], BF, k("KPf"))
        d["HTMP"], _ = carve([128, 8, 64], F32, k("HTMP"))
        d["WKI"] = carve([128, 16, 64], F32, k("WKI"))[0]
        d["GST"], _ = carve([128, 64], F32, k("GST"))
        d["SHI"] = carve([128, 128], F32, k("SHI"))[0]
        d["CATT"], _ = carve([128, 16, W], BF, k("CATT"))
        for n in ("EXT", "WA", "WB"):
            d[n], _ = carve([128, 4, 15 + W], F32, k(n))
        d["PLD"], _ = carve([128, 4, W], BF, k("PLD"))
        d["PTM"], _ = carve([128, 4, W], F32, k("PTM"))
        d["GB"], _ = carve([128, 4, W], F32, k("GB"))
        d["PIN"] = carve([128, 512], F32, k("PIN"))[0]
        d["VC"], _ = carve([128, 512], F32, k("VC"))
        d["VN32"], _ = carve([128, 512], F32, k("VN32"))
        d["VNB"], _ = carve([128, 512], BF, k("VNB"))
        for n in ("UC", "GC", "T1"):
            d[n], _ = carve([128, 4, W], F32, k(n))
        d["CST"], _ = carve([128, 16], F32, k("CST"))
        d["HS32"], _ = carve([128, 8, 64], F32, k("HS32"))
        d["HSB"], _ = carve([128, 8, 64], BF, k("HSB"))
        if big:
            d["Y32"] = V(d["AT"].ap.rearrange("p a b -> p (a b)"), k("AT"))
            d["VP"] = V(d["EL"].ap.rearrange("p a b -> p (a b)")[:, 0:1024], k("EL"))
            d["GA"] = d["ELI"]
            d["YA"] = d["KK"]
            d["SQT"] = V(d["RN"].ap.rearrange("p a b -> p (a b)"), k("RN"))
            d["BON"] = V(d["SIG"].ap.rearrange("p (a b) -> p a b", b=128), k("SIG"))
        else:
            d["Y32"], _ = carve([128, 1024], F32, k("Y32"))
            d["VP"], _ = carve([128, 1024], F32, k("VP"))
            d["GA"] = d["ELI"]
            d["YA"] = d["KK"]
            d["SQT"] = d["SIG"]
            d["BON"], _ = carve([128, 8, W], F32, k("BON"))
        return tuple(d[n] for n in NAMES)

    BP = make_set("p.", 128)
    print("arena words used", cur[0], "of", ARW)

    def xsrc(l, r0, R):
        if l == 0:
            if r0 < SEQ_P:
                return V(xp[r0:r0 + R, :], None)
            return V(xs[0:R, :], None)
        return V(xbuf[r0:r0 + R, :], ("xb", r0))

    def phase_norm(l, gsrc, final):
        P.barrier()
        P.dma(NGB, V(gsrc.broadcast_to([128, D]), None))
        for ti, (r0, R) in enumerate(rtiles):
            xt = XT[ti % 2]
            hb = HB[ti % 2]
            P.dma(xt[0:R], xsrc(l, r0, R), q=("sp" if ti % 2 == 0 else "pool"))
            ss = SSA[:, (ti % 2) * 2:(ti % 2) * 2 + 1]
            P.memset(ss[0:R], 0.0)
            P.act(SQJ[0:R], xt[0:R], AF.Square, accum=ss[0:R])
            rs = SSA[:, (ti % 2) * 2 + 1:(ti % 2) * 2 + 2]
            P.ts(rs[0:R], ss[0:R], 1.0 / D, EPS, AL.mult, AL.add)
            P.rsqrt(rs[0:R], rs[0:R])
            if final:
                P.stt(xt[0:R], xt[0:R], rs[0:R], NGB[0:R], AL.mult, AL.mult)
                if r0 < SEQ_P:
                    P.dma(V(yp[r0:r0 + R, :], None), xt[0:R], q="act")
                else:
                    P.dma(V(ys[0:R, :], None), xt[0:R], q="act")
                continue
            P.stt(hb[0:R], xt[0:R], rs[0:R], NGB[0:R], AL.mult, AL.mult)
            g = ti % 2
            for half in range(2):
                pb = psbf(g, half + 2 * ((ti // 2) % 2))
                for k in range(8):
                    kc = half * 8 + k
                    P.tr(pb[:, k * 128:k * 128 + R], hb[0:R, kc * 128:(kc + 1) * 128], IDB[0:R, 0:R])
                src = pb.re("p (a b) -> p a b", b=128)[:, :, 0:R]
                dst = HT[:, half * 8:half * 8 + 8, r0:r0 + R]
                if half == 0:
                    P.copy(dst, src, "act")
                else:
                    P.copy(dst, src, "dve")

    def phase_proj(l):
        P.barrier()
        wv = w_in[l].rearrange("(kc p) c -> p kc c", p=128)
        bank = 0
        sg = 0
        for ch in range(NCH):
            sc, sub = ch // 4, ch % 4
            if sub == 0:
                ncol = min(512, DIN - sc * 512)
                P.dma(WCH[sc % 3][:, :, 0:ncol], V(wv[:, :, sc * 512:sc * 512 + ncol], None), q="pool")
            wch = WCH[sc % 3][:, :, sub * 128:(sub + 1) * 128]
            if 45 <= ch < 49:
                for ti, (r0, R) in enumerate(rtiles):
                    ps = psum(bank // 4, bank % 4)
                    for kc in range(16):
                        P.mm(ps[0:R, 0:128], HT[:, kc, r0:r0 + R], wch[:, kc, :], kc == 0, kc == 15)
                    st = STG[sg % 4]
                    if sg % 2 == 0:
                        P.copy(st[0:R, 0:128], ps[0:R, 0:128], "act")
                    else:
                        P.copy(st[0:R, 0:128], ps[0:R, 0:128], "dve")
                    P.dma(V(vcTM[r0:r0 + R, (ch - 45) * 128:(ch - 44) * 128], ("vc", ti, ch)), st[0:R, 0:128])
                    bank = (bank + 1) % 8
                    sg += 1
                continue
            for gi, (t0, n) in enumerate(tgroups):
                ps = psum(bank // 4, bank % 4)
                for kc in range(16):
                    P.mm(ps[:, 0:n], wch[:, kc, :], HT[:, kc, t0:t0 + n], kc == 0, kc == 15)
                st = STG[sg % 4]
                if sg % 2 == 0:
                    P.copy(st[:, 0:n], ps[:, 0:n], "act")
                else:
                    P.copy(st[:, 0:n], ps[:, 0:n], "dve")
                P.dma(V(projT[ch * 128:(ch + 1) * 128, t0:t0 + n], ("pj", ch, gi)), st[:, 0:n])
                bank = (bank + 1) % 8
                sg += 1

    def pj(c0, c1, a, b):
        ks = set()
        for gi, (t0, n) in enumerate(tgroups):
            if a < t0 + n and b > t0:
                for ch in range(c0, c1):
                    ks.add(("pj", ch, gi))
        return projV[:, c0:c1, a:b], sorted(ks)

    def dma_pj(dst, c0, c1, a, b):
        ap, ks = pj(c0, c1, a, b)
        P.dma(dst, VK(ap, ks))

    def layer_consts(l):
        P.barrier()
        P.dma(PP, V(PPd[l], None))
        P.dma(W0R[0:1], V(w0d[l], None))
        P.dma(LW, V(LWd[l], None), q="pool")
        P.dma(PW, V(pwd[l].rearrange("g c d -> c g d"), None), q="pool")
        P.dma(LNG, V(lngd[l].broadcast_to([128, 512]), None))
        P.dma(BSB, V(gbd[l].broadcast_to([128, 512]).rearrange("p (a b) -> p a b", b=128), None))
        P.dma(WSTF, V(wsTd[l], None))
        P.tt(WST, WSTF, MASK4[:, 3:4, :].bc([128, 4, 128]), AL.mult)
        P.ts(OMK, PP[:, 33:41], -1.0, 1.0, AL.mult, AL.add)

    def tile_call(B, l, T, tok0, first, last, sb, rt, col0):
        (PS, XS, AM, SPm, AT, EL, ELI, KK, RN, SIG, KK2, KR, KHf, BHf, VBF, TW, GAM, KTtm, KHtm, BHtm, Vtm, Qm,
         AV, KPf, U, HTMP, WKI, GST, SHI, CATT, EXT, WA, WB, PLD, PTM, GB, PIN, VC, VN32, VNB, UC, GC, T1, CST,
         Y32, VP, GA, YA, SQT, BON, HS32, HSB) = B
        nlev = int(math.log2(T)) - 1
        bc8 = lambda c0: PP[:, c0:c0 + 8].us(2).bc([128, 8, T])
        Lx = 15 + T

        def pool_load():
            if first and sb is None:
                P.memset(EXT[:, :, 0:15], 0.0)
                dma_pj(EXT[:, :, 15:15 + T], 33, 37, tok0, tok0 + T)
            elif first:
                P.dma(PIN[0:15, :], V(st_pool[l, sb], None))
                dma_pj(EXT[:, :, 15:15 + T], 33, 37, tok0, tok0 + T)
            else:
                dma_pj(EXT[:, :, 0:15 + T], 33, 37, tok0 - 15, tok0 + T)
            dma_pj(GB[:, :, 0:T], 37, 41, tok0, tok0 + T)

        def cg_load():
            vks = [("vc", rt, ch) for ch in range(45, 49)]
            P.dma(VC[0:T, :], VK(vcTM[tok0:tok0 + T, :], vks))
            dma_pj(UC[:, :, 0:T], 41, 45, tok0, tok0 + T)
            dma_pj(GC[:, :, 0:T], 49, 53, tok0, tok0 + T)

        def pool_dve():
            if first and sb is not None:
                pp_ = psum3(0, 2, 1, 128)
                for g4 in range(4):
                    P.tr(pp_[:, g4, 0:15], PIN[0:15, g4 * 128:(g4 + 1) * 128], IDF[0:15, 0:15])
                P.copy(EXT[:, :, 0:15], pp_[:, :, 0:15], "dve")
            P.tt(WA[:, 0:4, 1:Lx], EXT[:, 0:4, 1:Lx], EXT[:, 0:4, 0:Lx - 1], AL.add, eng="pool")
            P.tt(WB[:, 1:4, 3:Lx], WA[:, 1:4, 3:Lx], WA[:, 1:4, 1:Lx - 2], AL.add, eng="pool")
            P.tt(WA[:, 2:4, 7:Lx], WB[:, 2:4, 7:Lx], WB[:, 2:4, 3:Lx - 4], AL.add, eng="pool")
            P.tt(WB[:, 3:4, 15:Lx], WA[:, 3:4, 15:Lx], WA[:, 3:4, 7:Lx - 8], AL.add, eng="pool")
            ic = IC0 if (first and sb is None) else IC1
            for g4 in range(4):
                wsrc = (WA, WB, WA, WB)[g4]
                P.tt(PTM[:, g4, 0:T], wsrc[:, g4, 15:Lx], ic[:, g4, 0:T], AL.mult, eng="pool")
            P.tt(PLD[:, :, 0:T], PTM[:, :, 0:T], EXT[:, :, 15:Lx], AL.subtract, eng="pool")

        def cg_dve1():
            cm = CST[0:T, 0:1]
            P.reduce(cm, VC[0:T, :])
            P.ts(cm, cm, -1.0 / 512, None, AL.mult)
            P.ts(VN32[0:T, :], VC[0:T, :], cm, None, AL.add)

        def cg_mid():
            cv = CST[0:T, 1:2]
            P.memset(cv, 0.0)
            P.act(VC[0:T, :], VN32[0:T, :], AF.Square, accum=cv)
            P.ts(cv, cv, 1.0 / 512, LN_EPS, AL.mult, AL.add)
            P.rsqrt(cv, cv)
            P.stt(VN32[0:T, :], VN32[0:T, :], cv, LNG[0:T, :], AL.mult, AL.mult)
            P.copy(VNB[0:T, :], VN32[0:T, :], "act")
            if sb is not None:
                P.dma(V(s_cv[l, sb], None), VN32[0:T, :], q="sp")
            P.act(GB[:, :, 0:T], GB[:, :, 0:T], AF.Silu)
            P.act(GC[:, :, 0:T], GC[:, :, 0:T], AF.Silu)

        def pool_tail():
            mx = psum3(0, 3, 1, 128)
            for g4 in range(4):
                P.mm(mx[:, g4, 0:T], PW[:, g4, :], PLD[:, g4, 0:T])
            P.tt(PTM[:, :, 0:T], mx[:, :, 0:T], PP[:, 73:77].us(2).bc([128, 4, T]), AL.mult)
            P.tt(CATT[:, 8:12, 0:T], PTM[:, :, 0:T], GB[:, :, 0:T], AL.mult)
            if last:
                po_ = psum(0, 2)
                for g4 in range(4):
                    P.tr(po_[0:15, g4 * 128:(g4 + 1) * 128], EXT[:, g4, T:T + 15], IDF)
                P.copy(POUT[0:15, :], po_[0:15, :], "act")
                dst = p_pool[l] if sb is None else s_pool[l, sb]
                P.dma(V(dst, None), POUT[0:15, :], q="act")

        def cg_tail():
            mxc = psum3(1, 3, 1, 128)
            for g4 in range(4):
                P.mm(mxc[:, g4, 0:T], VNB[0:T, g4 * 128:(g4 + 1) * 128], WST[0:T, g4, 0:T])
            t1 = T1[:, :, 0:T]
            P.tt(t1, mxc[:, :, 0:T], BSB[:, :, 0:T], AL.add)
            P.tt(t1, t1, UC[:, :, 0:T], AL.mult)
            P.tt(CATT[:, 12:16, 0:T], t1, GC[:, :, 0:T], AL.mult)

        if first and sb is None:
            P.memset(PS[:, :, 0:1], 0.0)
            dma_pj(PS[:, :, 1:T + 1], 0, 25, tok0, tok0 + T)
        elif first:
            P.dma(SHI[0:25, :], V(st_shift[l, sb].rearrange("(ch p) -> ch p", p=128), None))
            pt = psum(1, 3)
            P.tr(pt[:, 0:25], SHI[0:25, :], IDF[0:25, 0:25])
            P.copy(PS[:, :, 0], pt[:, 0:25], "dve")
            dma_pj(PS[:, :, 1:T + 1], 0, 25, tok0, tok0 + T)
        else:
            dma_pj(PS[:, :, 0:T + 1], 0, 25, tok0 - 1, tok0 + T)
        if last:
            pt = psum(1, 3)
            P.tr(pt[0:25, 0:128], PS[:, :, T], IDF)
            P.copy(SHO[0:25, :], pt[0:25, 0:128], "act")
            dst = p_shift[l] if sb is None else s_shift[l, sb]
            P.dma(V(dst.rearrange("(ch p) -> ch p", p=128), None), SHO[0:25, :], q="act")
        pool_load()
        cg_load()
        P.memset(EL[:, :, 0:1], 1.0)
        XSa = V(XS.ap[:, 0:16], XS.key + "a")
        XSb = V(XS.ap[:, 16:24], XS.key + "b")
        XSl = V(XS.ap[:, 24:25], XS.key + "l")
        for (c0, c1, eng, xv) in ((24, 25, "dve", XSl), (0, 16, "dve", XSa), (16, 24, "pool", XSb)):
            xx = xv[:, :, 0:T]
            P.tt(xx, PS[:, c0:c1, 0:T], PS[:, c0:c1, 1:T + 1], AL.subtract, eng=eng)
            P.tt(xx, xx, PP[:, c0:c1].us(2).bc([128, c1 - c0, T]), AL.mult, eng=eng)
            P.tt(xx, xx, PS[:, c0:c1, 1:T + 1], AL.add, eng=eng)
        Xr = XSa[:, 0:8, 0:T]
        Xk = XSa[:, 8:16, 0:T]
        Xv = XSb[:, 0:8, 0:T]
        yield
        P.act(TW[0:64, 0:T], XSl[0:64, 0, 0:T], AF.Tanh)
        P.act(TW[64:128, 0:T], XSl[64:128, 0, 0:T], AF.Copy)
        zt = psum(0, 0, 2)
        for h2 in range(2):
            P.mm(zt[0:T, h2 * 512:(h2 + 1) * 512], TW[0:64, 0:T], LW[0:64, h2 * 512:(h2 + 1) * 512], True, False)
            P.mm(zt[0:T, h2 * 512:(h2 + 1) * 512], ONESF[0:1, 0:T], W0R[0:1, h2 * 512:(h2 + 1) * 512], False, True)
        at = psum3(1, 0, 2, 128)
        for ch in range(8):
            P.mm(at[:, ch, 0:T], LW[64:128, ch * 128:(ch + 1) * 128], TW[64:128, 0:T])
        kk = KK[:, :, 0:T]
        P.tt(kk, Xk, bc8(25), AL.mult)
        P.act(KK2[:, :, 0:T], kk, AF.Square)
        hs = psum3(1, 2, 2, 128)
        for ch in range(8):
            P.mm(hs[:, ch, 0:T], BLK1, KK2[:, ch, 0:T])
        pool_dve()
        cg_dve1()
        P.act(SIG[0:T, :], zt[0:T, :], AF.Sigmoid)
        a_ = AT[:, :, 0:T]
        P.tt(a_, at[:, :, 0:T], bc8(65), AL.add)
        P.act(a_, a_, AF.Sigmoid)
        lt = psum3(0, 2, 2, 128)
        for ch in range(8):
            P.mm(lt[:, ch, 0:T], SIG[0:T, ch * 128:(ch + 1) * 128], TRIS[0:T, 0:T])
        P.act(EL[:, :, 1:T + 1], lt[:, :, 0:T], AF.Exp)
        P.act(ELI[:, :, 0:T], lt[:, :, 0:T], AF.Exp, scale=-1.0)
        rn = RN[:, :, 0:T]
        P.ts(rn, hs[:, :, 0:T], 1e-12, None, AL.max)
        P.rsqrt(rn, rn)
        P.tt(kk, kk, rn, AL.mult)
        yield
        P.tt(rn, a_, bc8(33), AL.mult)
        P.tt(rn, rn, OMK.us(2).bc([128, 8, T]), AL.add)
        P.tt(rn, rn, Xk, AL.mult)
        P.tt(a_, kk, a_, AL.mult)
        P.tt(KR[:, :, 0, 0:T], kk, EL[:, :, 0:T], AL.mult)
        P.tt(KR[:, :, 1, 0:T], Xr, EL[:, :, 1:T + 1], AL.mult)
        P.tt(KHf[:, :, 0:T], rn, ELI[:, :, 0:T], AL.mult)
        P.tt(BHf[:, :, 0:T], a_, ELI[:, :, 0:T], AL.mult)
        P.copy(GAM, EL[:, :, T], "dve")
        bon = BON[:, :, 0:T]
        P.tt(bon, Xr, bc8(41), AL.mult)
        P.tt(KK2[:, :, 0:T], bon, rn, AL.mult)
        bs = psum3(0, 0, 2, 128)
        for ch in range(8):
            P.mm(bs[:, ch, 0:T], BLK1, KK2[:, ch, 0:T])
        P.tt(bon, bs[:, :, 0:T], Xv, AL.mult)
        P.copy(VBF[:, :, 0:T], Xv, "act")
        cg_mid()
        yield
        for qi, (src, dst) in enumerate(((KR[:, :, 0, :], KTtm), (KHf, KHtm), (BHf, BHtm), (VBF, Vtm))):
            pb = psbf(1, qi)
            for ch in range(8):
                P.tr(pb[0:T, ch * 128:(ch + 1) * 128], src[:, ch, 0:T], IDB)
            if qi == 2:
                P.act(dst[0:T, :], pb[0:T, :], AF.Copy, scale=-1.0)
            else:
                P.copy(dst[0:T, :], pb[0:T, :], "act" if qi % 2 == 0 else "dve")
        pool_tail()
        cg_tail()
        dma_pj(GA[:, :, 0:T], 25, 33, tok0, tok0 + T)
        P.act(GA[:, :, 0:T], GA[:, :, 0:T], AF.Silu)
        yield "neu"
        m4 = MASK4[0:T, :, 0:T].us(1).bc([T, 4, 4, T])

        def pga(g):
            return V(PG[g % 2][:, :, :].rearrange("p h (a b) -> p h a b", b=128), "PSUM")

        def mm2(pg, hh, s0, lhsT, rhs3):
            if T == 128:
                P.mm(pg[0:T, hh, s0:s0 + 2, :].re("p a b -> p (a b)"), lhsT, rhs3.re("p a b -> p (a b)"))
            else:
                P.mm(pg[0:T, hh, s0, 0:T], lhsT, rhs3[:, 0, 0:T])
                P.mm(pg[0:T, hh, s0 + 1, 0:T], lhsT, rhs3[:, 1, 0:T])

        def a_mats(g):
            pg = pga(g)
            for hh in range(4):
                h = 4 * g + hh
                ch, po = h // 2, (h % 2) * 64
                mm2(pg, hh, 0, BHf[po:po + 64, ch, 0:T], KR[po:po + 64, ch])
                mm2(pg, hh, 2, KHf[po:po + 64, ch, 0:T], KR[po:po + 64, ch])
            P.tt(AM[g][0:T, :, :, 0:T], pg[0:T, :, :, 0:T], m4, AL.mult)
            for hh in range(4):
                h = 4 * g + hh
                ch, po = h // 2, (h % 2) * 64
                P.mm(pg[0:T, hh, 2, 0:T], KR[po:po + 64, ch, 0, 0:T], BHf[po:po + 64, ch, 0:T])
            P.tt(Qm[g][0:T, :, 0:T], pg[0:T, :, 2, 0:T], NSL[0:T, 0:T].us(1).bc([T, 4, T]), AL.mult)
            P.tt(SPm[g][0:T, :, 0, 0:T], AM[g][0:T, :, 0, 0:T], IDB[0:T, 0:T].us(1).bc([T, 4, T]), AL.add, eng="pool")

        def neu_pre_mm(g):
            pg = pga(g)
            for hh in range(4):
                P.mm(pg[0:T, hh, 1, 0:T], Qm[g][0:T, hh, 0:T], AM[g][0:T, hh, 0, 0:T])
                P.mm(pg[0:T, hh, 2, 0:T], AM[g][0:T, hh, 0, 0:T], Qm[g][0:T, hh, 0:T])

        def neu_pq_ev(g):
            pg = pga(g)
            P.copy(SPm[g][0:T, :, 1:3, 0:T], pg[0:T, :, 1:3, 0:T], "act")

        def neu_mm(g, lastlev):
            pg = pga(g)
            for hh in range(4):
                if lastlev:
                    P.mm(pg[0:T, hh, 0, 0:T], Qm[g][0:T, hh, 0:T], SPm[g][0:T, hh, 0, 0:T])
                else:
                    mm2(pg, hh, 0, Qm[g][0:T, hh, 0:T], SPm[g][0:T, hh, 0:2])
                    P.mm(pg[0:T, hh, 2, 0:T], SPm[g][0:T, hh, 1, 0:T], Qm[g][0:T, hh, 0:T])

        def neu_ev(g, lastlev):
            pg = pga(g)
            P.tt(SPm[g][0:T, :, 0, 0:T], SPm[g][0:T, :, 0, 0:T], pg[0:T, :, 0, 0:T], AL.add)
            if not lastlev:
                neu_pq_ev(g)

        for pair in range(2):
            ga, gb = 2 * pair, 2 * pair + 1
            a_mats(ga)
            a_mats(gb)
            yield "neu"
            neu_pre_mm(ga)
            neu_pre_mm(gb)
            neu_pq_ev(ga)
            neu_pq_ev(gb)
            for lev in range(nlev):
                yield "neu"
                lastlev = lev == nlev - 1
                neu_mm(ga, lastlev)
                neu_mm(gb, lastlev)
                neu_ev(ga, lastlev)
                neu_ev(gb, lastlev)
            yield "neu" if pair == 0 else None
        av = psum(0, 0, 2)
        for h in range(16):
            P.mm(av[0:T, h * 64:(h + 1) * 64], AM[h // 4][0:T, h % 4, 2, 0:T], Vtm[0:T, h * 64:(h + 1) * 64])
        P.copy(AV[0:T, :], av[0:T, :], "act")
        yield
        kp = V(PG[1][:, :, :].rearrange("p a (b c) -> p (a b) c", c=128), "PSUM")
        for h in range(16):
            ch = h // 2
            P.mm(kp[:, h, 0:T], KTtm[0:T, ch * 128:(ch + 1) * 128], SPm[h // 4][0:T, h % 4, 0, 0:T])
        kp4 = V(PG[1][:, :, :].rearrange("p a (b two c) -> p (a b) two c", two=2, c=128), "PSUM")
        P.copy(KPf[0:64, :, 0:T], kp4[0:64, :, 0, 0:T], "act")
        P.copy(KPf[64:128, :, 0:T], kp4[64:128, :, 1, 0:T], "dve")
        yield
        vp = psum(0, 2, 2)
        for h in range(16):
            P.mm(vp[0:T, h * 64:(h + 1) * 64], SPm[h // 4][0:T, h % 4, 0, 0:T], AV[0:T, h * 64:(h + 1) * 64])
        P.copy(VP[0:T, :], vp[0:T, :], "act")
        yield
        if first:
            if sb is None:
                P.memset(HS32, 0.0)
                P.memset(HSB, 0.0)
            else:
                P.dma(WKI[0:64], V(st_wkv[l, sb].rearrange("h i j -> i h j"), None))
                hp = psum3(1, 3, 1, 64)
                for ch in range(8):
                    P.tr(hp[:, ch, :], WKI[0:64, 2 * ch:2 * ch + 2, :].re("p a b -> p (a b)"), IDF[0:64, 0:64])
                P.copy(HS32, hp, "act")
                P.copy(HSB, hp, "dve")
        um = psum(0, 0, 2)
        hpar = [2 * c for c in range(8)] + [2 * c + 1 for c in range(8)]
        for h in hpar:
            ch, po = h // 2, (h % 2) * 64
            c0 = (h % 2) * 512 + ch * 64
            P.mm(um[0:T, c0:c0 + 64], KPf[po:po + 64, ch, 0:T], HSB[po:po + 64, ch, :])
        hm = lambda v: v[0:T, :].re("p (c two i) -> p c two i", two=2, i=64)
        pm = lambda v: v[0:T, :].re("p (two c i) -> p c two i", two=2, i=64)
        P.tt(hm(U), hm(VP), pm(um), AL.add)
        yield
        yps = psum(0, 2, 2)
        for h in hpar:
            ch, po = h // 2, (h % 2) * 64
            c0 = (h % 2) * 512 + ch * 64
            P.mm(yps[0:T, c0:c0 + 64], KR[po:po + 64, ch, 1, 0:T], HSB[po:po + 64, ch, :])
        yps2 = psum(1, 2, 2)
        for h in range(16):
            o = yps2[0:T, h * 64:(h + 1) * 64]
            P.mm(o, AM[h // 4][0:T, h % 4, 1, 0:T], U[0:T, h * 64:(h + 1) * 64], True, False)
            P.mm(o, AM[h // 4][0:T, h % 4, 3, 0:T], Vtm[0:T, h * 64:(h + 1) * 64], False, True)
        P.copy(Y32[0:T, :], yps2[0:T, :], "act")
        P.tt(hm(Y32), hm(Y32), pm(yps), AL.add)
        yield
        hn = psum3(1, 0, 2, 64)
        for h in range(16):
            ch = h // 2
            P.mm(hn[:, h, :], BHtm[0:T, ch * 128:(ch + 1) * 128], U[0:T, h * 64:(h + 1) * 64], True, False)
            P.mm(hn[:, h, :], KHtm[0:T, ch * 128:(ch + 1) * 128], Vtm[0:T, h * 64:(h + 1) * 64], False, True)
        hn4 = V(PG[1][:, 0:2, :].rearrange("p a (b two c) -> p (a b) two c", two=2, c=64), "PSUM")
        P.tt(HTMP[0:64], HS32[0:64], hn4[0:64, :, 0, :], AL.add)
        P.tt(HTMP[64:128], HS32[64:128], hn4[64:128, :, 1, :], AL.add)
        P.tt(HS32, HTMP, GAM.us(2).bc([128, 8, 64]), AL.mult)
        P.copy(HSB, HS32, "act")
        yield
        if last:
            wk = psum3(1, 2, 2, 128)
            for ch in range(8):
                P.tr(wk[0:64, ch, :], HS32[:, ch, :], IDF)
            P.copy(WKO[0:64], wk[0:64], "act")
            dst = p_wkv[l] if sb is None else s_wkv[l, sb]
            P.dma(V(dst.rearrange("(c two) i j -> i c two j", two=2), None),
                  WKO[0:64].re("p c (two j) -> p c two j", two=2), q="act")
        yield
        y3 = Y32[0:T, :].re("p (h i) -> p h i", i=64)
        mean = GST[0:T, 0:16]
        P.reduce(mean, y3)
        P.ts(mean, mean, -1.0 / 64, None, AL.mult)
        P.tt(y3, y3, mean.us(2).bc([T, 16, 64]), AL.add)
        P.act(SQT[0:T, :], Y32[0:T, :], AF.Square)
        var = GST[0:T, 16:32]
        P.reduce(var, SQT[0:T, :].re("p (h i) -> p h i", i=64))
        P.ts(var, var, 1.0 / 64, GN_EPS, AL.mult, AL.add)
        P.rsqrt(var, var)
        P.tt(y3, y3, var.us(2).bc([T, 16, 64]), AL.mult)
        ynT = psum3(0, 0, 2, 128)
        for ch in range(8):
            P.tr(ynT[:, ch, 0:T], Y32[0:T, ch * 128:(ch + 1) * 128], IDF[0:T, 0:T])
        ya = YA[:, :, 0:T]
        P.tt(ya, ynT[:, :, 0:T], bc8(49), AL.mult)
        P.tt(ya, ya, bc8(57), AL.add)
        P.tt(ya, ya, bon, AL.add)
        ga_ = GA[:, :, 0:T]
        P.tt(CATT[:, 0:8, 0:T], ya, ga_, AL.mult)
        yield
        P.dma(V(catT[rt, :, :, col0:col0 + T], ("cat", rt)), CATT[:, :, 0:T], q="act")

    def phase_mix(l):
        layer_consts(l)

        def chain(gens):
            for g in gens:
                yield from g

        for b in range(NSQ):
            for _ in tile_call(BP, l, 8, SEQ_P + 8 * b, True, True, b, NRT - 1, 8 * b):
                pass
        for i in range(NPT):
            for _ in tile_call(BP, l, 128, i * 128, i == 0, i == NPT - 1, None, i, 0):
                pass

    def phase_out(l):
        P.barrier()
        wv = w_out[l].rearrange("(kc p) n -> p kc n", p=128)
        for n4 in range(4):
            P.dma(WO[n4], V(wv[:, :, n4 * 512:(n4 + 1) * 512], None), q="pool")
        for ti, (r0, R) in enumerate(rtiles):
            ct = CT[ti % 2]
            xo = XO[ti % 2]
            P.dma(ct[:, :, 0:R], V(catT[ti, :, :, 0:R], ("cat", ti)), q="pool")
            P.dma(xo[0:R], xsrc(l, r0, R))
            for n4 in range(4):
                ps = psum(ti % 2, n4)
                for kc in range(16):
                    P.mm(ps[0:R, :], ct[:, kc, 0:R], WO[n4][:, kc, :], kc == 0, kc == 15)
                P.tt(xo[0:R, n4 * 512:(n4 + 1) * 512], xo[0:R, n4 * 512:(n4 + 1) * 512], ps[0:R, :], AL.add)
            P.dma(V(xbuf[r0:r0 + R, :], ("xb", r0)), xo[0:R], q="act")

    for l in range(NL):
        phase_norm(l, norm_g[l:l + 1, :], False)
        phase_proj(l)
        phase_mix(l)
        phase_out(l)
    phase_norm(NL, fnorm_g, True)
    P.barrier()

    with nc.Block() as block:
        def emit(name, e):
            for waits, fn, inc in P.ops[name]:
                for sem, val in waits:
                    e.wait_ge(sems[sem], val)
                if fn is not None:
                    fn(e).then_inc(sems[inc[0]], inc[1])

        @block.tensor
        def _(e):
            emit("pe", e)

        @block.scalar
        def _(e):
            emit("act", e)

        @block.vector
        def _(e):
            emit("dve", e)

        @block.gpsimd
        def _(e):
            emit("pool", e)

        @block.sync
        def _(e):
            emit("sp", e)
    es.close()
    stats = {k: len(v) for k, v in P.ops.items()}
    return nc, stats


def _consts():
    s = np.arange(128)[:, None]
    t = np.arange(128)[None, :]
    su = (s < t).astype(np.float32)
    ui = (s <= t).astype(np.float32)
    sl = (s > t).astype(np.float32)
    c = {}
    c["cmask4"] = np.concatenate([-su, -ui, su, ui], axis=1).astype(np.float32)
    c["cnsl"] = -sl
    c["ctris"] = (-math.exp(-0.5) * ui).astype(np.float32)
    c["cidf"] = np.eye(128, dtype=np.float32)
    blk = np.zeros((128, 128), np.float32)
    blk[:64, :64] = 1.0
    blk[64:, 64:] = 1.0
    c["cblk"] = blk
    wins = (2, 4, 8, 16)
    ic0 = np.zeros((128, 4, 128), np.float32)
    ic1 = np.zeros((128, 4, 128), np.float32)
    pos = np.arange(128)
    for g, w in enumerate(wins):
        ic0[:, g, :] = (1.0 / np.minimum(pos + 1, w))[None, :]
        ic1[:, g, :] = 1.0 / w
    c["cic0"] = ic0.reshape(128, 512)
    c["cic1"] = ic1.reshape(128, 512)
    return c


def _chunked(vec, n):
    NL = vec.shape[0]
    return np.ascontiguousarray(vec.reshape(NL, n, 128).transpose(0, 2, 1))


def make_in_maps(inp, NL, SEQ_P, NSQ, ncores, nprompt):
    f = lambda a: np.ascontiguousarray(np.asarray(a, dtype=np.float32))
    pp = np.concatenate([
        _chunked(f(inp["shift_mu"])[:NL], 25), _chunked(f(inp["k_k"])[:NL], 8), _chunked(f(inp["k_a"])[:NL], 8),
        _chunked(f(inp["r_k"])[:NL].reshape(NL, 1024), 8), _chunked(f(inp["lnx_g"])[:NL], 8),
        _chunked(f(inp["lnx_b"])[:NL], 8), _chunked(f(inp["a0"])[:NL], 8), _chunked(f(inp["pool_scale"])[:NL], 4)],
        axis=2)
    assert pp.shape[2] == NPP
    shared = {
        "norm_g": f(inp["norm_g"])[:NL], "fnorm_g": f(inp["final_norm_g"]).reshape(1, D),
        "w_in": f(inp["w_in"])[:NL], "w_out": f(inp["w_out"])[:NL], "pp": np.ascontiguousarray(pp),
        "w0": f(inp["w0"])[:NL].reshape(NL, 1, 1024),
        "lw": np.ascontiguousarray(np.concatenate([f(inp["w_up"])[:NL], f(inp["a_up"])[:NL]], axis=1)),
        "pool_w": f(inp["pool_w"])[:NL], "lng": f(inp["gmlp_ln_g"])[:NL].reshape(NL, 1, 512),
        "wsT": np.ascontiguousarray(f(inp["gmlp_ws"])[:NL].transpose(0, 3, 1, 2)),
        "gb": f(inp["gmlp_b"])[:NL].reshape(NL, 1, 512),
    }
    shared.update(_consts())
    xpr = f(inp["x_prompt"])
    xsa = f(inp["x_sample"])
    sts, stw, stp = f(inp["state_shift"]), f(inp["state_wkv"]), f(inp["state_pool"])
    maps = []
    for c in range(ncores):
        b = c % nprompt
        m = dict(shared)
        m["xp"] = np.ascontiguousarray(xpr[b, :SEQ_P])
        sl = slice(c * NSQ, (c + 1) * NSQ)
        m["xs"] = np.ascontiguousarray(xsa[sl].reshape(NSQ * 8, D))
        m["st_shift"] = np.ascontiguousarray(sts[:NL, sl])
        m["st_wkv"] = np.ascontiguousarray(stw[:NL, sl])
        m["st_pool"] = np.ascontiguousarray(stp[:NL, sl])
        maps.append(m)
    return maps


_CACHE = {}


def run(inp, NL, SEQ_P, NSQ, ncores, nprompt):
    key = (NL, SEQ_P, NSQ)
    if key not in _CACHE:
        _CACHE[key] = build(NL, SEQ_P, NSQ)[0]
    nc = _CACHE[key]
    maps = make_in_maps(inp, NL, SEQ_P, NSQ, ncores, nprompt)
    res = run_bass_kernel_spmd(nc, maps, core_ids=list(range(ncores)))
    R = res.results
    g = lambda name, cores: np.stack([np.asarray(R[c][name], dtype=np.float32) for c in cores])
    pc = list(range(nprompt))
    ac = list(range(ncores))
    y_prompt = g("yp", pc)
    y_sample = g("ys", ac).reshape(ncores * NSQ, 8, D)
    p_shift = g("p_shift", pc).transpose(1, 0, 2)
    p_wkv = g("p_wkv", pc).transpose(1, 0, 2, 3, 4)
    p_pool = g("p_pool", pc).transpose(1, 0, 2, 3)
    cat = lambda name: np.concatenate([np.asarray(R[c][name], dtype=np.float32) for c in ac], axis=1)
    return (y_prompt, y_sample, np.ascontiguousarray(p_shift), np.ascontiguousarray(p_wkv),
            np.ascontiguousarray(p_pool), cat("s_shift"), cat("s_wkv"), cat("s_pool"), cat("s_cv"))


def kernel(**inputs):
    return run(inputs, 4, 2048, 16, 8, 4)
```
